# Optimizing a Trainium2 kernel written in Bass

```python
import jax
import jax.numpy as jnp
from jax import lax
import numpy as np

D_MODEL = 1024
BATCH = 32
SEQ = 256
DEPTH = 4
DEC_BATCH = 4
DEC_SEQ = 2048
PAST_LEN = 256

GRID_W = 64
D_MIX = D_MODEL
M_HEADS = 4
M_DK = 32
M_DV = 64
G_HEADS = 4
G_DK = 32
G_DV = 64
G_RANK = 16
G_TAU = 16.0
A_HEADS = 8
A_DNOPE = 64
A_DROPE = 32
A_DV = 64
A_DQ = 256
A_DC = 128
A_SCALE = (A_DNOPE + A_DROPE) ** -0.5
CHUNK = 64
Q_BLOCK = 128
ROPE_BASE = 10000.0
FFN_HIDDEN = ((8 * D_MODEL + 3 * 256 - 1) // (3 * 256)) * 256
ALPHA = (2 * DEPTH) ** 0.25
BETA = (8 * DEPTH) ** -0.25
IN_SPLITS = (M_HEADS * M_DK, M_HEADS * M_DK, M_HEADS * M_DV, M_HEADS * M_DV, 4 * M_HEADS,
             G_HEADS * G_DK, G_HEADS * G_DK, G_HEADS * G_DV, G_HEADS * G_DV, 2 * G_RANK,
             A_DQ, A_DC, A_DROPE)
IN_COLS = sum(IN_SPLITS)

kernel_name = 'hybrid_mlstm_gla_mla_diffusion_step'


def layer_norm(x, g, b, eps=1e-5):
    xf = x.astype(jnp.float32)
    xc = xf - xf.mean(-1, keepdims=True)
    y = xc * lax.rsqrt((xc * xc).mean(-1, keepdims=True) + eps)
    return y.astype(x.dtype) * g + b


def rms_norm(x, g, eps=1e-6):
    xf = x.astype(jnp.float32)
    y = xf * lax.rsqrt((xf * xf).mean(-1, keepdims=True) + eps)
    return y.astype(x.dtype) * g


def head_norm(x, g, n_heads, center, eps=1e-6):
    b, t, w = x.shape
    xf = x.astype(jnp.float32).reshape(b, t, n_heads, w // n_heads)
    if center:
        xf = xf - xf.mean(-1, keepdims=True)
    y = xf * lax.rsqrt((xf * xf).mean(-1, keepdims=True) + eps)
    return y.reshape(b, t, w).astype(x.dtype) * g


def to_heads(t, n):
    b, s, _ = t.shape
    return t.reshape(b, s, n, -1).transpose(0, 2, 1, 3)


def from_heads(t):
    b, h, s, d = t.shape
    return t.transpose(0, 2, 1, 3).reshape(b, s, h * d)


def flip_t(t):
    return jnp.flip(t, axis=2)


def to_chunks(t):
    b, h, s = t.shape[:3]
    return jnp.moveaxis(t.reshape((b, h, s // CHUNK, CHUNK) + t.shape[3:]), 2, 0)


def from_chunks(t):
    t = jnp.moveaxis(t, 0, 2)
    return t.reshape(t.shape[:2] + (-1,) + t.shape[4:])


def rope_axis(x, pos):
    half = x.shape[-1] // 2
    inv = ROPE_BASE ** (-jnp.arange(half, dtype=jnp.float32) / half)
    ang = pos.astype(jnp.float32)[:, None] * inv[None, :]
    cos = jnp.cos(ang).astype(x.dtype)
    sin = jnp.sin(ang).astype(x.dtype)
    x1, x2 = x[..., :half], x[..., half:]
    return jnp.concatenate([x1 * cos - x2 * sin, x2 * cos + x1 * sin], axis=-1)


def rope_2d(x, rows, cols):
    d = x.shape[-1] // 2
    return jnp.concatenate([rope_axis(x[..., :d], rows), rope_axis(x[..., d:], cols)], axis=-1)


def mlstm_scan(q, k, v, ig, lf, C0, n0, m0):
    dt = q.dtype
    f32 = jnp.float32
    mask = jnp.tril(jnp.ones((CHUNK, CHUNK), bool))
    xs = (to_chunks(q.astype(f32)), to_chunks(k.astype(f32) * (M_DK ** -0.5)), to_chunks(v.astype(f32)),
          to_chunks(ig.astype(f32)), to_chunks(lf.astype(f32)))

    def step(carry, inp):
        C, n, m = carry
        qc, kc, vc, ic, fc = inp
        b = jnp.cumsum(fc, axis=-1)
        a = b + m[..., None]
        dmat = jnp.where(mask, b[..., :, None] - b[..., None, :] + ic[..., None, :], -jnp.inf)
        mloc = jnp.maximum(a, dmat.max(-1))
        s = jnp.einsum('bhtd,bhsd->bhts', qc, kc) * jnp.exp(dmat - mloc[..., None])
        inter = jnp.exp(a - mloc)
        num = jnp.einsum('bhts,bhsv->bhtv', s, vc) + inter[..., None] * jnp.einsum('bhtd,bhdv->bhtv', qc, C)
        den = s.sum(-1) + inter * jnp.einsum('bhtd,bhd->bht', qc, n)
        h = num / jnp.maximum(jnp.abs(den), jnp.exp(-mloc))[..., None]
        bl = b[..., -1]
        g = bl[..., None] - b + ic
        m_new = jnp.maximum(bl + m, g.max(-1))
        w = jnp.exp(g - m_new[..., None])
        decay = jnp.exp(bl + m - m_new)
        C_new = decay[..., None, None] * C + jnp.einsum('bhs,bhsd,bhsv->bhdv', w, kc, vc)
        n_new = decay[..., None] * n + jnp.einsum('bhs,bhsd->bhd', w, kc)
        return (C_new, n_new, m_new), h

    (C, n, m), hs = lax.scan(step, (C0.astype(f32), n0.astype(f32), m0.astype(f32)), xs)
    return from_chunks(hs).astype(dt), (C.astype(dt), n.astype(dt), m.astype(dt))


def gla_scan(q, k, v, lg, S0):
    dt = q.dtype
    f32 = jnp.float32
    mask3 = jnp.tril(jnp.ones((CHUNK, CHUNK), bool))[:, :, None]
    xs = (to_chunks(q.astype(f32) * (G_DK ** -0.5)), to_chunks(k.astype(f32)), to_chunks(v.astype(f32)),
          to_chunks(lg.astype(f32)))

    def step(S, inp):
        qc, kc, vc, gc = inp
        bc = jnp.cumsum(gc, axis=2)
        inter = jnp.einsum('bhtd,bhdv->bhtv', qc * jnp.exp(bc), S)
        diff = bc[:, :, :, None, :] - bc[:, :, None, :, :]
        dec = jnp.exp(jnp.where(mask3, diff, -jnp.inf))
        att = jnp.einsum('bhtd,bhsd,bhtsd->bhts', qc, kc, dec)
        out = inter + jnp.einsum('bhts,bhsv->bhtv', att, vc)
        bl = bc[:, :, -1:, :]
        S_new = jnp.exp(bl[:, :, 0])[..., None] * S + jnp.einsum('bhsd,bhsv->bhdv', kc * jnp.exp(bl - bc), vc)
        return S_new, out

    S, outs = lax.scan(step, S0.astype(f32), xs)
    return from_chunks(outs).astype(dt), S.astype(dt)


def mla_attention(q_nope, q_rope, k_nope, k_rope, v):
    b, h, tq, _ = q_nope.shape
    nb = tq // Q_BLOCK
    qn = jnp.moveaxis(q_nope.reshape(b, h, nb, Q_BLOCK, -1), 2, 0)
    qr = jnp.moveaxis(q_rope.reshape(b, h, nb, Q_BLOCK, -1), 2, 0)

    def block(args):
        qnb, qrb = args
        s = jnp.einsum('bhqd,bhkd->bhqk', qnb, k_nope) + jnp.einsum('bhqr,bkr->bhqk', qrb, k_rope)
        p = jax.nn.softmax(s.astype(jnp.float32) * A_SCALE, axis=-1)
        return jnp.einsum('bhqk,bhkv->bhqv', p.astype(v.dtype), v)

    o = lax.map(block, (qn, qr))
    return jnp.moveaxis(o, 0, 2).reshape(b, h, tq, -1)


def mixer(h, lp, init_states, ctx_kv, pos):
    bsz, t, _ = h.shape
    points = [int(p) for p in np.cumsum(IN_SPLITS)[:-1]]
    (mq, mk, mv, mo, mg, gq, gk, gv, gg, ga, acq, ackv, akr) = jnp.split(h @ lp['w_in'], points, axis=-1)
    C0, n0, m0, S0 = init_states

    q, k, v = to_heads(mq, M_HEADS), to_heads(mk, M_HEADS), to_heads(mv, M_HEADS)
    gates = (mg.astype(jnp.float32) + lp['m_gate_b'].astype(jnp.float32)).reshape(bsz, t, 4, M_HEADS)
    gates = gates.transpose(2, 0, 3, 1)
    hf, (Cf, nf, mf) = mlstm_scan(q, k, v, gates[0], jax.nn.log_sigmoid(gates[1]), C0[:, 0], n0[:, 0], m0[:, 0])
    hb, (Cb, nb, mb) = mlstm_scan(flip_t(q), flip_t(k), flip_t(v), flip_t(gates[2]),
                                  flip_t(jax.nn.log_sigmoid(gates[3])), C0[:, 1], n0[:, 1], m0[:, 1])
    m_out = head_norm(from_heads(hf + flip_t(hb)), lp['m_norm_g'], M_HEADS, True) * jax.nn.sigmoid(mo)

    q, k, v = to_heads(gq, G_HEADS), to_heads(gk, G_HEADS), to_heads(gv, G_HEADS)
    lg_f = to_heads(jax.nn.log_sigmoid((ga[..., :G_RANK] @ lp['g_w2'][0] + lp['g_b2'][0]).astype(jnp.float32)) / G_TAU, G_HEADS)
    lg_b = to_heads(jax.nn.log_sigmoid((ga[..., G_RANK:] @ lp['g_w2'][1] + lp['g_b2'][1]).astype(jnp.float32)) / G_TAU, G_HEADS)
    of, Sf = gla_scan(q, k, v, lg_f, S0[:, 0])
    ob, Sb = gla_scan(flip_t(q), flip_t(k), flip_t(v), flip_t(lg_b), S0[:, 1])
    g_out = head_norm(from_heads(of + flip_t(ob)), lp['g_norm_g'], G_HEADS, False) * jax.nn.silu(gg)

    cq = rms_norm(acq, lp['a_q_norm_g'])
    qf = to_heads(cq @ lp['a_w_uq'], A_HEADS)
    q_nope, q_rope = qf[..., :A_DNOPE], qf[..., A_DNOPE:]
    ckv = rms_norm(ackv, lp['a_kv_norm_g'])
    if pos is None:
        kr_own = akr
    else:
        rows, cols = pos
        q_rope = rope_2d(q_rope, rows, cols)
        kr_own = rope_2d(akr, rows, cols)
    if ctx_kv is None:
        ckv_all, kr_all = ckv, kr_own
    else:
        ckv_all = jnp.concatenate([ctx_kv[0], ckv], axis=1)
        kr_all = jnp.concatenate([ctx_kv[1], kr_own], axis=1)
    kv = to_heads(ckv_all @ lp['a_w_ukv'], A_HEADS)
    a_out = from_heads(mla_attention(q_nope, q_rope, kv[..., :A_DNOPE], kr_all, kv[..., A_DNOPE:]))

    out = jnp.concatenate([m_out, g_out, a_out], axis=-1) @ lp['w_out']
    new = (jnp.stack([Cf, Cb], axis=1), jnp.stack([nf, nb], axis=1), jnp.stack([mf, mb], axis=1),
           jnp.stack([Sf, Sb], axis=1), ckv, akr)
    return out, new


def layer(x, cond, lp, init_states, ctx_kv, pos):
    mod = jax.nn.silu(cond) @ lp['w_ada'] + lp['b_ada']
    sh1, sc1, g1, sh2, sc2, g2 = jnp.split(mod[:, None, :], 6, axis=-1)
    mix, new = mixer(x * (1 + sc1) + sh1, lp, init_states, ctx_kv, pos)
    x = layer_norm(ALPHA * x + g1 * mix, lp['ln1_g'], lp['ln1_b'])
    hh = x * (1 + sc2) + sh2
    ffn = (jax.nn.silu(hh @ lp['w_ffn_gate']) * (hh @ lp['w_ffn_up'])) @ lp['w_ffn_down']
    x = layer_norm(ALPHA * x + g2 * ffn, lp['ln2_g'], lp['ln2_b'])
    return x, new


def setup_inputs(seed: int = 0) -> dict:
    key = jax.random.key(seed)
    ks = jax.random.split(key, 40)

    def nrm(i, shape, scale=1.0):
        return jax.random.normal(ks[i], shape, jnp.float32) * scale

    D, L = D_MODEL, DEPTH
    gate_base = jnp.repeat(jnp.array([0.0, 3.0, 0.0, 3.0], jnp.float32), M_HEADS)
    return {
        'x_prompt': nrm(0, (BATCH, SEQ, D)),
        'x_sample': nrm(1, (DEC_BATCH, DEC_SEQ, D)),
        'state_mlstm_C': nrm(2, (DEC_BATCH, L, 2, M_HEADS, M_DK, M_DV), 0.5),
        'state_mlstm_n': nrm(3, (DEC_BATCH, L, 2, M_HEADS, M_DK), 0.5),
        'state_mlstm_m': nrm(4, (DEC_BATCH, L, 2, M_HEADS)),
        'state_gla_S': nrm(5, (DEC_BATCH, L, 2, G_HEADS, G_DK, G_DV), 0.5),
        'cache_mla_ckv': nrm(6, (DEC_BATCH, L, PAST_LEN, A_DC)),
        'cache_mla_krope': nrm(7, (DEC_BATCH, L, PAST_LEN, A_DROPE)),
        'c': nrm(8, (DEC_BATCH, D)),
        'c_ctx': nrm(9, (D,)),
        'w_ada': nrm(10, (L, D, 6 * D), 0.5 * D ** -0.5),
        'b_ada': nrm(11, (L, 6 * D), 0.02),
        'w_in': nrm(12, (L, D, IN_COLS), D ** -0.5),
        'm_gate_b': gate_base + nrm(13, (L, 4 * M_HEADS), 0.1),
        'm_norm_g': 1.0 + nrm(14, (L, M_HEADS * M_DV), 0.02),
        'g_w2': nrm(15, (L, 2, G_RANK, G_HEADS * G_DK), G_RANK ** -0.5),
        'g_b2': nrm(16, (L, 2, G_HEADS * G_DK), 0.1),
        'g_norm_g': 1.0 + nrm(17, (L, G_HEADS * G_DV), 0.02),
        'a_q_norm_g': 1.0 + nrm(18, (L, A_DQ), 0.02),
        'a_kv_norm_g': 1.0 + nrm(19, (L, A_DC), 0.02),
        'a_w_uq': nrm(20, (L, A_DQ, A_HEADS * (A_DNOPE + A_DROPE)), A_DQ ** -0.5),
        'a_w_ukv': nrm(21, (L, A_DC, A_HEADS * (A_DNOPE + A_DV)), A_DC ** -0.5),
        'w_out': nrm(22, (L, D_MIX, D), BETA * D_MIX ** -0.5),
        'ln1_g': 1.0 + nrm(23, (L, D), 0.02),
        'ln1_b': nrm(24, (L, D), 0.02),
        'w_ffn_gate': nrm(25, (L, D, FFN_HIDDEN), D ** -0.5),
        'w_ffn_up': nrm(26, (L, D, FFN_HIDDEN), D ** -0.5),
        'w_ffn_down': nrm(27, (L, FFN_HIDDEN, D), BETA * FFN_HIDDEN ** -0.5),
        'ln2_g': 1.0 + nrm(28, (L, D), 0.02),
        'ln2_b': nrm(29, (L, D), 0.02),
    }


def reference(x_prompt, x_sample, state_mlstm_C, state_mlstm_n, state_mlstm_m, state_gla_S,
              cache_mla_ckv, cache_mla_krope, c, c_ctx, w_ada, b_ada, w_in, m_gate_b, m_norm_g,
              g_w2, g_b2, g_norm_g, a_q_norm_g, a_kv_norm_g, a_w_uq, a_w_ukv, w_out, ln1_g, ln1_b,
              w_ffn_gate, w_ffn_up, w_ffn_down, ln2_g, ln2_b):
    def layer_params(l):
        return {'w_ada': w_ada[l], 'b_ada': b_ada[l], 'w_in': w_in[l], 'm_gate_b': m_gate_b[l],
                'm_norm_g': m_norm_g[l], 'g_w2': g_w2[l], 'g_b2': g_b2[l], 'g_norm_g': g_norm_g[l],
                'a_q_norm_g': a_q_norm_g[l], 'a_kv_norm_g': a_kv_norm_g[l], 'a_w_uq': a_w_uq[l],
                'a_w_ukv': a_w_ukv[l], 'w_out': w_out[l], 'ln1_g': ln1_g[l], 'ln1_b': ln1_b[l],
                'w_ffn_gate': w_ffn_gate[l], 'w_ffn_up': w_ffn_up[l], 'w_ffn_down': w_ffn_down[l],
                'ln2_g': ln2_g[l], 'ln2_b': ln2_b[l]}

    bp = x_prompt.shape[0]
    dt = x_prompt.dtype
    zero_init = (jnp.zeros((bp, 2, M_HEADS, M_DK, M_DV), dt), jnp.zeros((bp, 2, M_HEADS, M_DK), dt),
                 jnp.zeros((bp, 2, M_HEADS), dt), jnp.zeros((bp, 2, G_HEADS, G_DK, G_DV), dt))
    cond_ctx = c_ctx[None, :]
    y_prompt = x_prompt
    per_layer = []
    for l in range(DEPTH):
        y_prompt, new = layer(y_prompt, cond_ctx, layer_params(l), zero_init, None, None)
        per_layer.append(new)
    new_mlstm_C = jnp.stack([s[0] for s in per_layer], axis=1)
    new_mlstm_n = jnp.stack([s[1] for s in per_layer], axis=1)
    new_mlstm_m = jnp.stack([s[2] for s in per_layer], axis=1)
    new_gla_S = jnp.stack([s[3] for s in per_layer], axis=1)
    new_mla_ckv = jnp.stack([s[4] for s in per_layer], axis=1)
    new_mla_krope = jnp.stack([s[5] for s in per_layer], axis=1)

    n_lat = x_sample.shape[1]
    grid_rows = n_lat // GRID_W
    rows = jnp.repeat(jnp.arange(grid_rows, dtype=jnp.int32), GRID_W)
    cols = jnp.tile(jnp.arange(GRID_W, dtype=jnp.int32), grid_rows)
    y_sample = x_sample
    for l in range(DEPTH):
        init = (state_mlstm_C[:, l], state_mlstm_n[:, l], state_mlstm_m[:, l], state_gla_S[:, l])
        y_sample, _ = layer(y_sample, c, layer_params(l), init,
                            (cache_mla_ckv[:, l], cache_mla_krope[:, l]), (rows, cols))

    return (y_prompt, y_sample, new_mlstm_C, new_mlstm_n, new_mlstm_m, new_gla_S, new_mla_ckv, new_mla_krope)
```

```python
import types
import numpy as np
import ml_dtypes
from contextlib import ExitStack
import concourse.bass as bass
import concourse.mybir as mybir
from concourse.bass_utils import run_bass_kernel_spmd

F32 = mybir.dt.float32
BF16 = mybir.dt.bfloat16
AF = mybir.ActivationFunctionType
ALU = mybir.AluOpType
AX = mybir.AxisListType

DEPTH = 4
DEBUG = False
LOWP = True
STAGE = 0
T = 2048
NT = 16
NB = 4
KC = 8
NKEY = 2304
NKT = 18
FH = 2816
NJ = 22
DKS = 32 ** -0.5
A_SCALE = 96 ** -0.5
ALPHA = (2 * DEPTH) ** 0.25
NEG = -30000.0
NINF = -1.0e30


def _freeze(fn):
    if fn is None or fn.__closure__ is None:
        return fn
    cells = []
    for c in fn.__closure__:
        try:
            cells.append(types.CellType(c.cell_contents))
        except ValueError:
            cells.append(c)
    return types.FunctionType(fn.__code__, fn.__globals__, fn.__name__, fn.__defaults__, tuple(cells))


class Prog:
    ENGS = ("pe", "act", "dve", "pool", "sp")

    def __init__(self, nc):
        self.nc = nc
        self.ops = {e: [] for e in self.ENGS}
        self.cnt = {}
        self.sem_names = []
        self.waited = {e: {} for e in self.ENGS}
        self.last_w = {}
        self.reads = {}
        self.out_tickets = []

    def _getsem(self, name):
        if name not in self.cnt:
            self.cnt[name] = 0
            self.sem_names.append(name)
        return name

    def _deps(self, eng, reads, writes):
        need = {}

        def add(t):
            if t is None:
                return
            s, v = t
            if eng == "pe" and s == "c_pe":
                return
            if need.get(s, 0) < v:
                need[s] = v
        for k in reads:
            add(self.last_w.get(k))
        for k in writes:
            add(self.last_w.get(k))
            for t in self.reads.get(k, ()):
                add(t)
        waits = []
        wd = self.waited[eng]
        for s, v in need.items():
            if wd.get(s, 0) < v:
                wd[s] = v
                waits.append((s, v))
        return waits

    def _commit(self, ticket, reads, writes):
        for k in reads:
            self.reads.setdefault(k, []).append(ticket)
        for k in writes:
            self.last_w[k] = ticket
            self.reads[k] = []

    def op(self, eng, fn, reads=(), writes=()):
        reads = tuple(reads)
        writes = tuple(writes)
        waits = self._deps(eng, reads, writes)
        s = self._getsem("c_" + eng)
        self.cnt[s] += 1
        self.ops[eng].append((waits, _freeze(fn), s, 1))
        self._commit((s, self.cnt[s]), reads, writes)

    def dma(self, fn, reads=(), writes=(), semkey=None, is_out=False, eng="sp"):
        reads = tuple(reads)
        writes = tuple(writes)
        waits = self._deps(eng, reads, writes)
        s = self._getsem("d_" + str(semkey))
        self.cnt[s] += 16
        self.ops[eng].append((waits, _freeze(fn), s, 16))
        self._commit((s, self.cnt[s]), reads, writes)
        if is_out:
            self.out_tickets.append((s, self.cnt[s]))

    def barrier(self):
        allc = [(s, v) for s, v in self.cnt.items() if v > 0]
        for e in self.ENGS:
            waits = []
            wd = self.waited[e]
            for s, v in allc:
                if e == "pe" and s == "c_pe":
                    continue
                if wd.get(s, 0) < v:
                    wd[s] = v
                    waits.append((s, v))
            if waits:
                self.ops[e].append((waits, None, None, 0))
        self.last_w = {}
        self.reads = {}

    def emit(self):
        nc = self.nc
        fin = {}
        for s, v in self.out_tickets:
            fin[s] = max(fin.get(s, 0), v)
        sem = {}
        with ExitStack() as st:
            for i, name in enumerate(self.sem_names):
                sem[name] = st.enter_context(nc.semaphore("s%d" % i))
            block = st.enter_context(nc.Block())

            def replay(ename):
                def body(e):
                    for waits, fn, s, inc in self.ops[ename]:
                        for ws, wv in waits:
                            e.wait_ge(sem[ws], wv)
                        if fn is not None:
                            fn(e).then_inc(sem[s], inc)
                    if ename == "sp":
                        for s, v in fin.items():
                            e.wait_ge(sem[s], v)
                return body
            block.tensor(replay("pe"))
            block.scalar(replay("act"))
            block.vector(replay("dve"))
            block.gpsimd(replay("pool"))
            block.sync(replay("sp"))


def build(L=DEPTH):
    nc = bass.Bass("TRN2", target_bir_lowering=False)

    def din(name, shape, dt=F32):
        return nc.dram_tensor(name, list(shape), dt, kind="ExternalInput").ap()

    def dout(name, shape):
        return nc.dram_tensor(name, list(shape), F32, kind="ExternalOutput").ap()

    d_x = din("x", [T, 1024])
    d_cond = din("condT", [128, 8])
    d_wada = din("w_ada", [L, 1024, 6144])
    d_bada = din("b_adaT", [L, 128, 48])
    d_win = din("w_in", [L, 1024, 2000])
    d_wgi = din("w_gi", [L, 1024, 64])
    d_wgf = din("w_gf", [L, 1024, 64])
    d_gb = din("gb", [L, 64, 2])
    d_wkr = din("w_kr", [L, 1024, 128])
    d_wkrr = din("w_krr", [L, 1024, 128])
    d_gw2 = din("g_w2", [L, 2, 16, 128])
    d_gb2 = din("g_b2T", [L, 128, 2])
    d_ng = din("normg", [L, 128, 7])
    d_ln = din("lnp", [L, 128, 32])
    d_wuq = din("w_uqp", [L, 8, 256, 128])
    d_wuqr = din("w_uqr", [L, 8, 256, 128])
    d_wukv = din("w_ukv", [L, 128, 1024])
    d_wout = din("w_out", [L, 1024, 1024])
    d_wg = din("w_fg", [L, 1024, FH])
    d_wu = din("w_fu", [L, 1024, FH])
    d_wd = din("w_fd", [L, FH, 1024])
    d_cn0 = din("cn0", [L, 2, 128, 260])
    d_m0 = din("m0", [L, 64, 1])
    d_s0 = din("s0", [L, 2, 128, 256])
    d_ckvc = din("ckv_ctx", [L, 256, 128])
    d_krc = din("kr_ctx", [L, 256, 128])
    d_cst = din("cst", [128, 1292])
    d_rz = din("rz", [64, 32])
    d_rn = din("rn", [128, 32])
    d_kaug = din("kaugc", [16, NKEY], BF16 if LOWP else F32)
    d_qaug = din("qaugc", [16, T], BF16 if LOWP else F32)
    d_tab = din("ropetab", [128, 192])

    o_y = dout("o_y", [T, 1024])
    o_C = dout("o_C", [L, 8, 2, 128, 64])
    o_n = dout("o_n", [L, 16, 128])
    o_m = dout("o_m", [L, 2, 4, 8])
    o_S = dout("o_S", [L, 8, 2, 128, 64])
    o_ckv = dout("o_ckv", [L, T, 128])
    o_kr = dout("o_kr", [L, T, 128])

    o_dbg = dout("o_dbg", [128, 16384]) if DEBUG else None
    dbg_map = {}
    dbg_off = [0]
    P = Prog(nc)

    def DBG(name, ap, key):
        if not DEBUG:
            return
        n = ap.shape[1]
        p = ap.shape[0]
        off = dbg_off[0]
        dbg_off[0] += n
        dbg_map[name] = (off, p, n)
        P.dma(lambda e: e.dma_start(out=o_dbg[0:p, off:off + n], in_=ap), reads=[key], writes=(), semkey="dbg", is_out=True)
    build.dbg_map = dbg_map
    with ExitStack() as st:
        def sb(name, shape):
            return st.enter_context(nc.sbuf_tensor("sb_" + name, list(shape), F32))

        xT = sb("xT", [128, 8, T])
        mixT = sb("mixT", [128, 8, T])
        ARW = 17664
        arena = sb("arena", [128, ARW])
        cst = sb("cst", [128, 1292])
        MOD = sb("MOD", [128, L, 48])
        OP1 = sb("OP1", [128, L, 16])
        NG = sb("NG", [128, L, 7])
        LNP = sb("LNP", [128, L, 32])
        GB2 = sb("GB2", [128, L, 2])
        sc8 = sb("sc8", [128, 8])
        RZ = sb("RZ", [64, 32])
        RN = sb("RN", [128, 32])
        TAB = sb("TAB", [128, 192])
        bcolT = sb("bcolT", [128, 2])
        bn16 = sb("bn16", [16, 256])
        browT = bn16[0:1, :]
        gbt = sb("gbt", [64, 2])
        st4T = sb("st4T", [128, 8])
        ngb2 = sb("ngb2", [128, 2])
        noutT = bn16[:, 0:128]
        ps = [st.enter_context(nc.psum_tensor("psum%d" % i, [128, 512], F32)) for i in range(8)]

        ident = cst[:, 0:128]
        ones = cst[:, 128:256]
        m01f = cst[:, 256:384]
        m01b = cst[:, 384:512]
        mngf = cst[:, 512:640]
        mngb = cst[:, 640:768]
        bmask = cst[:, 768:772]
        BI = cst[0:64, 772:1284]
        nbi4 = cst[0:64, 1284:1288]
        id4 = cst[0:64, 1288:1292]
        bit = sb("bit", [64, 128])
        d_bit = din("bit", [64, 128])

        def AR(off, n):
            assert off + n <= ARW, (off, n)
            return arena[:, off:off + n]

        def MX(off, n):
            flat = mixT[:, 2:8, :].rearrange("p c t -> p (c t)")
            return flat[:, off:off + n]

        rr = [0]

        def evq():
            rr[0] ^= 1
            return "act" if rr[0] else "dve"

        def mm(out, lhsT, rhs, start, stop, reads, writes):
            P.op("pe", lambda e: e.matmul(out, lhsT=lhsT, rhs=rhs, start=start, stop=stop), reads, writes)

        def tr(out, in_, reads, writes):
            P.op("pe", lambda e: e.transpose(out=out, in_=in_, identity=ident[0:in_.shape[0], 0:in_.shape[0]]), list(reads) + ["cst"], writes)

        def act(out, in_, func, reads, writes, bias=0.0, scale=1.0):
            P.op("act", lambda e: e.activation(out=out, in_=in_, func=func, bias=bias, scale=scale), reads, writes)

        def ld(out, in_, key, reads=(), semkey=None):
            P.dma(lambda e: e.dma_start(out=out, in_=in_), reads=reads, writes=[key], semkey=semkey or key)

        def st_out(out, in_, key, semkey, nc_ok=False):
            if nc_ok:
                P.dma(lambda e: e.dma_start(out=out, in_=in_, allow_slow_non_contiguous=True), reads=[key], writes=(), semkey=semkey, is_out=True)
            else:
                P.dma(lambda e: e.dma_start(out=out, in_=in_), reads=[key], writes=(), semkey=semkey, is_out=True)

        def V(eng, fn, reads, writes):
            P.op(eng, fn, reads, writes)

        ld(cst[:], d_cst[:, :], "cst")
        ld(bit[:], d_bit[:, :], "bit")
        ld(RZ[:], d_rz[:, :], "RZ")
        ld(RN[:], d_rn[:, :], "RN")
        ld(TAB[:], d_tab[:, :], "TAB")
        ld(sc8[:], d_cond[:, :], "sc8")
        for l in range(L):
            ld(NG[:, l, :], d_ng[l], "NG", semkey="NG")
            ld(LNP[:, l, :], d_ln[l], "LNP", semkey="LNP")
            ld(GB2[:, l, :], d_gb2[l], "GB2", semkey="GB2")
            ld(MOD[:, l, :], d_bada[l], "MODb", semkey="MODb")
        act(sc8[:], sc8[:], AF.Silu, ["sc8"], ["sc8"])

        stg = [AR(4096, 1024), AR(5120, 1024)]
        for tt in range(NT):
            s_ = stg[tt % 2]
            ld(s_, d_x[tt * 128:(tt + 1) * 128, :], "stg%d" % (tt % 2))
            for half in range(2):
                pb = ps[(tt * 2 + half) % 4]
                pk = "ps%d" % ((tt * 2 + half) % 4)
                for q in range(4):
                    c = half * 4 + q
                    tr(pb[:, q * 128:(q + 1) * 128], s_[:, c * 128:(c + 1) * 128], ["stg%d" % (tt % 2)], [pk])
                e_ = evq()
                dst = xT[:, half * 4:half * 4 + 4, tt * 128:(tt + 1) * 128]
                src = pb[:, :].rearrange("p (q t) -> p q t", q=4)
                if e_ == "act":
                    V("act", lambda e, dst=dst, src=src: e.copy(out=dst, in_=src), [pk], ["x%d" % (tt // 4)])
                else:
                    V("dve", lambda e, dst=dst, src=src: e.tensor_copy(out=dst, in_=src), [pk], ["x%d" % (tt // 4)])
        DBG("xT0", xT[:, 0, 0:256], "x0")
        DBG("xT7", xT[:, 7, 1792:2048], "x3")

        WS = [AR(0, 2048), AR(2048, 2048)]
        wi = [0]

        def wslot():
            i = wi[0]
            wi[0] ^= 1
            return WS[i], "w%d" % i

        for l in range(L):
            for g in range(24):
                W_, wk = wslot()
                Wv = W_.rearrange("p (c n) -> p c n", c=8)
                ld(Wv, d_wada[l].rearrange("(c p) n -> p c n", p=128)[:, :, g * 256:(g + 1) * 256], wk)
                for jj in range(2):
                    j = g * 2 + jj
                    for c in range(KC):
                        mm(ps[4][:, j:j + 1], Wv[:, c, jj * 128:(jj + 1) * 128], sc8[:, c:c + 1], c == 0, c == KC - 1, [wk, "sc8"], ["ps4"])
            V("dve", lambda e, l=l: e.tensor_tensor(out=MOD[:, l, :], in0=ps[4][:, 0:48], in1=MOD[:, l, :], op=ALU.add), ["ps4", "MODb"], ["MOD"])
            V("dve", lambda e, l=l: e.tensor_scalar(out=OP1[:, l, 0:8], in0=MOD[:, l, 8:16], scalar1=1.0, scalar2=None, op0=ALU.add), ["MOD"], ["OP1"])
            V("dve", lambda e, l=l: e.tensor_scalar(out=OP1[:, l, 8:16], in0=MOD[:, l, 32:40], scalar1=1.0, scalar2=None, op0=ALU.add), ["MOD"], ["OP1"])
        DBG("MOD0", MOD[:, 0, :], "MOD")
        DBG("sc8", sc8[:, :], "sc8")
        P.barrier()

        XK = ["x0", "x1", "x2", "x3"]

        def load_w(dram_ap, ncols):
            W_, wk = wslot()
            Wv = W_[:, 0:8 * ncols].rearrange("p (c n) -> p c n", c=8)
            wks = ["%s_%d" % (wk, c) for c in range(KC)]
            P.dma(lambda e: e.dma_start(out=Wv, in_=dram_ap.rearrange("(c p) n -> p c n", p=128)), reads=(), writes=wks, semkey=wk)
            return Wv, wks

        def scale_w(Wv, wks, l, which):
            for c in range(KC):
                if c % 2 == 0:
                    V("dve", lambda e, c=c: e.tensor_scalar(out=Wv[:, c, :], in0=Wv[:, c, :], scalar1=OP1[:, l, which * 8 + c:which * 8 + c + 1],
                                                            scalar2=None, op0=ALU.mult), [wks[c], "OP1"], [wks[c]])
                else:
                    act(Wv[:, c, :], Wv[:, c, :], AF.Copy, [wks[c], "OP1"], [wks[c]], scale=OP1[:, l, which * 8 + c:which * 8 + c + 1])

        pend_main = [None]
        bsel = [0]

        def flush():
            if pend_main[0] is not None:
                f_ = pend_main[0]
                pend_main[0] = None
                f_()

        def BAR():
            flush()
            P.barrier()

        def proj_fm(l, dram_ap, ncols, evac, which=0, extra_bias=None, bkey="bcol"):
            Wv, wks = load_w(dram_ap, ncols)
            sh0 = 0 if which == 0 else 24
            for c in range(KC):
                mm(ps[6][0:ncols, 0:1], Wv[:, c, :], MOD[:, l, sh0 + c:sh0 + c + 1], c == 0, c == KC - 1, [wks[c], "MOD"], ["ps6"])
            kb = bsel[0]
            bsel[0] ^= 1
            bcol = bcolT[:, kb:kb + 1]
            bkey = "bcol%d" % kb
            if extra_bias is None:
                V("dve", lambda e: e.tensor_copy(out=bcol[0:ncols, :], in_=ps[6][0:ncols, 0:1]), ["ps6"], [bkey])
            else:
                V("dve", lambda e: e.tensor_tensor(out=bcol[0:ncols, :], in0=ps[6][0:ncols, 0:1], in1=extra_bias, op=ALU.add), ["ps6", "gbt"], [bkey])
            scale_w(Wv, wks, l, which)
            flush()

            def main():
                for tb in range(NB):
                    pb = ps[tb % 2]
                    pk = "ps%d" % (tb % 2)
                    for c in range(KC):
                        mm(pb[0:ncols, :], Wv[:, c, :], xT[:, c, tb * 512:(tb + 1) * 512], c == 0, c == KC - 1, [wks[c], XK[tb]], [pk])
                    evac(tb, pb[0:ncols, :], pk, bcol[0:ncols, :], bkey)
            pend_main[0] = main

        def proj_tm(l, dram_ap, ncols, evac):
            flush()
            Wv, wks = load_w(dram_ap, ncols)
            for c in range(KC):
                mm(ps[6][0:1, 0:ncols], MOD[:, l, c:c + 1], Wv[:, c, :], c == 0, c == KC - 1, [wks[c], "MOD"], ["ps6"])
            brow = browT[0:1, 0:ncols]
            V("dve", lambda e: e.tensor_copy(out=brow, in_=ps[6][0:1, 0:ncols]), ["ps6"], ["brow"])
            scale_w(Wv, wks, l, 0)
            for tt in range(NT):
                pb = ps[tt % 2]
                pk = "ps%d" % (tt % 2)
                for c in range(KC):
                    mm(pb[:, 0:ncols], xT[:, c, tt * 128:(tt + 1) * 128], Wv[:, c, :], c == 0, False, [wks[c], XK[tt // 4]], [pk])
                mm(pb[:, 0:ncols], ones[0:1, 0:128], brow, False, True, ["cst", "brow"], [pk])
                evac(tt, pb[:, 0:ncols], pk)

        def ev_copy_bias(dst_fn, key_fn, scale=None):
            def f(tb, pap, pk, bcol, bkey):
                dst = dst_fn(tb)
                if scale is None:
                    act(dst, pap, AF.Identity, [pk, bkey], [key_fn(tb)], bias=bcol)
                else:
                    V("dve", lambda e: e.tensor_scalar(out=dst, in0=pap, scalar1=bcol, scalar2=scale, op0=ALU.add, op1=ALU.mult), [pk, bkey], [key_fn(tb)])
            return f

        def ev_func_bias(dst_fn, key_fn, func):
            def f(tb, pap, pk, bcol, bkey):
                act(dst_fn(tb), pap, func, [pk, bkey], [key_fn(tb)], bias=bcol)
            return f

        def headnorm_out(src3, nh, center, l, gidx, mixc0, tt, srckey, pbi=7, kk=0):
            st4 = st4T[:, 0:nh]
            st4b = st4T[:, 4:4 + nh]
            flat = MXT["hn_tmp"][kk]
            hk = "hn_tmp%d" % kk
            tmp = flat.rearrange("p (h d) -> p h d", h=nh)
            if center:
                V("dve", lambda e: e.tensor_reduce(out=st4, in_=src3, axis=AX.X, op=ALU.add), [srckey], ["st4"])
                V("dve", lambda e: e.tensor_scalar(out=st4, in0=st4, scalar1=-1.0 / 64, scalar2=None, op0=ALU.mult), ["st4"], ["st4"])
                V("dve", lambda e: e.tensor_tensor(out=src3, in0=src3, in1=st4.unsqueeze(2).to_broadcast([128, nh, 64]), op=ALU.add), [srckey, "st4"], [srckey])
            V("dve", lambda e: e.tensor_tensor(out=tmp, in0=src3, in1=src3, op=ALU.mult), [srckey], [hk])
            V("dve", lambda e: e.tensor_reduce(out=st4b, in_=tmp, axis=AX.X, op=ALU.add), [hk], ["st4b"])
            act(st4b, st4b, AF.Ln, ["st4b", "EPS"], ["st4b"], bias=EPS6[:, 0:1], scale=1.0 / 64)
            act(st4b, st4b, AF.Exp, ["st4b"], ["st4b"], scale=-0.5)
            V("dve", lambda e: e.tensor_tensor(out=tmp, in0=src3, in1=st4b.unsqueeze(2).to_broadcast([128, nh, 64]), op=ALU.mult), [srckey, "st4b"], [hk])

            def fin():
                pkh = "ps%d" % pbi
                for j in range(2):
                    tr(ps[pbi][:, j * 128:(j + 1) * 128], flat[:, j * 128:(j + 1) * 128], [hk], [pkh])
                for j in range(2):
                    V("dve", lambda e, j=j: e.scalar_tensor_tensor(out=mixT[:, mixc0 + j, tt * 128:(tt + 1) * 128], in0=ps[pbi][:, j * 128:(j + 1) * 128],
                                                                   scalar=NG[:, l, gidx + j:gidx + j + 1], in1=mixT[:, mixc0 + j, tt * 128:(tt + 1) * 128],
                                                                   op0=ALU.mult, op1=ALU.mult), [pkh, "NG", "mix%d" % mixc0], ["mix%d" % mixc0])
            return fin

        EPS6 = sb("EPS6", [128, 2])
        V("pool", lambda e: e.memset(EPS6[:, 0:1], 1e-6), [], ["EPS"])
        V("pool", lambda e: e.memset(EPS6[:, 1:2], 1e-5), [], ["EPS"])
        MXT = {}

        def layernorm_block(l, tb, gi, bi, LNT, LNK):
            xk = XK[tb]
            sl = slice(tb * 512, (tb + 1) * 512)
            for c in range(KC):
                mm(ps[4][:, :], ones[:, 0:128], xT[:, c, sl], c == 0, c == KC - 1, ["cst", xk], ["ps4"])
            mean = LNT[0]
            V("dve", lambda e: e.tensor_scalar(out=mean, in0=ps[4][:, :], scalar1=-1.0 / 1024, scalar2=None, op0=ALU.mult), ["ps4"], [LNK[0]])
            sq = LNT[1]
            for c in range(KC):
                V("dve", lambda e, c=c: e.tensor_tensor(out=xT[:, c, sl], in0=xT[:, c, sl], in1=mean, op=ALU.add), [xk, LNK[0]], [xk])
                act(sq[c % 2], xT[:, c, sl], AF.Square, [xk], [LNK[1][c % 2]])
                mm(ps[5][:, :], ones[:, 0:128], sq[c % 2], c == 0, c == KC - 1, ["cst", LNK[1][c % 2]], ["ps5"])
            rstd = LNT[2]
            act(rstd, ps[5][:, :], AF.Sqrt, ["ps5", "EPS"], [LNK[2]], bias=EPS6[:, 1:2], scale=1.0 / 1024)
            V("dve", lambda e: e.reciprocal(out=rstd, in_=rstd), [LNK[2]], [LNK[2]])
            for c in range(KC):
                V("dve", lambda e, c=c: e.scalar_tensor_tensor(out=xT[:, c, sl], in0=xT[:, c, sl], scalar=LNP[:, l, gi + c:gi + c + 1], in1=rstd,
                                                               op0=ALU.mult, op1=ALU.mult), [xk, LNK[2], "LNP"], [xk])
                act(xT[:, c, sl], xT[:, c, sl], AF.Identity, [xk, "LNP"], [xk], bias=LNP[:, l, bi + c:bi + c + 1])

        for l in range(L):
            win = d_win[l]
            qT_m = AR(4096, 2048)
            kT_m = AR(6144, 2048)
            v_aug = AR(8192, 4160).rearrange("p (t h d) -> p t h d", t=16, h=4)
            RA = AR(12352, 2048)
            RB = AR(14400, 2048)
            Cn = AR(0, 260 * 2).rearrange("p (d n) -> p d n", d=2)
            DEC = AR(520, 32)
            stC = AR(552, 16 * 65).rearrange("p (s n) -> p s n", s=16)
            hsum = MX(0, 4096).rearrange("p (t n) -> p t n", t=16)
            kTM = MX(4096, 2048).rearrange("p (t n) -> p t n", t=16)
            RC = MX(6144, 2048)
            Qblk = MX(8192, 512)
            NMB = MX(8704, 512)
            Dsb = MX(9216, 512)
            Ssb = MX(9728, 512)
            wkt = MX(10240, 128)
            qi = MX(10368, 128)
            Utmp = MX(10496, 260)
            ecol = MX(10756, 8)
            hout = MX(10764, 256)
            MXT["hn_tmp"] = [MX(11020, 256), AR(1980, 256)]
            itmp = MX(11276, 128)
            iexp = MX(11404, 128)
            MST = MX(11532, 16)
            MPE = MX(11548, 16)
            DLG = MX(11564, 16)
            CML = MX(11580, 16)
            BLR = MX(11596, 16)
            D0 = MX(11612, 16)
            D1 = MX(11628, 16)
            r4 = MX(11644, 4)
            r4b = MX(11648, 4)
            m0t = MX(11652, 1)

            proj_fm(l, win[:, 0:128], 128, ev_copy_bias(lambda tb: qT_m[:, tb * 512:(tb + 1) * 512], lambda tb: "qTm"))
            proj_fm(l, win[:, 128:256], 128, ev_copy_bias(lambda tb: kT_m[:, tb * 512:(tb + 1) * 512], lambda tb: "kTm", scale=DKS))
            proj_tm(l, win[:, 128:256], 128, lambda tt, pap, pk: act(kTM[:, tt, :], pap, AF.Copy, [pk], ["kTM"], scale=DKS))
            V("pool", lambda e: e.memset(v_aug[:, :, :, 64:65], 1.0), [], ["vaug"])
            proj_tm(l, win[:, 256:512], 256, lambda tt, pap, pk: V(evq2(), lambda e: e.tensor_copy(out=v_aug[:, tt, :, 0:64], in_=pap.rearrange("p (h d) -> p h d", h=4)), [pk], ["vaug"]))
            for j in range(2):
                proj_fm(l, win[:, 512 + j * 128:640 + j * 128], 128, ev_func_bias(lambda tb, j=j: mixT[:, j, tb * 512:(tb + 1) * 512], lambda tb: "mix0", AF.Sigmoid))
            ld(gbt[:, :], d_gb[l], "gbt")
            proj_fm(l, d_wgi[l], 64, ev_copy_bias(lambda tb: RA[0:64, tb * 512:(tb + 1) * 512], lambda tb: "RA"), extra_bias=gbt[0:64, 0:1])
            proj_fm(l, d_wgf[l], 64, ev_copy_bias(lambda tb: RB[0:64, tb * 512:(tb + 1) * 512], lambda tb: "RB"), extra_bias=gbt[0:64, 1:2])
            if DEBUG:
                flush()
            DBG("qTm", qT_m[:, 0:256], "qTm")
            DBG("kTm", kT_m[:, 0:256], "kTm")
            DBG("kTM", kTM[:, 0, :], "kTM")
            DBG("vaug", v_aug[:, 0, :, :].rearrange("p h n -> p (h n)"), "vaug")
            DBG("mo", mixT[:, 0, 0:256], "mix0")
            DBG("RA0", RA[0:64, 0:256], "RA")
            DBG("RB0", RB[0:64, 0:256], "RB")
            BAR()
            if STAGE == 1:
                P.emit()
                return nc
            ld(Cn[:, :, :], d_cn0[l].rearrange("d p n -> p d n"), "Cn")
            ld(m0t[0:64, :], d_m0[l], "m0t")
            V("pool", lambda e: e.memset(RC[0:64, :], 0.0), [], ["RC"])
            V("pool", lambda e: e.memset(MX(11532, 112)[0:64, :], 0.0), [], ["MST", "MPE", "DLG", "CML", "BLR", "D0", "D1"])
            V("pool", lambda e: e.memset(itmp[0:64, :], 0.0), [], ["itmp"])
            R36 = slice(0, 36)
            act(RB[R36, :], RB[R36, :], AF.Exp, ["RB"], ["RB"], scale=-1.0)
            act(RB[R36, :], RB[R36, :], AF.Ln, ["RB"], ["RB"], bias=1.0)
            for c in range(NT):
                sl = slice(c * 128, (c + 1) * 128)
                V("dve", lambda e, sl=sl: e.tensor_tensor_scan(out=RC[0:4, sl], data0=ones[0:4, 0:128], data1=RB[0:4, sl], initial=0.0, op0=ALU.mult, op1=ALU.add), ["RB", "cst"], ["RC"])
                rs = slice((c + 1) * 128 - 1, c * 128 - 1 if c > 0 else None, -1)
                V("dve", lambda e, rs=rs: e.tensor_tensor_scan(out=RC[32:36, rs], data0=ones[32:36, 0:128], data1=RB[32:36, rs], initial=0.0, op0=ALU.mult, op1=ALU.add), ["RB", "cst"], ["RC"])
            V("dve", lambda e: e.tensor_tensor(out=RA[R36, :], in0=RA[R36, :], in1=RC[R36, :], op=ALU.add), ["RA", "RC"], ["RA"])
            for c in range(NT):
                sl = slice(c * 128, (c + 1) * 128)
                V("dve", lambda e, sl=sl: e.tensor_tensor_scan(out=RB[0:4, sl], data0=RA[0:4, sl], data1=RA[0:4, sl], initial=NINF, op0=ALU.max, op1=ALU.max), ["RA", "RB"], ["RB"])
                rs = slice((c + 1) * 128 - 1, c * 128 - 1 if c > 0 else None, -1)
                V("dve", lambda e, rs=rs: e.tensor_tensor_scan(out=RB[32:36, rs], data0=RA[32:36, rs], data1=RA[32:36, rs], initial=NINF, op0=ALU.max, op1=ALU.max), ["RA", "RB"], ["RB"])
            RB3 = RB.rearrange("p (c t) -> p c t", c=16)
            RC3 = RC.rearrange("p (c t) -> p c t", c=16)
            RA3 = RA.rearrange("p (c t) -> p c t", c=16)
            V("dve", lambda e: e.tensor_copy(out=CML[0:4, :], in_=RB3[0:4, :, 127]), ["RB"], ["CML"])
            V("dve", lambda e: e.tensor_copy(out=CML[32:36, :], in_=RB3[32:36, :, 0]), ["RB"], ["CML"])
            V("dve", lambda e: e.tensor_scalar(out=BLR[0:4, :], in0=RC3[0:4, :, 127], scalar1=-1.0, scalar2=None, op0=ALU.mult), ["RC"], ["BLR"])
            V("dve", lambda e: e.tensor_scalar(out=BLR[32:36, :], in0=RC3[32:36, :, 0], scalar1=-1.0, scalar2=None, op0=ALU.mult), ["RC"], ["BLR"])
            V("dve", lambda e: e.tensor_tensor(out=D0[R36, :], in0=RZ[R36, 0:16], in1=BLR[R36, :], op=ALU.add), ["RZ", "BLR"], ["D0"])
            V("dve", lambda e: e.tensor_tensor(out=D1[R36, :], in0=RZ[R36, 16:32], in1=CML[R36, :], op=ALU.max), ["RZ", "CML"], ["D1"])
            V("dve", lambda e: e.tensor_tensor(out=D1[R36, :], in0=D1[R36, :], in1=BLR[R36, :], op=ALU.add), ["D1", "BLR"], ["D1"])
            V("dve", lambda e: e.tensor_tensor_scan(out=MST[0:4, :], data0=D0[0:4, :], data1=D1[0:4, :], initial=m0t[0:4, 0:1], op0=ALU.add, op1=ALU.max), ["D0", "D1", "m0t"], ["MST"])
            V("dve", lambda e: e.tensor_tensor_scan(out=MST[32:36, ::-1], data0=D0[32:36, ::-1], data1=D1[32:36, ::-1], initial=m0t[32:36, 0:1], op0=ALU.add, op1=ALU.max), ["D0", "D1", "m0t"], ["MST"])
            V("dve", lambda e: e.tensor_copy(out=MPE[0:4, 1:16], in_=MST[0:4, 0:15]), ["MST"], ["MPE"])
            V("dve", lambda e: e.tensor_copy(out=MPE[0:4, 0:1], in_=m0t[0:4, 0:1]), ["m0t", "MPE"], ["MPE"])
            V("dve", lambda e: e.tensor_copy(out=MPE[32:36, 0:15], in_=MST[32:36, 1:16]), ["MST", "MPE"], ["MPE"])
            V("dve", lambda e: e.tensor_copy(out=MPE[32:36, 15:16], in_=m0t[32:36, 0:1]), ["m0t", "MPE"], ["MPE"])
            V("dve", lambda e: e.tensor_tensor(out=MPE[R36, :], in0=MPE[R36, :], in1=RZ[R36, 0:16], op=ALU.add), ["MPE", "RZ"], ["MPE"])
            V("dve", lambda e: e.tensor_tensor(out=MPE[R36, :], in0=MPE[R36, :], in1=RZ[R36, 16:32], op=ALU.max), ["MPE", "RZ"], ["MPE"])
            V("dve", lambda e: e.tensor_tensor(out=RB3[R36, :, :], in0=RB3[R36, :, :], in1=MPE[R36, :].unsqueeze(2).to_broadcast([36, 16, 128]), op=ALU.max), ["RB", "MPE"], ["RB"])
            V("dve", lambda e: e.tensor_tensor(out=CML[R36, :], in0=CML[R36, :], in1=MPE[R36, :], op=ALU.max), ["CML", "MPE"], ["CML"])
            V("dve", lambda e: e.tensor_tensor(out=RC[R36, :], in0=RC[R36, :], in1=RB[R36, :], op=ALU.subtract), ["RC", "RB"], ["RC"])
            V("dve", lambda e: e.tensor_tensor(out=RB3[R36, :, :], in0=RB3[R36, :, :], in1=CML[R36, :].unsqueeze(2).to_broadcast([36, 16, 128]), op=ALU.subtract), ["RB", "CML"], ["RB"])
            V("dve", lambda e: e.tensor_tensor(out=RA3[R36, :, :], in0=RA3[R36, :, :], in1=CML[R36, :].unsqueeze(2).to_broadcast([36, 16, 128]), op=ALU.subtract), ["RA", "CML"], ["RA"])
            V("dve", lambda e: e.tensor_tensor(out=DLG[R36, :], in0=MPE[R36, :], in1=CML[R36, :], op=ALU.subtract), ["MPE", "CML"], ["DLG"])
            for d in range(2):
                pb_ = d * 32
                mm(ps[6][:, 0:16], bit[pb_:pb_ + 4, :], DLG[pb_:pb_ + 4, :], True, True, ["bit", "DLG"], ["ps6"])
                act(DEC[:, d * 16:(d + 1) * 16], ps[6][:, 0:16], AF.Exp, ["ps6"], ["DEC"])
            st_out(o_m[l, 0], MST[0:4, 1::2], "MST", "o_m", nc_ok=True)
            st_out(o_m[l, 1], MST[32:36, 0::2], "MST", "o_m", nc_ok=True)
            V("pool", lambda e: e.memset(Qblk, 0.0), [], ["Qblk"])
            V("pool", lambda e: e.memset(NMB[0:64, :], 0.0), [], ["NMB"])

            Qb3 = Qblk.rearrange("p (h t) -> p h t", h=4)
            NMB3 = NMB.rearrange("p (h t) -> p h t", h=4)
            SsbK = [Ssb, MX(11656, 512)]
            ecolK = [ecol, MX(12168, 8)]
            qiK = [qi, AR(1592, 128)]
            UtK = [Utmp, AR(1720, 260)]
            seq = [(0, c) for c in range(NT)] + [(1, c) for c in range(NT - 1, -1, -1)]

            def mA(d, c, k):
                pb_ = d * 32
                R4 = slice(pb_, pb_ + 4)
                mng = mngf if d == 0 else mngb
                sl = slice(c * 128, (c + 1) * 128)
                Ssb_, ecol_, qi_, Ut_ = SsbK[k], ecolK[k], qiK[k], UtK[k]
                sk, ek, qk_, uk_ = "Ssb%d" % k, "ecol%d" % k, "qi%d" % k, "Ut%d" % k
                V("pool", lambda e: e.tensor_tensor(out=Qb3, in0=qT_m[:, sl].unsqueeze(1).to_broadcast([128, 4, 128]),
                                                   in1=bmask.unsqueeze(2).to_broadcast([128, 4, 128]), op=ALU.mult), ["qTm", "cst", "Qblk"], ["Qblk"])
                mm(ps[0][:, :], kT_m[:, sl], Qblk, True, True, ["kTm", "Qblk"], ["ps0"])
                V("dve", lambda e: e.tensor_tensor(out=NMB3[R4, :, :], in0=RB[R4, sl].unsqueeze(1).to_broadcast([4, 4, 128]),
                                                   in1=nbi4[R4, :].unsqueeze(2).to_broadcast([4, 4, 128]), op=ALU.mult), ["RB", "cst", "NMB"], ["NMB"])
                mm(ps[1][:, :], RA[R4, sl], BI[R4, :], True, False, ["RA", "cst"], ["ps1"])
                mm(ps[1][:, :], ones[R4, 0:128], NMB[R4, :], False, False, ["cst", "NMB"], ["ps1"])
                mm(ps[1][:, :], ident, mng.unsqueeze(1).to_broadcast([128, 4, 128]), False, True, ["cst"], ["ps1"])
                act(Dsb, ps[1][:, :], AF.Exp, ["ps1"], ["Dsb"])
                V("dve", lambda e: e.tensor_tensor(out=Ssb_, in0=ps[0][:, :], in1=Dsb, op=ALU.mult), ["ps0", "Dsb"], [sk])
                mm(ps[3][:, 0:4], RC[R4, sl], id4[R4, :], True, True, ["RC", "cst"], ["ps3"])
                mm(ps[3][:, 4:8], RA[R4, sl], id4[R4, :], True, True, ["RA", "cst"], ["ps3"])
                act(ecol_, ps[3][:, 0:8], AF.Exp, ["ps3"], [ek])
                V("dve", lambda e: e.tensor_scalar(out=itmp[R4, :], in0=RB[R4, sl], scalar1=-1.0, scalar2=DLG[R4, c:c + 1], op0=ALU.mult, op1=ALU.add),
                  ["RB", "DLG", "itmp"], ["itmp"])
                mm(ps[3][:, 128:256], bit[R4, :], itmp[R4, :], True, True, ["bit", "itmp"], ["ps3"])
                act(iexp, ps[3][:, 128:256], AF.Exp, ["ps3"], ["iexp"])
                V("pool", lambda e: e.tensor_tensor(out=qi_, in0=qT_m[:, sl], in1=iexp, op=ALU.mult), ["qTm", "iexp"], [qk_])
                V("pool", lambda e: e.tensor_tensor(out=wkt.rearrange("p (h k) -> p h k", h=4), in0=kTM[:, c, :].rearrange("p (h k) -> p h k", h=4),
                                                   in1=ecol_[:, 4:8].unsqueeze(2).to_broadcast([128, 4, 32]), op=ALU.mult), ["kTM", ek], ["wkt"])
                mm(ps[6][:, 0:260], wkt, v_aug[:, c, :, :].rearrange("p h n -> p (h n)"), True, True, ["wkt", "vaug"], ["ps6"])
                V("dve", lambda e: e.tensor_tensor(out=Ut_.rearrange("p (h n) -> p h n", h=4), in0=ps[6][:, 0:260].rearrange("p (h n) -> p h n", h=4),
                                                   in1=bmask.unsqueeze(2).to_broadcast([128, 4, 65]), op=ALU.mult), ["ps6", "cst"], [uk_])

            def mB(d, c, k):
                Ssb_, ecol_, qi_, Ut_ = SsbK[k], ecolK[k], qiK[k], UtK[k]
                sk, ek, qk_, uk_ = "Ssb%d" % k, "ecol%d" % k, "qi%d" % k, "Ut%d" % k
                pnd, pndk = (ps[2], "ps2") if k == 0 else (ps[4], "ps4")
                mm(pnd[:, 0:260], qi_, Cn[:, d, :], True, False, [qk_, "Cn"], [pndk])
                for h in range(4):
                    mm(pnd[:, h * 65:(h + 1) * 65], Ssb_[:, h * 128:(h + 1) * 128], v_aug[:, c, h, :], False, h == 3, [sk, "vaug"], [pndk])
                V("dve", lambda e: e.scalar_tensor_tensor(out=Cn[:, d, :], in0=Cn[:, d, :], scalar=DEC[:, d * 16 + c:d * 16 + c + 1], in1=Ut_,
                                                          op0=ALU.mult, op1=ALU.add), ["Cn", "DEC", uk_], ["Cn"])
                if (d == 0 and c % 2 == 1) or (d == 1 and c % 2 == 0):
                    sq_ = (c // 2) * 2 + d
                    V("dve", lambda e: e.tensor_reduce(out=stC[:, sq_, :], in_=Cn[:, d, :].rearrange("p (h n) -> p n h", h=4), axis=AX.X, op=ALU.add), ["Cn"], ["stC"])
                V("dve", lambda e: e.tensor_scalar(out=Cn[:, d, :], in0=Cn[:, d, :], scalar1=RN[:, d * 16 + c:d * 16 + c + 1], scalar2=None, op0=ALU.mult), ["Cn", "RN"], ["Cn"])
                nd = pnd[:, 0:260].rearrange("p (h n) -> p h n", h=4)
                act(r4b, nd[:, :, 64], AF.Copy, [pndk], ["r4b"])
                V("dve", lambda e: e.scalar_tensor_tensor(out=r4, in0=r4b, scalar=-1.0, in1=r4b, op0=ALU.mult, op1=ALU.max), ["r4b"], ["r4"])
                V("dve", lambda e: e.tensor_tensor(out=r4, in0=r4, in1=ecol_[:, 0:4], op=ALU.max), ["r4", ek], ["r4"])
                V("dve", lambda e: e.reciprocal(out=r4, in_=r4), ["r4"], ["r4"])
                if d == 0:
                    V("dve", lambda e: e.tensor_tensor(out=hsum[:, c, :].rearrange("p (h v) -> p h v", h=4), in0=nd[:, :, 0:64],
                                                       in1=r4.unsqueeze(2).to_broadcast([128, 4, 64]), op=ALU.mult), [pndk, "r4"], ["hsum"])
                else:
                    V("dve", lambda e: e.tensor_tensor(out=hout.rearrange("p (h v) -> p h v", h=4), in0=nd[:, :, 0:64],
                                                       in1=r4.unsqueeze(2).to_broadcast([128, 4, 64]), op=ALU.mult), [pndk, "r4"], ["hout"])
                    V("pool", lambda e: e.tensor_tensor(out=hsum[:, c, :], in0=hsum[:, c, :], in1=hout, op=ALU.add), ["hsum", "hout"], ["hsum"])
                    return headnorm_out(hsum[:, c, :].rearrange("p (h v) -> p h v", h=4), 4, True, l, 0, 0, c, "hsum", pbi=(7 if k == 0 else 5), kk=k)
                return None

            mA(seq[0][0], seq[0][1], 0)
            pend = None
            for i, (d, c) in enumerate(seq):
                if i + 1 < len(seq):
                    mA(seq[i + 1][0], seq[i + 1][1], (i + 1) % 2)
                fin_ = mB(d, c, i % 2)
                if pend is not None:
                    pend()
                pend = fin_
            if pend is not None:
                pend()
            st_out(o_C[l].rearrange("s d p v -> p (s d) v"), stC[:, :, 0:64], "stC", "o_C")
            tr(ps[7][0:16, 0:128], stC[:, :, 64], ["stC"], ["ps7"])
            V("dve", lambda e: e.tensor_copy(out=noutT[:, :], in_=ps[7][0:16, 0:128]), ["ps7"], ["nout"])
            st_out(o_n[l], noutT[:, :], "nout", "o_n")
            BAR()

            qT_g = AR(4096, 2048)
            kT_g = AR(6144, 2048)
            v_g = AR(8192, 4096).rearrange("p (t n) -> p t n", t=16)
            SPf = AR(12288, 2048)
            SPb = AR(14336, 2048)
            Sblk = AR(0, 512).rearrange("p (d n) -> p d n", d=2)
            stS = AR(512, 16 * 64).rearrange("p (s n) -> p s n", s=16)
            gw2 = AR(1536, 256).rearrange("p (d n) -> p d n", d=2)
            flatA = mixT[:, 4:8, :].rearrange("p c t -> p (c t)")

            def MA(off, n):
                return flatA[:, off:off + n]
            osum = MA(0, 4096).rearrange("p (t n) -> p t n", t=16)
            gaT = [MA(4096, 2048), MA(6144, 2048)]
            proj_fm(l, win[:, 784:912], 128, ev_copy_bias(lambda tb: qT_g[:, tb * 512:(tb + 1) * 512], lambda tb: "qTg", scale=DKS))
            proj_fm(l, win[:, 912:1040], 128, ev_copy_bias(lambda tb: kT_g[:, tb * 512:(tb + 1) * 512], lambda tb: "kTg"))
            proj_tm(l, win[:, 1040:1296], 256, lambda tt, pap, pk: V(evq2(), lambda e: e.tensor_copy(out=v_g[:, tt, :], in_=pap), [pk], ["vg"]))
            for j in range(2):
                proj_fm(l, win[:, 1296 + j * 128:1424 + j * 128], 128, ev_func_bias(lambda tb, j=j: mixT[:, 2 + j, tb * 512:(tb + 1) * 512], lambda tb: "mix2", AF.Silu))
            for d in range(2):
                proj_fm(l, win[:, 1552 + d * 16:1568 + d * 16], 16, ev_copy_bias(lambda tb, d=d: gaT[d][0:16, tb * 512:(tb + 1) * 512], lambda tb: "gaT"))
            BAR()
            ld(gw2[0:16, :, :], d_gw2[l].rearrange("d r n -> r d n"), "gw2")
            ld(Sblk[:, :, :], d_s0[l].rearrange("d p n -> p d n"), "Sblk")
            V("dve", lambda e: e.tensor_scalar(out=ngb2[:, :], in0=GB2[:, l, :], scalar1=-1.0, scalar2=None, op0=ALU.mult), ["GB2"], ["ngb2"])
            for d in range(2):
                SP = SPf if d == 0 else SPb
                for tb in range(NB):
                    pb = ps[tb % 2]
                    pk = "ps%d" % (tb % 2)
                    mm(pb[:, :], gw2[0:16, d, :], gaT[d][0:16, tb * 512:(tb + 1) * 512], True, True, ["gw2", "gaT"], [pk])
                    act(SP[:, tb * 512:(tb + 1) * 512], pb[:, :], AF.Exp, [pk, "ngb2"], ["SP%d" % d], bias=ngb2[:, d:d + 1], scale=-1.0)
                act(SP, SP, AF.Ln, ["SP%d" % d], ["SP%d" % d], bias=1.0)
            BAR()
            CS = MA(4096, 128)
            E1 = MA(4224, 128)
            E2 = MA(4352, 128)
            E3 = MA(4480, 128)
            qe = MA(4608, 128)
            ke = MA(4736, 128)
            kd = MA(4864, 128)
            kdT = MA(4992, 128)
            Qeb = MA(5120, 512)
            Asb = MA(5632, 512)
            Ug = MA(6144, 256)
            og = MA(6400, 256)
            MXT["hn_tmp"] = [MA(6656, 256), MA(7818, 256)]
            blc = MA(6912, 2)
            V("pool", lambda e: e.memset(Qeb, 0.0), [], ["Qeb"])
            Qe3 = Qeb.rearrange("p (h t) -> p h t", h=4)
            qeK = [qe, MA(6920, 128)]
            AsbK = [Asb, MA(7048, 512)]
            UgK = [Ug, MA(7560, 256)]
            blcK = [blc, MA(7816, 2)]
            seq = [(0, c) for c in range(NT)] + [(1, c) for c in range(NT - 1, -1, -1)]

            def gA(d, c, k):
                SP = SPf if d == 0 else SPb
                m01 = m01f if d == 0 else m01b
                sl = slice(c * 128, (c + 1) * 128)
                qe_, Asb_, Ug_, blc_ = qeK[k], AsbK[k], UgK[k], blcK[k]
                qk_, ak_, uk_, bk_ = "qe%d" % k, "Asb%d" % k, "Ug%d" % k, "blc%d" % k
                if d == 0:
                    V("dve", lambda e: e.tensor_tensor_scan(out=CS, data0=ones[:, 0:128], data1=SP[:, sl], initial=0.0, op0=ALU.mult, op1=ALU.add), ["SP%d" % d, "cst", "CS"], ["CS"])
                    last = CS[:, 127:128]
                else:
                    rs = slice((c + 1) * 128 - 1, c * 128 - 1 if c > 0 else None, -1)
                    V("dve", lambda e: e.tensor_tensor_scan(out=CS[:, ::-1], data0=ones[:, 0:128], data1=SP[:, rs], initial=0.0, op0=ALU.mult, op1=ALU.add), ["SP%d" % d, "cst", "CS"], ["CS"])
                    last = CS[:, 0:1]
                V("dve", lambda e: e.tensor_scalar(out=blc_[:, 0:1], in0=last, scalar1=-1.0 / 16, scalar2=None, op0=ALU.mult), ["CS", bk_], [bk_])
                act(blc_[:, 1:2], blc_[:, 0:1], AF.Exp, [bk_], [bk_])
                act(E1, CS, AF.Exp, ["CS"], ["E1"], scale=-1.0 / 16)
                act(E2, CS, AF.Exp, ["CS"], ["E2"], scale=1.0 / 16)
                act(E3, CS, AF.Exp, ["CS", bk_], ["E3"], scale=1.0 / 16, bias=blc_[:, 0:1])
                V("dve", lambda e: e.tensor_tensor(out=qe_, in0=qT_g[:, sl], in1=E1, op=ALU.mult), ["qTg", "E1"], [qk_])
                V("dve", lambda e: e.tensor_tensor(out=ke, in0=kT_g[:, sl], in1=E2, op=ALU.mult), ["kTg", "E2"], ["ke"])
                V("pool", lambda e: e.tensor_tensor(out=kd, in0=kT_g[:, sl], in1=E3, op=ALU.mult), ["kTg", "E3"], ["kd"])
                V("pool", lambda e: e.tensor_tensor(out=Qe3, in0=qe_.unsqueeze(1).to_broadcast([128, 4, 128]), in1=bmask.unsqueeze(2).to_broadcast([128, 4, 128]), op=ALU.mult), [qk_, "cst", "Qeb"], ["Qeb"])
                mm(ps[0][:, :], ke, Qeb, True, True, ["ke", "Qeb"], ["ps0"])
                V("dve", lambda e: e.tensor_tensor(out=Asb_.rearrange("p (h t) -> p h t", h=4), in0=ps[0][:, :].rearrange("p (h t) -> p h t", h=4),
                                                   in1=m01.unsqueeze(1).to_broadcast([128, 4, 128]), op=ALU.mult), ["ps0", "cst"], [ak_])
                tr(ps[3][:, 0:128], kd, ["kd"], ["ps3"])
                act(kdT, ps[3][:, 0:128], AF.Copy, ["ps3"], ["kdT"])
                mm(ps[6][:, 0:256], kdT, v_g[:, c, :], True, True, ["kdT", "vg"], ["ps6"])
                V("dve", lambda e: e.tensor_tensor(out=Ug_.rearrange("p (h n) -> p h n", h=4), in0=ps[6][:, 0:256].rearrange("p (h n) -> p h n", h=4),
                                                   in1=bmask.unsqueeze(2).to_broadcast([128, 4, 64]), op=ALU.mult), ["ps6", "cst"], [uk_])

            def gB(d, c, k):
                qe_, Asb_, Ug_, blc_ = qeK[k], AsbK[k], UgK[k], blcK[k]
                qk_, ak_, uk_, bk_ = "qe%d" % k, "Asb%d" % k, "Ug%d" % k, "blc%d" % k
                po_, pok_ = (ps[2], "ps2") if k == 0 else (ps[4], "ps4")
                mm(po_[:, 0:256], qe_, Sblk[:, d, :], True, False, [qk_, "Sblk"], [pok_])
                for h in range(4):
                    mm(po_[:, h * 64:(h + 1) * 64], Asb_[:, h * 128:(h + 1) * 128], v_g[:, c, h * 64:(h + 1) * 64], False, h == 3, [ak_, "vg"], [pok_])
                V("dve", lambda e: e.scalar_tensor_tensor(out=Sblk[:, d, :], in0=Sblk[:, d, :], scalar=blc_[:, 1:2], in1=Ug_, op0=ALU.mult, op1=ALU.add), ["Sblk", bk_, uk_], ["Sblk"])
                if (d == 0 and c % 2 == 1) or (d == 1 and c % 2 == 0):
                    sq_ = (c // 2) * 2 + d
                    V("dve", lambda e: e.tensor_reduce(out=stS[:, sq_, :], in_=Sblk[:, d, :].rearrange("p (h n) -> p n h", h=4), axis=AX.X, op=ALU.add), ["Sblk"], ["stS"])
                V("dve", lambda e: e.tensor_scalar(out=Sblk[:, d, :], in0=Sblk[:, d, :], scalar1=RN[:, d * 16 + c:d * 16 + c + 1], scalar2=None, op0=ALU.mult), ["Sblk", "RN"], ["Sblk"])
                if d == 0:
                    act(osum[:, c, :], po_[:, 0:256], AF.Copy, [pok_], ["osum"])
                else:
                    V("dve", lambda e: e.tensor_tensor(out=osum[:, c, :], in0=po_[:, 0:256], in1=osum[:, c, :], op=ALU.add), [pok_, "osum"], ["osum"])
                    return headnorm_out(osum[:, c, :].rearrange("p (h v) -> p h v", h=4), 4, False, l, 2, 2, c, "osum", pbi=(7 if k == 0 else 5), kk=k)
                return None

            gA(seq[0][0], seq[0][1], 0)
            pend = None
            for i, (d, c) in enumerate(seq):
                if i + 1 < len(seq):
                    gA(seq[i + 1][0], seq[i + 1][1], (i + 1) % 2)
                fin_ = gB(d, c, i % 2)
                if pend is not None:
                    pend()
                pend = fin_
            if pend is not None:
                pend()
            st_out(o_S[l].rearrange("s d p v -> p (s d) v"), stS[:, :, :], "stS", "o_S")
            BAR()

            ckvT = AR(4096, NKEY)
            if LOWP:
                Kaug = AR(6400, NKEY).bitcast(BF16)[:, 0:NKEY]
                Vh = AR(8704, NKEY).bitcast(BF16)[:, 0:NKEY].rearrange("p (t n) -> p t n", t=NKT)
            else:
                Kaug = AR(6400, NKEY)
                Vh = AR(8704, NKEY).rearrange("p (t n) -> p t n", t=NKT)
            cqT = AR(11008, 4096).rearrange("p (c t) -> p c t", c=2)
            Qaug = AR(15104, 512)
            PT = [AR(15616, 512), AR(16128, 512)]
            if LOWP:
                Qaug = Qaug.bitcast(BF16)[:, 0:512]
                PT = [p_.bitcast(BF16)[:, 0:512] for p_ in PT]
            rt = [AR(16640, 512), AR(17152, 512)]
            wukv = AR(0, 1024)
            rcp = AR(1024, 512)
            ctxs = AR(1536, 256).rearrange("p (t n) -> p t n", t=2)
            sqt = AR(1792, 512)
            rst = AR(2304, 512)
            wq = AR(2816, 256).rearrange("p (c n) -> p c n", c=2)
            wqr = AR(3072, 256).rearrange("p (c n) -> p c n", c=2)
            ostg = AR(3328, 256)

            for j in range(2):
                proj_fm(l, win[:, 1584 + j * 128:1712 + j * 128], 128, ev_copy_bias(lambda tb, j=j: cqT[:, j, tb * 512:(tb + 1) * 512], lambda tb: "cqT"))
            proj_fm(l, win[:, 1840:1968], 128, ev_copy_bias(lambda tb: ckvT[:, 256 + tb * 512:256 + (tb + 1) * 512], lambda tb: "ckvT"))
            krraw = MA(0, 2048)
            krrot = MA(2048, 2048)
            proj_fm(l, d_wkr[l], 128, ev_copy_bias(lambda tb: krraw[:, tb * 512:(tb + 1) * 512], lambda tb: "krraw"))
            proj_fm(l, d_wkrr[l], 128, ev_copy_bias(lambda tb: krrot[:, tb * 512:(tb + 1) * 512], lambda tb: "krrot"))
            BAR()
            ld(wukv, d_wukv[l], "wukv")
            ld(Kaug[112:128, :], d_kaug[:, :], "Kaug")
            V("pool", lambda e: e.memset(Vh[:, :, 64:128], 1.0), [], ["Vh"])
            V("pool", lambda e: e.memset(Kaug[64:112, :], 0.0), ["Kaug"], ["Kaug"])
            V("pool", lambda e: e.memset(Qaug[64:112, :], 0.0), ["Qaug"], ["Qaug"])
            for tb in range(NB):
                sl = slice(tb * 512, (tb + 1) * 512)
                for j in range(2):
                    act(sqt, cqT[:, j, sl], AF.Square, ["cqT"], ["sqt"])
                    mm(ps[4][:, :], ones[:, 0:128], sqt, j == 0, j == 1, ["cst", "sqt"], ["ps4"])
                act(rst, ps[4][:, :], AF.Sqrt, ["ps4", "EPS"], ["rst"], bias=EPS6[:, 0:1], scale=1.0 / 256)
                V("dve", lambda e: e.reciprocal(out=rst, in_=rst), ["rst"], ["rst"])
                for j in range(2):
                    V("dve", lambda e, j=j, sl=sl: e.scalar_tensor_tensor(out=cqT[:, j, sl], in0=cqT[:, j, sl], scalar=NG[:, l, 4 + j:5 + j], in1=rst, op0=ALU.mult, op1=ALU.mult), ["cqT", "NG", "rst"], ["cqT"])
                ksl = slice(256 + tb * 512, 256 + (tb + 1) * 512)
                act(sqt, ckvT[:, ksl], AF.Square, ["ckvT"], ["sqt"])
                mm(ps[5][:, :], ones[:, 0:128], sqt, True, True, ["cst", "sqt"], ["ps5"])
                act(rst, ps[5][:, :], AF.Sqrt, ["ps5", "EPS"], ["rst"], bias=EPS6[:, 0:1], scale=1.0 / 128)
                V("dve", lambda e: e.reciprocal(out=rst, in_=rst), ["rst"], ["rst"])
                V("dve", lambda e, ksl=ksl: e.scalar_tensor_tensor(out=ckvT[:, ksl], in0=ckvT[:, ksl], scalar=NG[:, l, 6:7], in1=rst, op0=ALU.mult, op1=ALU.mult), ["ckvT", "NG", "rst"], ["ckvT"])
            ostg4 = MA(4096, 1024).rearrange("p (b n) -> p b n", b=4)
            for tt in range(NT):
                pbo, pko = ps[6 + tt % 2], "ps%d" % (6 + tt % 2)
                ob, ok_ = ostg4[:, tt % 4, :], "ostg%d" % (tt % 4)
                tr(pbo[:, 0:128], ckvT[:, 256 + tt * 128:256 + (tt + 1) * 128], ["ckvT"], [pko])
                tr(pbo[:, 128:256], krraw[:, tt * 128:(tt + 1) * 128], ["krraw"], [pko])
                if tt % 2 == 0:
                    act(ob, pbo[:, 0:256], AF.Copy, [pko], [ok_])
                else:
                    V("dve", lambda e, ob=ob, pbo=pbo: e.tensor_copy(out=ob, in_=pbo[:, 0:256]), [pko], [ok_])
                st_out(o_ckv[l, tt * 128:(tt + 1) * 128, :], ob[:, 0:128], ok_, "o_ckv%d" % (tt % 4))
                st_out(o_kr[l, tt * 128:(tt + 1) * 128, :], ob[:, 128:256], ok_, "o_kr%d" % (tt % 4))
            ld(ctxs[:, :, :], d_ckvc[l].rearrange("(t p) n -> p t n", p=128), "ctxs")
            for t2 in range(2):
                tr(ps[7][:, t2 * 128:(t2 + 1) * 128], ctxs[:, t2, :], ["ctxs"], ["ps7"])
            act(ckvT[:, 0:256], ps[7][:, 0:256], AF.Copy, ["ps7"], ["ckvT"])
            ld(ctxs[:, :, :], d_krc[l].rearrange("(t p) n -> p t n", p=128), "ctxs")
            for t2 in range(2):
                tr(ps[7][:, t2 * 128:(t2 + 1) * 128], ctxs[:, t2, :], ["ctxs"], ["ps7"])
            act(Kaug[64:112, 0:256], ps[7][64:112, 0:256], AF.Copy, ["ps7", "Kaug"], ["Kaug"])

            def rope(dst, raw, rot, rawk, rotk, dstk, qb):
                for (p0, tcos, tsin, mode) in ((64, 0, 32, "r"), (96, 64, 128, "c")):
                    pr = slice(p0, p0 + 16)
                    if mode == "r":
                        cosb = TAB[pr, tcos + qb * 8:tcos + qb * 8 + 8].unsqueeze(2).to_broadcast([16, 8, 64])
                        sinb = TAB[pr, tsin + qb * 8:tsin + qb * 8 + 8].unsqueeze(2).to_broadcast([16, 8, 64])
                    else:
                        cosb = TAB[pr, tcos:tcos + 64].unsqueeze(1).to_broadcast([16, 8, 64])
                        sinb = TAB[pr, tsin:tsin + 64].unsqueeze(1).to_broadcast([16, 8, 64])
                    r3 = lambda ap: ap.rearrange("p (r c) -> p r c", r=8)
                    V("dve", lambda e, pr=pr, cosb=cosb: e.tensor_tensor(out=r3(rt[0][pr, :]), in0=r3(raw[pr, :]), in1=cosb, op=ALU.mult), [rawk, "TAB", "rt0"], ["rt0"])
                    V("dve", lambda e, pr=pr, sinb=sinb: e.tensor_tensor(out=r3(rt[1][pr, :]), in0=r3(rot[pr, :]), in1=sinb, op=ALU.mult), [rotk, "TAB", "rt1"], ["rt1"])
                    V("dve", lambda e, pr=pr: e.tensor_tensor(out=dst[pr, :], in0=rt[0][pr, :], in1=rt[1][pr, :], op=ALU.add), ["rt0", "rt1", dstk], [dstk])

            for tb in range(NB):
                rope(Kaug[:, 256 + tb * 512:256 + (tb + 1) * 512], krraw[:, tb * 512:(tb + 1) * 512], krrot[:, tb * 512:(tb + 1) * 512], "krraw", "krrot", "Kaug", tb)
            BAR()

            KB = [(0, 512), (512, 512), (1024, 512), (1536, 512), (2048, 256)]
            if LOWP:
                ckvM = AR(7552, 1152).bitcast(BF16)[:, 0:NKEY]
                V("dve", lambda e: e.tensor_copy(out=ckvM, in_=ckvT), ["ckvT"], ["ckvM"])
                wukvM = AR(9856, 512).bitcast(BF16)[:, 0:1024]
                act(wukvM, wukv, AF.Copy, ["wukv"], ["wukvM"])
                tA = rt[0].bitcast(BF16)
                tB = rt[1].bitcast(BF16)
                cqf = cqT.rearrange("p c t -> p (c t)")
                cqMf = cqf.bitcast(BF16)
                for j in range(2):
                    for hf in range(2):
                        V("dve", lambda e, j=j, hf=hf: e.tensor_copy(out=(tA if hf == 0 else tB), in_=cqT[:, j, hf * 1024:(hf + 1) * 1024]), ["cqT", "rt%d" % hf], ["rt%d" % hf])
                    if j == 0:
                        cq0a = AR(16640 - 2048, 0) if False else None
                        stash = [AR(15616, 512).bitcast(BF16), AR(16128, 512).bitcast(BF16)]
                        V("dve", lambda e: e.tensor_copy(out=stash[0], in_=tA), ["rt0", "PT0"], ["PT0"])
                        act(stash[1], tB, AF.Copy, ["rt1", "PT1"], ["PT1"])
                V("dve", lambda e: e.tensor_copy(out=cqMf[:, 0:1024], in_=stash[0]), ["PT0", "cqT"], ["cqT"])
                act(cqMf[:, 1024:2048], stash[1], AF.Copy, ["PT1", "cqT"], ["cqT"])
                V("dve", lambda e: e.tensor_copy(out=cqMf[:, 2048:3072], in_=tA), ["rt0", "cqT"], ["cqT"])
                act(cqMf[:, 3072:4096], tB, AF.Copy, ["rt1", "cqT"], ["cqT"])
                cqM = cqMf[:, 0:4096].rearrange("p (c t) -> p c t", c=2)
                wqM = AR(10368, 128).bitcast(BF16).rearrange("p (c n) -> p c n", c=2)
                wqrM = AR(10496, 128).bitcast(BF16).rearrange("p (c n) -> p c n", c=2)
                BAR()
            else:
                ckvM, wukvM, cqM = ckvT, wukv, cqT
            for h in range(8):
                for bi_, (k0, kn) in enumerate(KB):
                    pb = ps[4 + bi_ % 2]
                    pk = "ps%d" % (4 + bi_ % 2)
                    mm(pb[0:64, 0:kn], wukvM[:, h * 128:h * 128 + 64], ckvM[:, k0:k0 + kn], True, True, ["wukvM", "ckvM"], [pk])
                    if bi_ % 2 == 0:
                        act(Kaug[0:64, k0:k0 + kn], pb[0:64, 0:kn], AF.Copy, [pk, "Kaug"], ["Kaug"])
                    else:
                        V("dve", lambda e, k0=k0, kn=kn, pb=pb: e.tensor_copy(out=Kaug[0:64, k0:k0 + kn], in_=pb[0:64, 0:kn]), [pk, "Kaug"], ["Kaug"])
                for g8 in range(3):
                    nt_ = 8 if g8 < 2 else 2
                    pb = ps[6 + g8 % 2]
                    pk = "ps%d" % (6 + g8 % 2)
                    for i in range(nt_):
                        kt = g8 * 8 + i
                        mm(pb[:, i * 64:(i + 1) * 64], ckvM[:, kt * 128:(kt + 1) * 128], wukvM[:, h * 128 + 64:h * 128 + 128], True, True, ["ckvM", "wukvM"], [pk])
                    V("dve", lambda e, g8=g8, nt_=nt_, pb=pb: e.tensor_copy(out=Vh[:, g8 * 8:g8 * 8 + nt_, 0:64], in_=pb[:, 0:nt_ * 64].rearrange("p (t n) -> p t n", t=nt_)), [pk, "Vh"], ["Vh"])
                ld(wq[:, :, :], d_wuq[l, h].rearrange("(c p) n -> p c n", p=128), "wq")
                ld(wqr[:, :, :], d_wuqr[l, h].rearrange("(c p) n -> p c n", p=128), "wqr")
                if LOWP:
                    act(wqM, wq, AF.Copy, ["wq"], ["wqM"])
                    V("dve", lambda e: e.tensor_copy(out=wqrM, in_=wqr), ["wqr"], ["wqrM"])
                    wq_, wqr_, wqk, wqrk = wqM, wqrM, "wqM", "wqrM"
                else:
                    wq_, wqr_, wqk, wqrk = wq, wqr, "wq", "wqr"
                QA = [Qaug, AR(3584, 512).bitcast(BF16)[:, 0:512] if LOWP else AR(3584, 512)]

                def qbuild(qb, Qa, qk):
                    sl = slice(qb * 512, (qb + 1) * 512)
                    for j in range(2):
                        mm(ps[4][:, :], wq_[:, j, :], cqM[:, j, sl], j == 0, j == 1, [wqk, "cqT"], ["ps4"])
                    for j in range(2):
                        mm(ps[5][:, :], wqr_[:, j, :], cqM[:, j, sl], j == 0, j == 1, [wqrk, "cqT"], ["ps5"])
                    ld(Qa[112:128, :], d_qaug[:, sl], qk)
                    act(Qa[0:64, :], ps[4][0:64, :], AF.Copy, ["ps4", qk], [qk])
                    rope(Qa, ps[4], ps[5], "ps4", "ps5", qk, qb)

                if h == 0:
                    V("pool", lambda e: e.memset(QA[1][64:112, :], 0.0), [], ["Qaug1"])
                qbuild(0, QA[0], "Qaug")
                for qb in range(NB):
                    sl = slice(qb * 512, (qb + 1) * 512)
                    Qa, qk = QA[qb % 2], ("Qaug" if qb % 2 == 0 else "Qaug1")
                    po = ps[2 + (h * 4 + qb) % 2]
                    pok = "ps%d" % (2 + (h * 4 + qb) % 2)

                    def mm1(kt):
                        mm(ps[kt % 2][:, :], Kaug[:, kt * 128:(kt + 1) * 128], Qa, True, True, ["Kaug", qk], ["ps%d" % (kt % 2)])

                    mm1(0)
                    for kt in range(NKT):
                        if kt == 3 and qb + 1 < NB:
                            qbuild(qb + 1, QA[(qb + 1) % 2], "Qaug" if (qb + 1) % 2 == 0 else "Qaug1")
                        if kt + 1 < NKT:
                            mm1(kt + 1)
                        ptile = PT[kt % 2]
                        act(ptile, ps[kt % 2][:, :], AF.Exp, ["ps%d" % (kt % 2)], ["PT%d" % (kt % 2)], scale=A_SCALE)
                        mm(po[:, :], Vh[:, kt, :], ptile, kt == 0, kt == NKT - 1, ["Vh", "PT%d" % (kt % 2)], [pok])
                    V("dve", lambda e, po=po: e.reciprocal(out=rcp[64:128, :], in_=po[64:128, :]), [pok], ["rcp"])
                    mc = 4 + h // 2
                    p0 = (h % 2) * 64
                    V("dve", lambda e, po=po, mc=mc, p0=p0, sl=sl: e.tensor_tensor(out=mixT[p0:p0 + 64, mc, sl], in0=po[0:64, :], in1=rcp[64:128, :], op=ALU.mult), [pok, "rcp"], ["mixA"])
            BAR()

            LNT1 = (AR(4096, 512), [AR(4608, 512), AR(5120, 512)], AR(5632, 512))
            LNK1 = ("lnmean", ["lnsq0", "lnsq1"], "lnrstd")
            for tb in range(NB):
                sl = slice(tb * 512, (tb + 1) * 512)
                for c in range(KC):
                    if c % 2 == 0:
                        act(xT[:, c, sl], xT[:, c, sl], AF.Copy, [XK[tb]], [XK[tb]], scale=ALPHA)
                    else:
                        V("dve", lambda e, c=c, sl=sl: e.tensor_scalar(out=xT[:, c, sl], in0=xT[:, c, sl], scalar1=ALPHA, scalar2=None, op0=ALU.mult), [XK[tb]], [XK[tb]])
            for oc in range(KC):
                W_, wk = wslot()
                Wv = W_[:, 0:1024].rearrange("p (c n) -> p c n", c=8)
                ld(Wv, d_wout[l].rearrange("(c p) n -> p c n", p=128)[:, :, oc * 128:(oc + 1) * 128], wk)
                for tb in range(NB):
                    sl = slice(tb * 512, (tb + 1) * 512)
                    pb = ps[tb % 2]
                    pk = "ps%d" % (tb % 2)
                    for c in range(KC):
                        mm(pb[:, :], Wv[:, c, :], mixT[:, c, sl], c == 0, c == KC - 1, [wk, "mixall"], [pk])
                    V("dve", lambda e, oc=oc, sl=sl, pb=pb: e.scalar_tensor_tensor(out=xT[:, oc, sl], in0=pb[:, :], scalar=MOD[:, l, 16 + oc:17 + oc], in1=xT[:, oc, sl],
                                                                                   op0=ALU.mult, op1=ALU.add), [pk, "MOD", XK[tb]], [XK[tb]])
            for tb in range(NB):
                layernorm_block(l, tb, 0, 8, LNT1, LNK1)
            BAR()

            mflat = mixT[:, :, :].rearrange("p c t -> p (c t)")
            if LOWP:
                mfb = mflat.bitcast(BF16)
                hid = mfb[:, 0:NJ * 512].rearrange("p (j t) -> p j t", j=NJ)
                hh = mfb[:, NJ * 512:NJ * 512 + 4096].rearrange("p (c t) -> p c t", c=8)
            else:
                hid = mflat[:, 0:NJ * 512].rearrange("p (j t) -> p j t", j=NJ)
                hh = mflat[:, NJ * 512:NJ * 512 + 4096].rearrange("p (c t) -> p c t", c=8)
            WG = [AR(0, 2048), AR(2048, 2048)]
            WU = [AR(4096, 2048), AR(6144, 2048)]
            WD = [AR(8192 + i * 1024, 1024) for i in range(4)]
            gtmp = [AR(12288, 512), AR(12800, 512)]
            LNT2 = (AR(13312, 512), [AR(13824, 512), AR(14336, 512)], AR(14848, 512))
            LNK2 = ("lnmean", ["lnsq0", "lnsq1"], "lnrstd")
            if LOWP:
                WGB = [AR(15360, 1024).bitcast(BF16), AR(16384, 1024).bitcast(BF16)]
                WUB = [mflat[:, 8192:9216].bitcast(BF16), mflat[:, 9216:10240].bitcast(BF16)]
                WDB = [mflat[:, 10240:10752].bitcast(BF16), mflat[:, 10752:11264].bitcast(BF16)]
            wdi = 0
            for tb in range(NB):
                sl = slice(tb * 512, (tb + 1) * 512)
                xk = XK[tb]
                for c in range(KC):
                    V("dve", lambda e, c=c, sl=sl: e.tensor_scalar(out=hh[:, c, :], in0=xT[:, c, sl], scalar1=OP1[:, l, 8 + c:9 + c], scalar2=MOD[:, l, 24 + c:25 + c],
                                                                   op0=ALU.mult, op1=ALU.add), [xk, "OP1", "MOD"], ["hh"])
                    act(xT[:, c, sl], xT[:, c, sl], AF.Copy, [xk], [xk], scale=ALPHA)
                for jp in range(NJ // 2):
                    wg_ = WG[jp % 2].rearrange("p (c n) -> p c n", c=8)
                    wu_ = WU[jp % 2].rearrange("p (c n) -> p c n", c=8)
                    gk, uk = "wg%d" % (jp % 2), "wu%d" % (jp % 2)
                    ld(wg_, d_wg[l].rearrange("(c p) n -> p c n", p=128)[:, :, jp * 256:(jp + 1) * 256], gk)
                    ld(wu_, d_wu[l].rearrange("(c p) n -> p c n", p=128)[:, :, jp * 256:(jp + 1) * 256], uk)
                    if LOWP:
                        wgb = WGB[jp % 2].rearrange("p (c n) -> p c n", c=8)
                        wub = WUB[jp % 2].rearrange("p (c n) -> p c n", c=8)
                        gbk, ubk = "wgb%d" % (jp % 2), "wub%d" % (jp % 2)
                        act(wgb, wg_, AF.Copy, [gk], [gbk])
                        V("dve", lambda e, wub=wub, wu_=wu_: e.tensor_copy(out=wub, in_=wu_), [uk], [ubk])
                        wgm, wum, gmk, umk = wgb, wub, gbk, ubk
                    else:
                        wgm, wum, gmk, umk = wg_, wu_, gk, uk
                    for jj in range(2):
                        j = jp * 2 + jj
                        pg, pgk = ps[j % 2], "ps%d" % (j % 2)
                        pu, puk = ps[2 + j % 2], "ps%d" % (2 + j % 2)
                        for c in range(KC):
                            mm(pg[:, :], wgm[:, c, jj * 128:(jj + 1) * 128], hh[:, c, :], c == 0, c == KC - 1, [gmk, "hh"], [pgk])
                        for c in range(KC):
                            mm(pu[:, :], wum[:, c, jj * 128:(jj + 1) * 128], hh[:, c, :], c == 0, c == KC - 1, [umk, "hh"], [puk])
                        gt = gtmp[j % 2]
                        act(gt, pg[:, :], AF.Silu, [pgk], ["gt%d" % (j % 2)])
                        V("dve", lambda e, j=j, pu=pu, gt=gt: e.tensor_tensor(out=hid[:, j, :], in0=pu[:, :], in1=gt, op=ALU.mult), [puk, "gt%d" % (j % 2)], ["hid"])
                for op_ in range(KC // 2):
                    for j4 in range(0, NJ, 4):
                        nj = min(4, NJ - j4)
                        wd_ = WD[wdi % 4][:, 0:nj * 256].rearrange("p (c n) -> p c n", c=nj)
                        dk_ = "wd%d" % (wdi % 4)
                        ld(wd_, d_wd[l][j4 * 128:(j4 + nj) * 128, op_ * 256:(op_ + 1) * 256].rearrange("(c p) n -> p c n", p=128), dk_)
                        if LOWP:
                            wdb = WDB[wdi % 2][:, 0:nj * 256].rearrange("p (c n) -> p c n", c=nj)
                            dbk = "wdb%d" % (wdi % 2)
                            if wdi % 2 == 0:
                                act(wdb, wd_, AF.Copy, [dk_], [dbk])
                            else:
                                V("dve", lambda e, wdb=wdb, wd_=wd_: e.tensor_copy(out=wdb, in_=wd_), [dk_], [dbk])
                            wdm, dmk = wdb, dbk
                        else:
                            wdm, dmk = wd_, dk_
                        wdi += 1
                        for jj in range(nj):
                            j = j4 + jj
                            for o2 in range(2):
                                mm(ps[4 + o2][:, :], wdm[:, jj, o2 * 128:(o2 + 1) * 128], hid[:, j, :], j == 0, j == NJ - 1, [dmk, "hid"], ["ps%d" % (4 + o2)])
                    for o2 in range(2):
                        oc = op_ * 2 + o2
                        V("dve", lambda e, oc=oc, sl=sl, o2=o2: e.scalar_tensor_tensor(out=xT[:, oc, sl], in0=ps[4 + o2][:, :], scalar=MOD[:, l, 40 + oc:41 + oc], in1=xT[:, oc, sl],
                                                                                       op0=ALU.mult, op1=ALU.add), ["ps%d" % (4 + o2), "MOD", xk], [xk])
                layernorm_block(l, tb, 16, 24, LNT2, LNK2)
            BAR()

        ystg = [AR(0, 1024), AR(1024, 1024)]
        for tt in range(NT):
            ys = ystg[tt % 2]
            yk = "ystg%d" % (tt % 2)
            for half in range(2):
                pb = ps[(tt * 2 + half) % 4]
                pk = "ps%d" % ((tt * 2 + half) % 4)
                for q in range(4):
                    c = half * 4 + q
                    tr(pb[:, q * 128:(q + 1) * 128], xT[:, c, tt * 128:(tt + 1) * 128], [XK[tt // 4]], [pk])
                if half == 0:
                    act(ys[:, 0:512], pb[:, :], AF.Copy, [pk, yk], [yk])
                else:
                    V("dve", lambda e, ys=ys, pb=pb: e.tensor_copy(out=ys[:, 512:1024], in_=pb[:, :]), [pk, yk], [yk])
            st_out(o_y[tt * 128:(tt + 1) * 128, :], ys, yk, "o_y%d" % (tt % 2))
        P.emit()
    return nc


_rr2 = [0]


def evq2():
    _rr2[0] ^= 1
    return "act" if False else "dve"


def _consts():
    cst = np.zeros((128, 1292), np.float32)
    cst[:, 0:128] = np.eye(128)
    cst[:, 128:256] = 1.0
    s = np.arange(128)[:, None]
    t = np.arange(128)[None, :]
    cst[:, 256:384] = (s <= t)
    cst[:, 384:512] = (s >= t)
    cst[:, 512:640] = np.where(s <= t, 0.0, NEG)
    cst[:, 640:768] = np.where(s >= t, 0.0, NEG)
    for h in range(4):
        cst[h * 32:(h + 1) * 32, 768 + h] = 1.0
    for base in (0, 32):
        for r in range(4):
            cst[base + r, 772 + r * 128:772 + (r + 1) * 128] = 1.0
            cst[base + r, 1284 + r] = -1.0
            cst[base + r, 1288 + r] = 1.0
    bit = np.zeros((64, 128), np.float32)
    for base in (0, 32):
        for r in range(4):
            bit[base + r, r * 32:(r + 1) * 32] = 1.0
    return cst, bit


def _rope_tab(is_sample):
    tab = np.zeros((128, 192), np.float32)
    half = 8
    inv = (10000.0 ** (-np.arange(half, dtype=np.float32) / half)).astype(np.float32)
    if is_sample:
        rows = np.arange(32, dtype=np.float32)
        cols = np.arange(64, dtype=np.float32)
        angr = (rows[None, :] * inv[:, None]).astype(np.float32)
        angc = (cols[None, :] * inv[:, None]).astype(np.float32)
        cr, sr = np.cos(angr).astype(np.float32), np.sin(angr).astype(np.float32)
        cc, sc = np.cos(angc).astype(np.float32), np.sin(angc).astype(np.float32)
    else:
        cr, sr = np.ones((8, 32), np.float32), np.zeros((8, 32), np.float32)
        cc, sc = np.ones((8, 64), np.float32), np.zeros((8, 64), np.float32)
    tab[64:72, 0:32] = cr
    tab[72:80, 0:32] = cr
    tab[64:72, 32:64] = -sr
    tab[72:80, 32:64] = sr
    tab[96:104, 64:128] = cc
    tab[104:112, 64:128] = cc
    tab[96:104, 128:192] = -sc
    tab[104:112, 128:192] = sc
    return tab


def _pad_rope_cols(w32, rot):
    out = np.zeros(w32.shape[:-1] + (128,), np.float32)
    a = w32[..., 0:16]
    b = w32[..., 16:32]
    if rot:
        a = np.concatenate([a[..., 8:16], a[..., 0:8]], -1)
        b = np.concatenate([b[..., 8:16], b[..., 0:8]], -1)
    out[..., 64:80] = a
    out[..., 96:112] = b
    return out


_NC_CACHE = {}


def kernel(x_prompt, x_sample, state_mlstm_C, state_mlstm_n, state_mlstm_m, state_gla_S,
           cache_mla_ckv, cache_mla_krope, c, c_ctx, w_ada, b_ada, w_in, m_gate_b, m_norm_g,
           g_w2, g_b2, g_norm_g, a_q_norm_g, a_kv_norm_g, a_w_uq, a_w_ukv, w_out, ln1_g, ln1_b,
           w_ffn_gate, w_ffn_up, w_ffn_down, ln2_g, ln2_b, _L=None):
    f = lambda a: np.ascontiguousarray(np.asarray(a, dtype=np.float32))
    L = int(_L) if _L is not None else int(np.asarray(w_ada).shape[0])
    (x_prompt, x_sample, state_mlstm_C, state_mlstm_n, state_mlstm_m, state_gla_S, cache_mla_ckv, cache_mla_krope,
     c, c_ctx, w_ada, b_ada, w_in, m_gate_b, m_norm_g, g_w2, g_b2, g_norm_g, a_q_norm_g, a_kv_norm_g, a_w_uq, a_w_ukv,
     w_out, ln1_g, ln1_b, w_ffn_gate, w_ffn_up, w_ffn_down, ln2_g, ln2_b) = [f(a) for a in (
        x_prompt, x_sample, state_mlstm_C, state_mlstm_n, state_mlstm_m, state_gla_S, cache_mla_ckv, cache_mla_krope,
        c, c_ctx, w_ada, b_ada, w_in, m_gate_b, m_norm_g, g_w2, g_b2, g_norm_g, a_q_norm_g, a_kv_norm_g, a_w_uq, a_w_ukv,
        w_out, ln1_g, ln1_b, w_ffn_gate, w_ffn_up, w_ffn_down, ln2_g, ln2_b)]
    if L not in _NC_CACHE:
        _NC_CACHE[L] = build(L)
    nc = _NC_CACHE[L]

    colT = lambda v: np.ascontiguousarray(v.reshape(-1, 128).T)
    shared = {}
    shared["w_ada"] = w_ada[:L]
    shared["b_adaT"] = np.stack([colT(b_ada[l]) for l in range(L)])
    shared["w_in"] = w_in[:L]
    wgi = np.zeros((L, 1024, 64), np.float32)
    wgf = np.zeros((L, 1024, 64), np.float32)
    gb = np.zeros((L, 64, 2), np.float32)
    wgi[:, :, 0:4] = w_in[:L, :, 768:772]
    wgi[:, :, 32:36] = w_in[:L, :, 776:780]
    wgf[:, :, 0:4] = w_in[:L, :, 772:776]
    wgf[:, :, 32:36] = w_in[:L, :, 780:784]
    gb[:, 0:4, 0] = m_gate_b[:L, 0:4]
    gb[:, 32:36, 0] = m_gate_b[:L, 8:12]
    gb[:, 0:4, 1] = m_gate_b[:L, 4:8]
    gb[:, 32:36, 1] = m_gate_b[:L, 12:16]
    shared["w_gi"], shared["w_gf"], shared["gb"] = wgi, wgf, gb
    shared["w_kr"] = _pad_rope_cols(w_in[:L, :, 1968:2000], False)
    shared["w_krr"] = _pad_rope_cols(w_in[:L, :, 1968:2000], True)
    shared["g_w2"] = g_w2[:L]
    shared["g_b2T"] = np.ascontiguousarray(np.transpose(g_b2[:L], (0, 2, 1)))
    shared["normg"] = np.stack([np.concatenate([colT(m_norm_g[l]), colT(g_norm_g[l]), colT(a_q_norm_g[l]), colT(a_kv_norm_g[l])], 1) for l in range(L)])
    shared["lnp"] = np.stack([np.concatenate([colT(ln1_g[l]), colT(ln1_b[l]), colT(ln2_g[l]), colT(ln2_b[l])], 1) for l in range(L)])
    uq = a_w_uq[:L].reshape(L, 256, 8, 96)
    wuqp = np.zeros((L, 8, 256, 128), np.float32)
    wuqr = np.zeros((L, 8, 256, 128), np.float32)
    for h in range(8):
        wuqp[:, h, :, 0:64] = uq[:, :, h, 0:64]
        wuqr[:, h, :, 0:64] = uq[:, :, h, 0:64]
        wuqp[:, h] += _pad_rope_cols(uq[:, :, h, 64:96], False)
        wuqr[:, h] += _pad_rope_cols(uq[:, :, h, 64:96], True)
    shared["w_uqp"], shared["w_uqr"] = wuqp, wuqr
    shared["w_ukv"] = a_w_ukv[:L]
    shared["w_out"] = w_out[:L]
    shared["w_fg"], shared["w_fu"], shared["w_fd"] = w_ffn_gate[:L], w_ffn_up[:L], w_ffn_down[:L]
    cst, bit = _consts()
    shared["cst"], shared["bit"] = cst, bit

    in_maps = []
    for core in range(8):
        m = dict(shared)
        sample = core >= 4
        b = core - 4
        if sample:
            m["x"] = x_sample[b]
            m["condT"] = colT(c[b])
        else:
            m["x"] = x_prompt[core * 8:(core + 1) * 8].reshape(T, 1024)
            m["condT"] = colT(c_ctx)
        cn0 = np.zeros((L, 2, 128, 4, 65), np.float32)
        s0 = np.zeros((L, 2, 128, 4, 64), np.float32)
        m0 = np.zeros((L, 64, 1), np.float32)
        ckvc = np.zeros((L, 256, 128), np.float32)
        krc = np.zeros((L, 256, 128), np.float32)
        if sample:
            for h in range(4):
                cn0[:, :, h * 32:(h + 1) * 32, h, 0:64] = state_mlstm_C[b, :L, :, h]
                cn0[:, :, h * 32:(h + 1) * 32, h, 64] = state_mlstm_n[b, :L, :, h]
                s0[:, :, h * 32:(h + 1) * 32, h, :] = state_gla_S[b, :L, :, h]
            m0[:, 0:4, 0] = state_mlstm_m[b, :L, 0]
            m0[:, 32:36, 0] = state_mlstm_m[b, :L, 1]
            ckvc = cache_mla_ckv[b, :L]
            krc = _pad_rope_cols(cache_mla_krope[b, :L], False)
        m["cn0"] = cn0.reshape(L, 2, 128, 260)
        m["s0"] = s0.reshape(L, 2, 128, 256)
        m["m0"] = m0
        m["ckv_ctx"] = np.ascontiguousarray(ckvc)
        m["kr_ctx"] = krc
        rz = np.zeros((64, 32), np.float32)
        rn = np.ones((128, 32), np.float32)
        rz[:, 16:32] = NINF
        if not sample:
            for cch in range(16):
                if cch % 2 == 0:
                    rz[0:4, cch] = NINF
                    rz[0:4, 16 + cch] = 0.0
                else:
                    rz[32:36, cch] = NINF
                    rz[32:36, 16 + cch] = 0.0
                if cch % 2 == 1:
                    rn[:, cch] = 0.0
                else:
                    rn[:, 16 + cch] = 0.0
        m["rz"], m["rn"] = rz, rn
        kaug = np.zeros((16, NKEY), np.float32)
        qaug = np.zeros((16, T), np.float32)
        kaug[0, :] = 1.0
        for j in range(8):
            kaug[1 + j, 256 + j * 256:256 + (j + 1) * 256] = 1.0
        kaug[9, 0:256] = 1.0
        if not sample:
            for j in range(8):
                qaug[1 + j, :] = NEG
                qaug[1 + j, j * 256:(j + 1) * 256] = 0.0
            qaug[9, :] = NEG
        if LOWP:
            kaug = kaug.astype(ml_dtypes.bfloat16)
            qaug = qaug.astype(ml_dtypes.bfloat16)
        m["kaugc"], m["qaugc"] = kaug, qaug
        m["ropetab"] = _rope_tab(sample)
        in_maps.append({k: (np.ascontiguousarray(v) if v.dtype == ml_dtypes.bfloat16 else np.ascontiguousarray(v, dtype=np.float32)) for k, v in m.items()})

    res = run_bass_kernel_spmd(nc, in_maps, core_ids=list(range(8)))
    R = res.results
    if DEBUG:
        kernel.dbg = [R[i]["o_dbg"] for i in range(8)]
    y_prompt = np.concatenate([R[i]["o_y"].reshape(8, 256, 1024) for i in range(4)], 0)
    y_sample = np.stack([R[4 + i]["o_y"] for i in range(4)], 0)
    oC = np.concatenate([R[i]["o_C"] for i in range(4)], 1)
    new_C = np.ascontiguousarray(np.transpose(oC, (1, 0, 2, 3, 4)).reshape(32, L, 2, 4, 32, 64))
    on = np.concatenate([R[i]["o_n"].reshape(L, 8, 2, 128) for i in range(4)], 1)
    new_n = np.ascontiguousarray(np.transpose(on, (1, 0, 2, 3)).reshape(32, L, 2, 4, 32))
    om = np.concatenate([R[i]["o_m"] for i in range(4)], 3)
    new_m = np.ascontiguousarray(np.transpose(om, (3, 0, 1, 2)))
    oS = np.concatenate([R[i]["o_S"] for i in range(4)], 1)
    new_S = np.ascontiguousarray(np.transpose(oS, (1, 0, 2, 3, 4)).reshape(32, L, 2, 4, 32, 64))
    ockv = np.concatenate([R[i]["o_ckv"].reshape(L, 8, 256, 128) for i in range(4)], 1)
    new_ckv = np.ascontiguousarray(np.transpose(ockv, (1, 0, 2, 3)))
    okr = np.concatenate([R[i]["o_kr"].reshape(L, 8, 256, 128) for i in range(4)], 1)
    okr = np.concatenate([okr[..., 64:80], okr[..., 96:112]], -1)
    new_kr = np.ascontiguousarray(np.transpose(okr, (1, 0, 2, 3)))
    return (y_prompt, y_sample, new_C, new_n, new_m, new_S, new_ckv, new_kr)
```

```python
import types
import numpy as np
import ml_dtypes
from contextlib import ExitStack
import concourse.bass as bass
import concourse.mybir as mybir
from concourse.bass_utils import run_bass_kernel_spmd

F32 = mybir.dt.float32
BF16 = mybir.dt.bfloat16
AF = mybir.ActivationFunctionType
ALU = mybir.AluOpType
AX = mybir.AxisListType

DEPTH = 4
DEBUG = False
LOWP = True
STAGE = 0
T = 2048
NT = 16
NB = 4
KC = 8
NKEY = 2304
NKT = 18
FH = 2816
NJ = 22
DKS = 32 ** -0.5
A_SCALE = 96 ** -0.5
ALPHA = (2 * DEPTH) ** 0.25
NEG = -30000.0
NINF = -1.0e30


def _freeze(fn):
    if fn is None or fn.__closure__ is None:
        return fn
    cells = []
    for c in fn.__closure__:
        try:
            cells.append(types.CellType(c.cell_contents))
        except ValueError:
            cells.append(c)
    return types.FunctionType(fn.__code__, fn.__globals__, fn.__name__, fn.__defaults__, tuple(cells))


class Prog:
    ENGS = ("pe", "act", "dve", "pool", "sp")

    def __init__(self, nc):
        self.nc = nc
        self.ops = {e: [] for e in self.ENGS}
        self.cnt = {}
        self.sem_names = []
        self.waited = {e: {} for e in self.ENGS}
        self.last_w = {}
        self.reads = {}
        self.out_tickets = []

    def _getsem(self, name):
        if name not in self.cnt:
            self.cnt[name] = 0
            self.sem_names.append(name)
        return name

    def _deps(self, eng, reads, writes):
        need = {}

        def add(t):
            if t is None:
                return
            s, v = t
            if eng == "pe" and s == "c_pe":
                return
            if need.get(s, 0) < v:
                need[s] = v
        for k in reads:
            add(self.last_w.get(k))
        for k in writes:
            add(self.last_w.get(k))
            for t in self.reads.get(k, ()):
                add(t)
        waits = []
        wd = self.waited[eng]
        for s, v in need.items():
            if wd.get(s, 0) < v:
                wd[s] = v
                waits.append((s, v))
        return waits

    def _commit(self, ticket, reads, writes):
        for k in reads:
            self.reads.setdefault(k, []).append(ticket)
        for k in writes:
            self.last_w[k] = ticket
            self.reads[k] = []

    def op(self, eng, fn, reads=(), writes=()):
        reads = tuple(reads)
        writes = tuple(writes)
        waits = self._deps(eng, reads, writes)
        s = self._getsem("c_" + eng)
        self.cnt[s] += 1
        self.ops[eng].append((waits, _freeze(fn), s, 1))
        self._commit((s, self.cnt[s]), reads, writes)

    def dma(self, fn, reads=(), writes=(), semkey=None, is_out=False, eng="sp"):
        reads = tuple(reads)
        writes = tuple(writes)
        waits = self._deps(eng, reads, writes)
        s = self._getsem("d_" + str(semkey))
        self.cnt[s] += 16
        self.ops[eng].append((waits, _freeze(fn), s, 16))
        self._commit((s, self.cnt[s]), reads, writes)
        if is_out:
            self.out_tickets.append((s, self.cnt[s]))

    def barrier(self):
        allc = [(s, v) for s, v in self.cnt.items() if v > 0]
        for e in self.ENGS:
            waits = []
            wd = self.waited[e]
            for s, v in allc:
                if e == "pe" and s == "c_pe":
                    continue
                if wd.get(s, 0) < v:
                    wd[s] = v
                    waits.append((s, v))
            if waits:
                self.ops[e].append((waits, None, None, 0))
        self.last_w = {}
        self.reads = {}

    def emit(self):
        nc = self.nc
        fin = {}
        for s, v in self.out_tickets:
            fin[s] = max(fin.get(s, 0), v)
        sem = {}
        with ExitStack() as st:
            for i, name in enumerate(self.sem_names):
                sem[name] = st.enter_context(nc.semaphore("s%d" % i))
            block = st.enter_context(nc.Block())

            def replay(ename):
                def body(e):
                    for waits, fn, s, inc in self.ops[ename]:
                        for ws, wv in waits:
                            e.wait_ge(sem[ws], wv)
                        if fn is not None:
                            fn(e).then_inc(sem[s], inc)
                    if ename == "sp":
                        for s, v in fin.items():
                            e.wait_ge(sem[s], v)
                return body
            block.tensor(replay("pe"))
            block.scalar(replay("act"))
            block.vector(replay("dve"))
            block.gpsimd(replay("pool"))
            block.sync(replay("sp"))


def build(L=DEPTH):
    nc = bass.Bass("TRN2", target_bir_lowering=False)

    def din(name, shape, dt=F32):
        return nc.dram_tensor(name, list(shape), dt, kind="ExternalInput").ap()

    def dout(name, shape):
        return nc.dram_tensor(name, list(shape), F32, kind="ExternalOutput").ap()

    d_x = din("x", [T, 1024])
    d_cond = din("condT", [128, 8])
    d_wada = din("w_ada", [L, 1024, 6144])
    d_bada = din("b_adaT", [L, 128, 48])
    d_win = din("w_in", [L, 1024, 2000])
    d_wgi = din("w_gi", [L, 1024, 64])
    d_wgf = din("w_gf", [L, 1024, 64])
    d_gb = din("gb", [L, 64, 2])
    d_wkr = din("w_kr", [L, 1024, 128])
    d_wkrr = din("w_krr", [L, 1024, 128])
    d_gw2 = din("g_w2", [L, 2, 16, 128])
    d_gb2 = din("g_b2T", [L, 128, 2])
    d_ng = din("normg", [L, 128, 7])
    d_ln = din("lnp", [L, 128, 32])
    d_wuq = din("w_uqp", [L, 8, 256, 128])
    d_wuqr = din("w_uqr", [L, 8, 256, 128])
    d_wukv = din("w_ukv", [L, 128, 1024])
    d_wout = din("w_out", [L, 1024, 1024])
    d_wg = din("w_fg", [L, 1024, FH])
    d_wu = din("w_fu", [L, 1024, FH])
    d_wd = din("w_fd", [L, FH, 1024])
    d_cn0 = din("cn0", [L, 2, 128, 260])
    d_m0 = din("m0", [L, 64, 1])
    d_s0 = din("s0", [L, 2, 128, 256])
    d_ckvc = din("ckv_ctx", [L, 256, 128])
    d_krc = din("kr_ctx", [L, 256, 128])
    d_cst = din("cst", [128, 1292])
    d_rz = din("rz", [64, 32])
    d_rn = din("rn", [128, 32])
    d_kaug = din("kaugc", [16, NKEY], BF16 if LOWP else F32)
    d_qaug = din("qaugc", [16, T], BF16 if LOWP else F32)
    d_tab = din("ropetab", [128, 192])

    o_y = dout("o_y", [T, 1024])
    o_C = dout("o_C", [L, 8, 2, 128, 64])
    o_n = dout("o_n", [L, 16, 128])
    o_m = dout("o_m", [L, 2, 4, 8])
    o_S = dout("o_S", [L, 8, 2, 128, 64])
    o_ckv = dout("o_ckv", [L, T, 128])
    o_kr = dout("o_kr", [L, T, 128])

    o_dbg = dout("o_dbg", [128, 16384]) if DEBUG else None
    dbg_map = {}
    dbg_off = [0]
    P = Prog(nc)

    def DBG(name, ap, key):
        if not DEBUG:
            return
        n = ap.shape[1]
        p = ap.shape[0]
        off = dbg_off[0]
        dbg_off[0] += n
        dbg_map[name] = (off, p, n)
        P.dma(lambda e: e.dma_start(out=o_dbg[0:p, off:off + n], in_=ap), reads=[key], writes=(), semkey="dbg", is_out=True)
    build.dbg_map = dbg_map
    with ExitStack() as st:
        def sb(name, shape):
            return st.enter_context(nc.sbuf_tensor("sb_" + name, list(shape), F32))

        xT = sb("xT", [128, 8, T])
        mixT = sb("mixT", [128, 8, T])
        ARW = 17664
        arena = sb("arena", [128, ARW])
        cst = sb("cst", [128, 1292])
        MOD = sb("MOD", [128, L, 48])
        OP1 = sb("OP1", [128, L, 16])
        NG = sb("NG", [128, L, 7])
        LNP = sb("LNP", [128, L, 32])
        GB2 = sb("GB2", [128, L, 2])
        sc8 = sb("sc8", [128, 8])
        RZ = sb("RZ", [64, 32])
        RN = sb("RN", [128, 32])
        TAB = sb("TAB", [128, 192])
        bcolT = sb("bcolT", [128, 2])
        bn16 = sb("bn16", [16, 256])
        browT = bn16[0:1, :]
        gbt = sb("gbt", [64, 2])
        dmy = sb("dmy", [1, 1])
        st4T = sb("st4T", [128, 8])
        ngb2 = sb("ngb2", [128, 2])
        noutT = bn16[:, 0:128]
        ps = [st.enter_context(nc.psum_tensor("psum%d" % i, [128, 512], F32)) for i in range(8)]

        ident = cst[:, 0:128]
        ones = cst[:, 128:256]
        m01f = cst[:, 256:384]
        m01b = cst[:, 384:512]
        mngf = cst[:, 512:640]
        mngb = cst[:, 640:768]
        bmask = cst[:, 768:772]
        BI = cst[0:64, 772:1284]
        nbi4 = cst[0:64, 1284:1288]
        id4 = cst[0:64, 1288:1292]
        bit = sb("bit", [64, 128])
        d_bit = din("bit", [64, 128])

        def AR(off, n):
            assert off + n <= ARW, (off, n)
            return arena[:, off:off + n]

        def MX(off, n):
            flat = mixT[:, 2:8, :].rearrange("p c t -> p (c t)")
            return flat[:, off:off + n]

        rr = [0]

        def evq():
            rr[0] ^= 1
            return "act" if rr[0] else "dve"

        def mm(out, lhsT, rhs, start, stop, reads, writes):
            P.op("pe", lambda e: e.matmul(out, lhsT=lhsT, rhs=rhs, start=start, stop=stop), reads, writes)

        def tr(out, in_, reads, writes):
            P.op("pe", lambda e: e.transpose(out=out, in_=in_, identity=ident[0:in_.shape[0], 0:in_.shape[0]]), list(reads) + ["cst"], writes)

        def act(out, in_, func, reads, writes, bias=0.0, scale=1.0):
            P.op("act", lambda e: e.activation(out=out, in_=in_, func=func, bias=bias, scale=scale), reads, writes)

        def ld(out, in_, key, reads=(), semkey=None):
            P.dma(lambda e: e.dma_start(out=out, in_=in_), reads=reads, writes=[key], semkey=semkey or key)

        def st_out(out, in_, key, semkey, nc_ok=False):
            if nc_ok:
                P.dma(lambda e: e.dma_start(out=out, in_=in_, allow_slow_non_contiguous=True), reads=[key], writes=(), semkey=semkey, is_out=True)
            else:
                P.dma(lambda e: e.dma_start(out=out, in_=in_), reads=[key], writes=(), semkey=semkey, is_out=True)

        def V(eng, fn, reads, writes):
            P.op(eng, fn, reads, writes)

        ld(cst[:], d_cst[:, :], "cst")
        ld(bit[:], d_bit[:, :], "bit")
        ld(RZ[:], d_rz[:, :], "RZ")
        ld(RN[:], d_rn[:, :], "RN")
        ld(TAB[:], d_tab[:, :], "TAB")
        ld(sc8[:], d_cond[:, :], "sc8")
        for l in range(L):
            ld(NG[:, l, :], d_ng[l], "NG", semkey="NG")
            ld(LNP[:, l, :], d_ln[l], "LNP", semkey="LNP")
            ld(GB2[:, l, :], d_gb2[l], "GB2", semkey="GB2")
            ld(MOD[:, l, :], d_bada[l], "MODb", semkey="MODb")
        act(sc8[:], sc8[:], AF.Silu, ["sc8"], ["sc8"])

        stg = [AR(4096, 1024), AR(5120, 1024)]
        for tt in range(NT):
            s_ = stg[tt % 2]
            ld(s_, d_x[tt * 128:(tt + 1) * 128, :], "stg%d" % (tt % 2))
            for half in range(2):
                pb = ps[(tt * 2 + half) % 4]
                pk = "ps%d" % ((tt * 2 + half) % 4)
                for q in range(4):
                    c = half * 4 + q
                    tr(pb[:, q * 128:(q + 1) * 128], s_[:, c * 128:(c + 1) * 128], ["stg%d" % (tt % 2)], [pk])
                e_ = evq()
                dst = xT[:, half * 4:half * 4 + 4, tt * 128:(tt + 1) * 128]
                src = pb[:, :].rearrange("p (q t) -> p q t", q=4)
                if e_ == "act":
                    V("act", lambda e, dst=dst, src=src: e.copy(out=dst, in_=src), [pk], ["x%d" % (tt // 4)])
                else:
                    V("dve", lambda e, dst=dst, src=src: e.tensor_copy(out=dst, in_=src), [pk], ["x%d" % (tt // 4)])
        DBG("xT0", xT[:, 0, 0:256], "x0")
        DBG("xT7", xT[:, 7, 1792:2048], "x3")

        WS = [AR(0, 2048), AR(2048, 2048)]
        wi = [0]

        def wslot():
            i = wi[0]
            wi[0] ^= 1
            return WS[i], "w%d" % i

        for l in range(L):
            for g in range(24):
                W_, wk = wslot()
                Wv = W_.rearrange("p (c n) -> p c n", c=8)
                ld(Wv, d_wada[l].rearrange("(c p) n -> p c n", p=128)[:, :, g * 256:(g + 1) * 256], wk)
                for jj in range(2):
                    j = g * 2 + jj
                    for c in range(KC):
                        mm(ps[4][:, j:j + 1], Wv[:, c, jj * 128:(jj + 1) * 128], sc8[:, c:c + 1], c == 0, c == KC - 1, [wk, "sc8"], ["ps4"])
            V("dve", lambda e, l=l: e.tensor_tensor(out=MOD[:, l, :], in0=ps[4][:, 0:48], in1=MOD[:, l, :], op=ALU.add), ["ps4", "MODb"], ["MOD"])
            V("dve", lambda e, l=l: e.tensor_scalar(out=OP1[:, l, 0:8], in0=MOD[:, l, 8:16], scalar1=1.0, scalar2=None, op0=ALU.add), ["MOD"], ["OP1"])
            V("dve", lambda e, l=l: e.tensor_scalar(out=OP1[:, l, 8:16], in0=MOD[:, l, 32:40], scalar1=1.0, scalar2=None, op0=ALU.add), ["MOD"], ["OP1"])
        DBG("MOD0", MOD[:, 0, :], "MOD")
        DBG("sc8", sc8[:, :], "sc8")
        P.barrier()

        XK = ["x0", "x1", "x2", "x3"]

        def load_w(dram_ap, ncols, extra_writes=()):
            W_, wk = wslot()
            Wv = W_[:, 0:8 * ncols].rearrange("p (c n) -> p c n", c=8)
            wks = ["%s_%d" % (wk, c) for c in range(KC)]
            P.dma(lambda e: e.dma_start(out=Wv, in_=dram_ap.rearrange("(c p) n -> p c n", p=128)), reads=(), writes=list(wks) + list(extra_writes), semkey=wk)
            return Wv, wks

        def scale_w(Wv, wks, l, which):
            for c in range(KC):
                if c % 2 == 0:
                    V("dve", lambda e, c=c: e.tensor_scalar(out=Wv[:, c, :], in0=Wv[:, c, :], scalar1=OP1[:, l, which * 8 + c:which * 8 + c + 1],
                                                            scalar2=None, op0=ALU.mult), [wks[c], "OP1"], [wks[c]])
                else:
                    act(Wv[:, c, :], Wv[:, c, :], AF.Copy, [wks[c], "OP1"], [wks[c]], scale=OP1[:, l, which * 8 + c:which * 8 + c + 1])

        pend_main = [None]
        bsel = [0]
        ring_i = [0]
        RINGK = ["xring%d" % i for i in range(4)] + ["wb%d_%d" % (s_, c) for s_ in range(2) for c in range(KC)]

        def flush():
            if pend_main[0] is not None:
                f_ = pend_main[0]
                pend_main[0] = None
                f_()

        def BAR():
            flush()
            P.barrier()

        def proj_fm(l, dram_ap, ncols, evac, which=0, extra_bias=None, bkey="bcol"):
            Wv, wks = load_w(dram_ap, ncols)
            sh0 = 0 if which == 0 else 24
            for c in range(KC):
                mm(ps[6][0:ncols, 0:1], Wv[:, c, :], MOD[:, l, sh0 + c:sh0 + c + 1], c == 0, c == KC - 1, [wks[c], "MOD"], ["ps6"])
            kb = bsel[0]
            bsel[0] ^= 1
            bcol = bcolT[:, kb:kb + 1]
            bkey = "bcol%d" % kb
            if extra_bias is None:
                V("dve", lambda e: e.tensor_copy(out=bcol[0:ncols, :], in_=ps[6][0:ncols, 0:1]), ["ps6"], [bkey])
            else:
                V("dve", lambda e: e.tensor_tensor(out=bcol[0:ncols, :], in0=ps[6][0:ncols, 0:1], in1=extra_bias, op=ALU.add), ["ps6", "gbt"], [bkey])
            if LOWP and ncols <= 128:
                slot = 0 if wks[0].startswith("w0") else 1
                WBv = AR(3072 + slot * 512, 512).bitcast(BF16)[:, 0:8 * ncols].rearrange("p (c n) -> p c n", c=8)
                wbks = ["wb%d_%d" % (slot, c) for c in range(KC)]
                for c in range(KC):
                    if c % 2 == 0:
                        V("dve", lambda e, c=c: e.tensor_scalar(out=WBv[:, c, :], in0=Wv[:, c, :], scalar1=OP1[:, l, which * 8 + c:which * 8 + c + 1],
                                                                scalar2=None, op0=ALU.mult), [wks[c], "OP1"], [wbks[c]])
                    else:
                        act(WBv[:, c, :], Wv[:, c, :], AF.Copy, [wks[c], "OP1"], [wbks[c]], scale=OP1[:, l, which * 8 + c:which * 8 + c + 1])
                flush()

                def main():
                    for tb in range(NB):
                        pb = ps[tb % 2]
                        pk = "ps%d" % (tb % 2)
                        for c in range(KC):
                            i_ = ring_i[0]
                            ring_i[0] = (i_ + 1) % 4
                            xb = AR(1024 + i_ * 256, 256).bitcast(BF16)
                            xbk = "xring%d" % i_
                            if i_ % 2 == 0:
                                act(xb, xT[:, c, tb * 512:(tb + 1) * 512], AF.Copy, [XK[tb]], [xbk])
                            else:
                                V("dve", lambda e, xb=xb, c=c, tb=tb: e.tensor_copy(out=xb, in_=xT[:, c, tb * 512:(tb + 1) * 512]), [XK[tb]], [xbk])
                            mm(pb[0:ncols, :], WBv[:, c, :], xb, c == 0, c == KC - 1, [wbks[c], xbk], [pk])
                        evac(tb, pb[0:ncols, :], pk, bcol[0:ncols, :], bkey)
                pend_main[0] = main
                return
            scale_w(Wv, wks, l, which)
            flush()

            def main():
                for tb in range(NB):
                    pb = ps[tb % 2]
                    pk = "ps%d" % (tb % 2)
                    for c in range(KC):
                        mm(pb[0:ncols, :], Wv[:, c, :], xT[:, c, tb * 512:(tb + 1) * 512], c == 0, c == KC - 1, [wks[c], XK[tb]], [pk])
                    evac(tb, pb[0:ncols, :], pk, bcol[0:ncols, :], bkey)
            pend_main[0] = main

        def proj_tm(l, dram_ap, ncols, evac):
            flush()
            Wv, wks = load_w(dram_ap, ncols, extra_writes=(RINGK if LOWP else ()))
            for c in range(KC):
                mm(ps[6][0:1, 0:ncols], MOD[:, l, c:c + 1], Wv[:, c, :], c == 0, c == KC - 1, [wks[c], "MOD"], ["ps6"])
            brow = browT[0:1, 0:ncols]
            V("dve", lambda e: e.tensor_copy(out=brow, in_=ps[6][0:1, 0:ncols]), ["ps6"], ["brow"])
            scale_w(Wv, wks, l, 0)
            for tt in range(NT):
                pb = ps[tt % 2]
                pk = "ps%d" % (tt % 2)
                for c in range(KC):
                    mm(pb[:, 0:ncols], xT[:, c, tt * 128:(tt + 1) * 128], Wv[:, c, :], c == 0, False, [wks[c], XK[tt // 4]], [pk])
                mm(pb[:, 0:ncols], ones[0:1, 0:128], brow, False, True, ["cst", "brow"], [pk])
                evac(tt, pb[:, 0:ncols], pk)
            if LOWP:
                V("pool", lambda e: e.memset(dmy[:, :], 0.0), [], list(wks) + RINGK)

        def ev_copy_bias(dst_fn, key_fn, scale=None):
            def f(tb, pap, pk, bcol, bkey):
                dst = dst_fn(tb)
                if scale is None:
                    act(dst, pap, AF.Identity, [pk, bkey], [key_fn(tb)], bias=bcol)
                else:
                    V("dve", lambda e: e.tensor_scalar(out=dst, in0=pap, scalar1=bcol, scalar2=scale, op0=ALU.add, op1=ALU.mult), [pk, bkey], [key_fn(tb)])
            return f

        def ev_func_bias(dst_fn, key_fn, func):
            def f(tb, pap, pk, bcol, bkey):
                act(dst_fn(tb), pap, func, [pk, bkey], [key_fn(tb)], bias=bcol)
            return f

        def headnorm_out(src3, nh, center, l, gidx, mixc0, tt, srckey, pbi=7, kk=0):
            st4 = st4T[:, 0:nh]
            st4b = st4T[:, 4:4 + nh]
            flat = MXT["hn_tmp"][kk]
            hk = "hn_tmp%d" % kk
            tmp = flat.rearrange("p (h d) -> p h d", h=nh)
            if center:
                V("dve", lambda e: e.tensor_reduce(out=st4, in_=src3, axis=AX.X, op=ALU.add), [srckey], ["st4"])
                V("dve", lambda e: e.tensor_scalar(out=st4, in0=st4, scalar1=-1.0 / 64, scalar2=None, op0=ALU.mult), ["st4"], ["st4"])
                V("dve", lambda e: e.tensor_tensor(out=src3, in0=src3, in1=st4.unsqueeze(2).to_broadcast([128, nh, 64]), op=ALU.add), [srckey, "st4"], [srckey])
            V("dve", lambda e: e.tensor_tensor(out=tmp, in0=src3, in1=src3, op=ALU.mult), [srckey], [hk])
            V("dve", lambda e: e.tensor_reduce(out=st4b, in_=tmp, axis=AX.X, op=ALU.add), [hk], ["st4b"])
            act(st4b, st4b, AF.Ln, ["st4b", "EPS"], ["st4b"], bias=EPS6[:, 0:1], scale=1.0 / 64)
            act(st4b, st4b, AF.Exp, ["st4b"], ["st4b"], scale=-0.5)
            V("dve", lambda e: e.tensor_tensor(out=tmp, in0=src3, in1=st4b.unsqueeze(2).to_broadcast([128, nh, 64]), op=ALU.mult), [srckey, "st4b"], [hk])

            def fin():
                pkh = "ps%d" % pbi
                for j in range(2):
                    tr(ps[pbi][:, j * 128:(j + 1) * 128], flat[:, j * 128:(j + 1) * 128], [hk], [pkh])
                for j in range(2):
                    V("dve", lambda e, j=j: e.scalar_tensor_tensor(out=mixT[:, mixc0 + j, tt * 128:(tt + 1) * 128], in0=ps[pbi][:, j * 128:(j + 1) * 128],
                                                                   scalar=NG[:, l, gidx + j:gidx + j + 1], in1=mixT[:, mixc0 + j, tt * 128:(tt + 1) * 128],
                                                                   op0=ALU.mult, op1=ALU.mult), [pkh, "NG", "mix%d" % mixc0], ["mix%d" % mixc0])
            return fin

        EPS6 = sb("EPS6", [128, 2])
        V("pool", lambda e: e.memset(EPS6[:, 0:1], 1e-6), [], ["EPS"])
        V("pool", lambda e: e.memset(EPS6[:, 1:2], 1e-5), [], ["EPS"])
        MXT = {}

        def layernorm_block(l, tb, gi, bi, LNT, LNK):
            xk = XK[tb]
            sl = slice(tb * 512, (tb + 1) * 512)
            for c in range(KC):
                mm(ps[4][:, :], ones[:, 0:128], xT[:, c, sl], c == 0, c == KC - 1, ["cst", xk], ["ps4"])
            mean = LNT[0]
            V("dve", lambda e: e.tensor_scalar(out=mean, in0=ps[4][:, :], scalar1=-1.0 / 1024, scalar2=None, op0=ALU.mult), ["ps4"], [LNK[0]])
            sq = LNT[1]
            for c in range(KC):
                V("dve", lambda e, c=c: e.tensor_tensor(out=xT[:, c, sl], in0=xT[:, c, sl], in1=mean, op=ALU.add), [xk, LNK[0]], [xk])
                act(sq[c % 2], xT[:, c, sl], AF.Square, [xk], [LNK[1][c % 2]])
                mm(ps[5][:, :], ones[:, 0:128], sq[c % 2], c == 0, c == KC - 1, ["cst", LNK[1][c % 2]], ["ps5"])
            rstd = LNT[2]
            act(rstd, ps[5][:, :], AF.Sqrt, ["ps5", "EPS"], [LNK[2]], bias=EPS6[:, 1:2], scale=1.0 / 1024)
            V("dve", lambda e: e.reciprocal(out=rstd, in_=rstd), [LNK[2]], [LNK[2]])
            for c in range(KC):
                V("dve", lambda e, c=c: e.scalar_tensor_tensor(out=xT[:, c, sl], in0=xT[:, c, sl], scalar=LNP[:, l, gi + c:gi + c + 1], in1=rstd,
                                                               op0=ALU.mult, op1=ALU.mult), [xk, LNK[2], "LNP"], [xk])
                act(xT[:, c, sl], xT[:, c, sl], AF.Identity, [xk, "LNP"], [xk], bias=LNP[:, l, bi + c:bi + c + 1])

        for l in range(L):
            win = d_win[l]
            qT_m = AR(4096, 2048)
            kT_m = AR(6144, 2048)
            v_aug = AR(8192, 4160).rearrange("p (t h d) -> p t h d", t=16, h=4)
            RA = AR(12352, 2048)
            RB = AR(14400, 2048)
            Cn = AR(0, 260 * 2).rearrange("p (d n) -> p d n", d=2)
            DEC = AR(520, 32)
            stC = AR(552, 16 * 65).rearrange("p (s n) -> p s n", s=16)
            hsum = MX(0, 4096).rearrange("p (t n) -> p t n", t=16)
            kTM = MX(4096, 2048).rearrange("p (t n) -> p t n", t=16)
            RC = MX(6144, 2048)
            Qblk = MX(8192, 512)
            NMB = MX(8704, 512)
            Dsb = MX(9216, 512)
            Ssb = MX(9728, 512)
            wkt = MX(10240, 128)
            qi = MX(10368, 128)
            Utmp = MX(10496, 260)
            ecol = MX(10756, 8)
            hout = MX(10764, 256)
            MXT["hn_tmp"] = [MX(11020, 256), AR(1980, 256)]
            itmp = MX(11276, 128)
            iexp = MX(11404, 128)
            MST = MX(11532, 16)
            MPE = MX(11548, 16)
            DLG = MX(11564, 16)
            CML = MX(11580, 16)
            BLR = MX(11596, 16)
            D0 = MX(11612, 16)
            D1 = MX(11628, 16)
            r4 = MX(11644, 4)
            r4b = MX(11648, 4)
            m0t = MX(11652, 1)

            proj_fm(l, win[:, 0:128], 128, ev_copy_bias(lambda tb: qT_m[:, tb * 512:(tb + 1) * 512], lambda tb: "qTm"))
            proj_fm(l, win[:, 128:256], 128, ev_copy_bias(lambda tb: kT_m[:, tb * 512:(tb + 1) * 512], lambda tb: "kTm", scale=DKS))
            proj_tm(l, win[:, 128:256], 128, lambda tt, pap, pk: act(kTM[:, tt, :], pap, AF.Copy, [pk], ["kTM"], scale=DKS))
            V("pool", lambda e: e.memset(v_aug[:, :, :, 64:65], 1.0), [], ["vaug"])
            proj_tm(l, win[:, 256:512], 256, lambda tt, pap, pk: V(evq2(), lambda e: e.tensor_copy(out=v_aug[:, tt, :, 0:64], in_=pap.rearrange("p (h d) -> p h d", h=4)), [pk], ["vaug"]))
            for j in range(2):
                proj_fm(l, win[:, 512 + j * 128:640 + j * 128], 128, ev_func_bias(lambda tb, j=j: mixT[:, j, tb * 512:(tb + 1) * 512], lambda tb: "mix0", AF.Sigmoid))
            ld(gbt[:, :], d_gb[l], "gbt")
            proj_fm(l, d_wgi[l], 64, ev_copy_bias(lambda tb: RA[0:64, tb * 512:(tb + 1) * 512], lambda tb: "RA"), extra_bias=gbt[0:64, 0:1])
            proj_fm(l, d_wgf[l], 64, ev_copy_bias(lambda tb: RB[0:64, tb * 512:(tb + 1) * 512], lambda tb: "RB"), extra_bias=gbt[0:64, 1:2])
            if DEBUG:
                flush()
            DBG("qTm", qT_m[:, 0:256], "qTm")
            DBG("kTm", kT_m[:, 0:256], "kTm")
            DBG("kTM", kTM[:, 0, :], "kTM")
            DBG("vaug", v_aug[:, 0, :, :].rearrange("p h n -> p (h n)"), "vaug")
            DBG("mo", mixT[:, 0, 0:256], "mix0")
            DBG("RA0", RA[0:64, 0:256], "RA")
            DBG("RB0", RB[0:64, 0:256], "RB")
            BAR()
            if STAGE == 1:
                P.emit()
                return nc
            ld(Cn[:, :, :], d_cn0[l].rearrange("d p n -> p d n"), "Cn")
            ld(m0t[0:64, :], d_m0[l], "m0t")
            V("pool", lambda e: e.memset(RC[0:64, :], 0.0), [], ["RC"])
            V("pool", lambda e: e.memset(MX(11532, 112)[0:64, :], 0.0), [], ["MST", "MPE", "DLG", "CML", "BLR", "D0", "D1"])
            V("pool", lambda e: e.memset(itmp[0:64, :], 0.0), [], ["itmp"])
            R36 = slice(0, 36)
            act(RB[R36, :], RB[R36, :], AF.Exp, ["RB"], ["RB"], scale=-1.0)
            act(RB[R36, :], RB[R36, :], AF.Ln, ["RB"], ["RB"], bias=1.0)
            for c in range(NT):
                sl = slice(c * 128, (c + 1) * 128)
                V("dve", lambda e, sl=sl: e.tensor_tensor_scan(out=RC[0:4, sl], data0=ones[0:4, 0:128], data1=RB[0:4, sl], initial=0.0, op0=ALU.mult, op1=ALU.add), ["RB", "cst"], ["RC"])
                rs = slice((c + 1) * 128 - 1, c * 128 - 1 if c > 0 else None, -1)
                V("dve", lambda e, rs=rs: e.tensor_tensor_scan(out=RC[32:36, rs], data0=ones[32:36, 0:128], data1=RB[32:36, rs], initial=0.0, op0=ALU.mult, op1=ALU.add), ["RB", "cst"], ["RC"])
            V("dve", lambda e: e.tensor_tensor(out=RA[R36, :], in0=RA[R36, :], in1=RC[R36, :], op=ALU.add), ["RA", "RC"], ["RA"])
            for c in range(NT):
                sl = slice(c * 128, (c + 1) * 128)
                V("dve", lambda e, sl=sl: e.tensor_tensor_scan(out=RB[0:4, sl], data0=RA[0:4, sl], data1=RA[0:4, sl], initial=NINF, op0=ALU.max, op1=ALU.max), ["RA", "RB"], ["RB"])
                rs = slice((c + 1) * 128 - 1, c * 128 - 1 if c > 0 else None, -1)
                V("dve", lambda e, rs=rs: e.tensor_tensor_scan(out=RB[32:36, rs], data0=RA[32:36, rs], data1=RA[32:36, rs], initial=NINF, op0=ALU.max, op1=ALU.max), ["RA", "RB"], ["RB"])
            RB3 = RB.rearrange("p (c t) -> p c t", c=16)
            RC3 = RC.rearrange("p (c t) -> p c t", c=16)
            RA3 = RA.rearrange("p (c t) -> p c t", c=16)
            V("dve", lambda e: e.tensor_copy(out=CML[0:4, :], in_=RB3[0:4, :, 127]), ["RB"], ["CML"])
            V("dve", lambda e: e.tensor_copy(out=CML[32:36, :], in_=RB3[32:36, :, 0]), ["RB"], ["CML"])
            V("dve", lambda e: e.tensor_scalar(out=BLR[0:4, :], in0=RC3[0:4, :, 127], scalar1=-1.0, scalar2=None, op0=ALU.mult), ["RC"], ["BLR"])
            V("dve", lambda e: e.tensor_scalar(out=BLR[32:36, :], in0=RC3[32:36, :, 0], scalar1=-1.0, scalar2=None, op0=ALU.mult), ["RC"], ["BLR"])
            V("dve", lambda e: e.tensor_tensor(out=D0[R36, :], in0=RZ[R36, 0:16], in1=BLR[R36, :], op=ALU.add), ["RZ", "BLR"], ["D0"])
            V("dve", lambda e: e.tensor_tensor(out=D1[R36, :], in0=RZ[R36, 16:32], in1=CML[R36, :], op=ALU.max), ["RZ", "CML"], ["D1"])
            V("dve", lambda e: e.tensor_tensor(out=D1[R36, :], in0=D1[R36, :], in1=BLR[R36, :], op=ALU.add), ["D1", "BLR"], ["D1"])
            V("dve", lambda e: e.tensor_tensor_scan(out=MST[0:4, :], data0=D0[0:4, :], data1=D1[0:4, :], initial=m0t[0:4, 0:1], op0=ALU.add, op1=ALU.max), ["D0", "D1", "m0t"], ["MST"])
            V("dve", lambda e: e.tensor_tensor_scan(out=MST[32:36, ::-1], data0=D0[32:36, ::-1], data1=D1[32:36, ::-1], initial=m0t[32:36, 0:1], op0=ALU.add, op1=ALU.max), ["D0", "D1", "m0t"], ["MST"])
            V("dve", lambda e: e.tensor_copy(out=MPE[0:4, 1:16], in_=MST[0:4, 0:15]), ["MST"], ["MPE"])
            V("dve", lambda e: e.tensor_copy(out=MPE[0:4, 0:1], in_=m0t[0:4, 0:1]), ["m0t", "MPE"], ["MPE"])
            V("dve", lambda e: e.tensor_copy(out=MPE[32:36, 0:15], in_=MST[32:36, 1:16]), ["MST", "MPE"], ["MPE"])
            V("dve", lambda e: e.tensor_copy(out=MPE[32:36, 15:16], in_=m0t[32:36, 0:1]), ["m0t", "MPE"], ["MPE"])
            V("dve", lambda e: e.tensor_tensor(out=MPE[R36, :], in0=MPE[R36, :], in1=RZ[R36, 0:16], op=ALU.add), ["MPE", "RZ"], ["MPE"])
            V("dve", lambda e: e.tensor_tensor(out=MPE[R36, :], in0=MPE[R36, :], in1=RZ[R36, 16:32], op=ALU.max), ["MPE", "RZ"], ["MPE"])
            V("dve", lambda e: e.tensor_tensor(out=RB3[R36, :, :], in0=RB3[R36, :, :], in1=MPE[R36, :].unsqueeze(2).to_broadcast([36, 16, 128]), op=ALU.max), ["RB", "MPE"], ["RB"])
            V("dve", lambda e: e.tensor_tensor(out=CML[R36, :], in0=CML[R36, :], in1=MPE[R36, :], op=ALU.max), ["CML", "MPE"], ["CML"])
            V("dve", lambda e: e.tensor_tensor(out=RC[R36, :], in0=RC[R36, :], in1=RB[R36, :], op=ALU.subtract), ["RC", "RB"], ["RC"])
            V("dve", lambda e: e.tensor_tensor(out=RB3[R36, :, :], in0=RB3[R36, :, :], in1=CML[R36, :].unsqueeze(2).to_broadcast([36, 16, 128]), op=ALU.subtract), ["RB", "CML"], ["RB"])
            V("dve", lambda e: e.tensor_tensor(out=RA3[R36, :, :], in0=RA3[R36, :, :], in1=CML[R36, :].unsqueeze(2).to_broadcast([36, 16, 128]), op=ALU.subtract), ["RA", "CML"], ["RA"])
            V("dve", lambda e: e.tensor_tensor(out=DLG[R36, :], in0=MPE[R36, :], in1=CML[R36, :], op=ALU.subtract), ["MPE", "CML"], ["DLG"])
            for d in range(2):
                pb_ = d * 32
                mm(ps[6][:, 0:16], bit[pb_:pb_ + 4, :], DLG[pb_:pb_ + 4, :], True, True, ["bit", "DLG"], ["ps6"])
                act(DEC[:, d * 16:(d + 1) * 16], ps[6][:, 0:16], AF.Exp, ["ps6"], ["DEC"])
            st_out(o_m[l, 0], MST[0:4, 1::2], "MST", "o_m", nc_ok=True)
            st_out(o_m[l, 1], MST[32:36, 0::2], "MST", "o_m", nc_ok=True)
            V("pool", lambda e: e.memset(Qblk, 0.0), [], ["Qblk"])
            V("pool", lambda e: e.memset(NMB[0:64, :], 0.0), [], ["NMB"])

            Qb3 = Qblk.rearrange("p (h t) -> p h t", h=4)
            NMB3 = NMB.rearrange("p (h t) -> p h t", h=4)
            SsbK = [Ssb, MX(11656, 512)]
            ecolK = [ecol, MX(12168, 8)]
            qiK = [qi, AR(1592, 128)]
            UtK = [Utmp, AR(1720, 260)]
            seq = [(0, c) for c in range(NT)] + [(1, c) for c in range(NT - 1, -1, -1)]

            def mA(d, c, k):
                pb_ = d * 32
                R4 = slice(pb_, pb_ + 4)
                mng = mngf if d == 0 else mngb
                sl = slice(c * 128, (c + 1) * 128)
                Ssb_, ecol_, qi_, Ut_ = SsbK[k], ecolK[k], qiK[k], UtK[k]
                sk, ek, qk_, uk_ = "Ssb%d" % k, "ecol%d" % k, "qi%d" % k, "Ut%d" % k
                V("pool", lambda e: e.tensor_tensor(out=Qb3, in0=qT_m[:, sl].unsqueeze(1).to_broadcast([128, 4, 128]),
                                                   in1=bmask.unsqueeze(2).to_broadcast([128, 4, 128]), op=ALU.mult), ["qTm", "cst", "Qblk"], ["Qblk"])
                mm(ps[0][:, :], kT_m[:, sl], Qblk, True, True, ["kTm", "Qblk"], ["ps0"])
                V("dve", lambda e: e.tensor_tensor(out=NMB3[R4, :, :], in0=RB[R4, sl].unsqueeze(1).to_broadcast([4, 4, 128]),
                                                   in1=nbi4[R4, :].unsqueeze(2).to_broadcast([4, 4, 128]), op=ALU.mult), ["RB", "cst", "NMB"], ["NMB"])
                mm(ps[1][:, :], RA[R4, sl], BI[R4, :], True, False, ["RA", "cst"], ["ps1"])
                mm(ps[1][:, :], ones[R4, 0:128], NMB[R4, :], False, False, ["cst", "NMB"], ["ps1"])
                mm(ps[1][:, :], ident, mng.unsqueeze(1).to_broadcast([128, 4, 128]), False, True, ["cst"], ["ps1"])
                act(Dsb, ps[1][:, :], AF.Exp, ["ps1"], ["Dsb"])
                V("dve", lambda e: e.tensor_tensor(out=Ssb_, in0=ps[0][:, :], in1=Dsb, op=ALU.mult), ["ps0", "Dsb"], [sk])
                mm(ps[3][:, 0:4], RC[R4, sl], id4[R4, :], True, True, ["RC", "cst"], ["ps3"])
                mm(ps[3][:, 4:8], RA[R4, sl], id4[R4, :], True, True, ["RA", "cst"], ["ps3"])
                act(ecol_, ps[3][:, 0:8], AF.Exp, ["ps3"], [ek])
                V("dve", lambda e: e.tensor_scalar(out=itmp[R4, :], in0=RB[R4, sl], scalar1=-1.0, scalar2=DLG[R4, c:c + 1], op0=ALU.mult, op1=ALU.add),
                  ["RB", "DLG", "itmp"], ["itmp"])
                mm(ps[3][:, 128:256], bit[R4, :], itmp[R4, :], True, True, ["bit", "itmp"], ["ps3"])
                act(iexp, ps[3][:, 128:256], AF.Exp, ["ps3"], ["iexp"])
                V("pool", lambda e: e.tensor_tensor(out=qi_, in0=qT_m[:, sl], in1=iexp, op=ALU.mult), ["qTm", "iexp"], [qk_])
                V("pool", lambda e: e.tensor_tensor(out=wkt.rearrange("p (h k) -> p h k", h=4), in0=kTM[:, c, :].rearrange("p (h k) -> p h k", h=4),
                                                   in1=ecol_[:, 4:8].unsqueeze(2).to_broadcast([128, 4, 32]), op=ALU.mult), ["kTM", ek], ["wkt"])
                mm(ps[6][:, 0:260], wkt, v_aug[:, c, :, :].rearrange("p h n -> p (h n)"), True, True, ["wkt", "vaug"], ["ps6"])
                V("dve", lambda e: e.tensor_tensor(out=Ut_.rearrange("p (h n) -> p h n", h=4), in0=ps[6][:, 0:260].rearrange("p (h n) -> p h n", h=4),
                                                   in1=bmask.unsqueeze(2).to_broadcast([128, 4, 65]), op=ALU.mult), ["ps6", "cst"], [uk_])

            def mB(d, c, k):
                Ssb_, ecol_, qi_, Ut_ = SsbK[k], ecolK[k], qiK[k], UtK[k]
                sk, ek, qk_, uk_ = "Ssb%d" % k, "ecol%d" % k, "qi%d" % k, "Ut%d" % k
                pnd, pndk = (ps[2], "ps2") if k == 0 else (ps[4], "ps4")
                mm(pnd[:, 0:260], qi_, Cn[:, d, :], True, False, [qk_, "Cn"], [pndk])
                for h in range(4):
                    mm(pnd[:, h * 65:(h + 1) * 65], Ssb_[:, h * 128:(h + 1) * 128], v_aug[:, c, h, :], False, h == 3, [sk, "vaug"], [pndk])
                V("dve", lambda e: e.scalar_tensor_tensor(out=Cn[:, d, :], in0=Cn[:, d, :], scalar=DEC[:, d * 16 + c:d * 16 + c + 1], in1=Ut_,
                                                          op0=ALU.mult, op1=ALU.add), ["Cn", "DEC", uk_], ["Cn"])
                if (d == 0 and c % 2 == 1) or (d == 1 and c % 2 == 0):
                    sq_ = (c // 2) * 2 + d
                    V("dve", lambda e: e.tensor_reduce(out=stC[:, sq_, :], in_=Cn[:, d, :].rearrange("p (h n) -> p n h", h=4), axis=AX.X, op=ALU.add), ["Cn"], ["stC"])
                V("dve", lambda e: e.tensor_scalar(out=Cn[:, d, :], in0=Cn[:, d, :], scalar1=RN[:, d * 16 + c:d * 16 + c + 1], scalar2=None, op0=ALU.mult), ["Cn", "RN"], ["Cn"])
                nd = pnd[:, 0:260].rearrange("p (h n) -> p h n", h=4)
                act(r4b, nd[:, :, 64], AF.Copy, [pndk], ["r4b"])
                V("dve", lambda e: e.scalar_tensor_tensor(out=r4, in0=r4b, scalar=-1.0, in1=r4b, op0=ALU.mult, op1=ALU.max), ["r4b"], ["r4"])
                V("dve", lambda e: e.tensor_tensor(out=r4, in0=r4, in1=ecol_[:, 0:4], op=ALU.max), ["r4", ek], ["r4"])
                V("dve", lambda e: e.reciprocal(out=r4, in_=r4), ["r4"], ["r4"])
                if d == 0:
                    V("dve", lambda e: e.tensor_tensor(out=hsum[:, c, :].rearrange("p (h v) -> p h v", h=4), in0=nd[:, :, 0:64],
                                                       in1=r4.unsqueeze(2).to_broadcast([128, 4, 64]), op=ALU.mult), [pndk, "r4"], ["hsum"])
                else:
                    V("dve", lambda e: e.tensor_tensor(out=hout.rearrange("p (h v) -> p h v", h=4), in0=nd[:, :, 0:64],
                                                       in1=r4.unsqueeze(2).to_broadcast([128, 4, 64]), op=ALU.mult), [pndk, "r4"], ["hout"])
                    V("pool", lambda e: e.tensor_tensor(out=hsum[:, c, :], in0=hsum[:, c, :], in1=hout, op=ALU.add), ["hsum", "hout"], ["hsum"])
                    return headnorm_out(hsum[:, c, :].rearrange("p (h v) -> p h v", h=4), 4, True, l, 0, 0, c, "hsum", pbi=(7 if k == 0 else 5), kk=k)
                return None

            mA(seq[0][0], seq[0][1], 0)
            pend = None
            for i, (d, c) in enumerate(seq):
                if i + 1 < len(seq):
                    mA(seq[i + 1][0], seq[i + 1][1], (i + 1) % 2)
                fin_ = mB(d, c, i % 2)
                if pend is not None:
                    pend()
                pend = fin_
            if pend is not None:
                pend()
            st_out(o_C[l].rearrange("s d p v -> p (s d) v"), stC[:, :, 0:64], "stC", "o_C")
            tr(ps[7][0:16, 0:128], stC[:, :, 64], ["stC"], ["ps7"])
            V("dve", lambda e: e.tensor_copy(out=noutT[:, :], in_=ps[7][0:16, 0:128]), ["ps7"], ["nout"])
            st_out(o_n[l], noutT[:, :], "nout", "o_n")
            BAR()

            qT_g = AR(4096, 2048)
            kT_g = AR(6144, 2048)
            v_g = AR(8192, 4096).rearrange("p (t n) -> p t n", t=16)
            SPf = AR(12288, 2048)
            SPb = AR(14336, 2048)
            Sblk = AR(0, 512).rearrange("p (d n) -> p d n", d=2)
            stS = AR(512, 16 * 64).rearrange("p (s n) -> p s n", s=16)
            gw2 = AR(1536, 256).rearrange("p (d n) -> p d n", d=2)
            flatA = mixT[:, 4:8, :].rearrange("p c t -> p (c t)")

            def MA(off, n):
                return flatA[:, off:off + n]
            osum = MA(0, 4096).rearrange("p (t n) -> p t n", t=16)
            gaT = [MA(4096, 2048), MA(6144, 2048)]
            proj_fm(l, win[:, 784:912], 128, ev_copy_bias(lambda tb: qT_g[:, tb * 512:(tb + 1) * 512], lambda tb: "qTg", scale=DKS))
            proj_fm(l, win[:, 912:1040], 128, ev_copy_bias(lambda tb: kT_g[:, tb * 512:(tb + 1) * 512], lambda tb: "kTg"))
            proj_tm(l, win[:, 1040:1296], 256, lambda tt, pap, pk: V(evq2(), lambda e: e.tensor_copy(out=v_g[:, tt, :], in_=pap), [pk], ["vg"]))
            for j in range(2):
                proj_fm(l, win[:, 1296 + j * 128:1424 + j * 128], 128, ev_func_bias(lambda tb, j=j: mixT[:, 2 + j, tb * 512:(tb + 1) * 512], lambda tb: "mix2", AF.Silu))
            for d in range(2):
                proj_fm(l, win[:, 1552 + d * 16:1568 + d * 16], 16, ev_copy_bias(lambda tb, d=d: gaT[d][0:16, tb * 512:(tb + 1) * 512], lambda tb: "gaT"))
            BAR()
            ld(gw2[0:16, :, :], d_gw2[l].rearrange("d r n -> r d n"), "gw2")
            ld(Sblk[:, :, :], d_s0[l].rearrange("d p n -> p d n"), "Sblk")
            V("dve", lambda e: e.tensor_scalar(out=ngb2[:, :], in0=GB2[:, l, :], scalar1=-1.0, scalar2=None, op0=ALU.mult), ["GB2"], ["ngb2"])
            for d in range(2):
                SP = SPf if d == 0 else SPb
                for tb in range(NB):
                    pb = ps[tb % 2]
                    pk = "ps%d" % (tb % 2)
                    mm(pb[:, :], gw2[0:16, d, :], gaT[d][0:16, tb * 512:(tb + 1) * 512], True, True, ["gw2", "gaT"], [pk])
                    act(SP[:, tb * 512:(tb + 1) * 512], pb[:, :], AF.Exp, [pk, "ngb2"], ["SP%d" % d], bias=ngb2[:, d:d + 1], scale=-1.0)
                act(SP, SP, AF.Ln, ["SP%d" % d], ["SP%d" % d], bias=1.0)
            BAR()
            CS = MA(4096, 128)
            E1 = MA(4224, 128)
            E2 = MA(4352, 128)
            E3 = MA(4480, 128)
            qe = MA(4608, 128)
            ke = MA(4736, 128)
            kd = MA(4864, 128)
            kdT = MA(4992, 128)
            Qeb = MA(5120, 512)
            Asb = MA(5632, 512)
            Ug = MA(6144, 256)
            og = MA(6400, 256)
            MXT["hn_tmp"] = [MA(6656, 256), MA(7818, 256)]
            blc = MA(6912, 2)
            V("pool", lambda e: e.memset(Qeb, 0.0), [], ["Qeb"])
            Qe3 = Qeb.rearrange("p (h t) -> p h t", h=4)
            qeK = [qe, MA(6920, 128)]
            AsbK = [Asb, MA(7048, 512)]
            UgK = [Ug, MA(7560, 256)]
            blcK = [blc, MA(7816, 2)]
            seq = [(0, c) for c in range(NT)] + [(1, c) for c in range(NT - 1, -1, -1)]

            def gA(d, c, k):
                SP = SPf if d == 0 else SPb
                m01 = m01f if d == 0 else m01b
                sl = slice(c * 128, (c + 1) * 128)
                qe_, Asb_, Ug_, blc_ = qeK[k], AsbK[k], UgK[k], blcK[k]
                qk_, ak_, uk_, bk_ = "qe%d" % k, "Asb%d" % k, "Ug%d" % k, "blc%d" % k
                if d == 0:
                    V("dve", lambda e: e.tensor_tensor_scan(out=CS, data0=ones[:, 0:128], data1=SP[:, sl], initial=0.0, op0=ALU.mult, op1=ALU.add), ["SP%d" % d, "cst", "CS"], ["CS"])
                    last = CS[:, 127:128]
                else:
                    rs = slice((c + 1) * 128 - 1, c * 128 - 1 if c > 0 else None, -1)
                    V("dve", lambda e: e.tensor_tensor_scan(out=CS[:, ::-1], data0=ones[:, 0:128], data1=SP[:, rs], initial=0.0, op0=ALU.mult, op1=ALU.add), ["SP%d" % d, "cst", "CS"], ["CS"])
                    last = CS[:, 0:1]
                V("dve", lambda e: e.tensor_scalar(out=blc_[:, 0:1], in0=last, scalar1=-1.0 / 16, scalar2=None, op0=ALU.mult), ["CS", bk_], [bk_])
                act(blc_[:, 1:2], blc_[:, 0:1], AF.Exp, [bk_], [bk_])
                act(E1, CS, AF.Exp, ["CS"], ["E1"], scale=-1.0 / 16)
                act(E2, CS, AF.Exp, ["CS"], ["E2"], scale=1.0 / 16)
                act(E3, CS, AF.Exp, ["CS", bk_], ["E3"], scale=1.0 / 16, bias=blc_[:, 0:1])
                V("dve", lambda e: e.tensor_tensor(out=qe_, in0=qT_g[:, sl], in1=E1, op=ALU.mult), ["qTg", "E1"], [qk_])
                V("dve", lambda e: e.tensor_tensor(out=ke, in0=kT_g[:, sl], in1=E2, op=ALU.mult), ["kTg", "E2"], ["ke"])
                V("pool", lambda e: e.tensor_tensor(out=kd, in0=kT_g[:, sl], in1=E3, op=ALU.mult), ["kTg", "E3"], ["kd"])
                V("pool", lambda e: e.tensor_tensor(out=Qe3, in0=qe_.unsqueeze(1).to_broadcast([128, 4, 128]), in1=bmask.unsqueeze(2).to_broadcast([128, 4, 128]), op=ALU.mult), [qk_, "cst", "Qeb"], ["Qeb"])
                mm(ps[0][:, :], ke, Qeb, True, True, ["ke", "Qeb"], ["ps0"])
                V("dve", lambda e: e.tensor_tensor(out=Asb_.rearrange("p (h t) -> p h t", h=4), in0=ps[0][:, :].rearrange("p (h t) -> p h t", h=4),
                                                   in1=m01.unsqueeze(1).to_broadcast([128, 4, 128]), op=ALU.mult), ["ps0", "cst"], [ak_])
                tr(ps[3][:, 0:128], kd, ["kd"], ["ps3"])
                act(kdT, ps[3][:, 0:128], AF.Copy, ["ps3"], ["kdT"])
                mm(ps[6][:, 0:256], kdT, v_g[:, c, :], True, True, ["kdT", "vg"], ["ps6"])
                V("dve", lambda e: e.tensor_tensor(out=Ug_.rearrange("p (h n) -> p h n", h=4), in0=ps[6][:, 0:256].rearrange("p (h n) -> p h n", h=4),
                                                   in1=bmask.unsqueeze(2).to_broadcast([128, 4, 64]), op=ALU.mult), ["ps6", "cst"], [uk_])

            def gB(d, c, k):
                qe_, Asb_, Ug_, blc_ = qeK[k], AsbK[k], UgK[k], blcK[k]
                qk_, ak_, uk_, bk_ = "qe%d" % k, "Asb%d" % k, "Ug%d" % k, "blc%d" % k
                po_, pok_ = (ps[2], "ps2") if k == 0 else (ps[4], "ps4")
                mm(po_[:, 0:256], qe_, Sblk[:, d, :], True, False, [qk_, "Sblk"], [pok_])
                for h in range(4):
                    mm(po_[:, h * 64:(h + 1) * 64], Asb_[:, h * 128:(h + 1) * 128], v_g[:, c, h * 64:(h + 1) * 64], False, h == 3, [ak_, "vg"], [pok_])
                V("dve", lambda e: e.scalar_tensor_tensor(out=Sblk[:, d, :], in0=Sblk[:, d, :], scalar=blc_[:, 1:2], in1=Ug_, op0=ALU.mult, op1=ALU.add), ["Sblk", bk_, uk_], ["Sblk"])
                if (d == 0 and c % 2 == 1) or (d == 1 and c % 2 == 0):
                    sq_ = (c // 2) * 2 + d
                    V("dve", lambda e: e.tensor_reduce(out=stS[:, sq_, :], in_=Sblk[:, d, :].rearrange("p (h n) -> p n h", h=4), axis=AX.X, op=ALU.add), ["Sblk"], ["stS"])
                V("dve", lambda e: e.tensor_scalar(out=Sblk[:, d, :], in0=Sblk[:, d, :], scalar1=RN[:, d * 16 + c:d * 16 + c + 1], scalar2=None, op0=ALU.mult), ["Sblk", "RN"], ["Sblk"])
                if d == 0:
                    act(osum[:, c, :], po_[:, 0:256], AF.Copy, [pok_], ["osum"])
                else:
                    V("dve", lambda e: e.tensor_tensor(out=osum[:, c, :], in0=po_[:, 0:256], in1=osum[:, c, :], op=ALU.add), [pok_, "osum"], ["osum"])
                    return headnorm_out(osum[:, c, :].rearrange("p (h v) -> p h v", h=4), 4, False, l, 2, 2, c, "osum", pbi=(7 if k == 0 else 5), kk=k)
                return None

            gA(seq[0][0], seq[0][1], 0)
            pend = None
            for i, (d, c) in enumerate(seq):
                if i + 1 < len(seq):
                    gA(seq[i + 1][0], seq[i + 1][1], (i + 1) % 2)
                fin_ = gB(d, c, i % 2)
                if pend is not None:
                    pend()
                pend = fin_
            if pend is not None:
                pend()
            st_out(o_S[l].rearrange("s d p v -> p (s d) v"), stS[:, :, :], "stS", "o_S")
            BAR()

            ckvT = AR(4096, NKEY)
            if LOWP:
                Kaug = AR(6400, NKEY).bitcast(BF16)[:, 0:NKEY]
                Vh = AR(8704, NKEY).bitcast(BF16)[:, 0:NKEY].rearrange("p (t n) -> p t n", t=NKT)
            else:
                Kaug = AR(6400, NKEY)
                Vh = AR(8704, NKEY).rearrange("p (t n) -> p t n", t=NKT)
            cqT = AR(11008, 4096).rearrange("p (c t) -> p c t", c=2)
            Qaug = AR(15104, 512)
            PT = [AR(15616, 512), AR(16128, 512)]
            if LOWP:
                Qaug = Qaug.bitcast(BF16)[:, 0:512]
                PT = [p_.bitcast(BF16)[:, 0:512] for p_ in PT]
            rt = [AR(16640, 512), AR(17152, 512)]
            wukv = AR(0, 1024)
            rcp = AR(1024, 512)
            ctxs = AR(1536, 256).rearrange("p (t n) -> p t n", t=2)
            sqt = AR(1792, 512)
            rst = AR(2304, 512)
            wq = AR(2816, 256).rearrange("p (c n) -> p c n", c=2)
            wqr = AR(3072, 256).rearrange("p (c n) -> p c n", c=2)
            ostg = AR(3328, 256)

            for j in range(2):
                proj_fm(l, win[:, 1584 + j * 128:1712 + j * 128], 128, ev_copy_bias(lambda tb, j=j: cqT[:, j, tb * 512:(tb + 1) * 512], lambda tb: "cqT"))
            proj_fm(l, win[:, 1840:1968], 128, ev_copy_bias(lambda tb: ckvT[:, 256 + tb * 512:256 + (tb + 1) * 512], lambda tb: "ckvT"))
            krraw = MA(0, 2048)
            krrot = MA(2048, 2048)
            proj_fm(l, d_wkr[l], 128, ev_copy_bias(lambda tb: krraw[:, tb * 512:(tb + 1) * 512], lambda tb: "krraw"))
            proj_fm(l, d_wkrr[l], 128, ev_copy_bias(lambda tb: krrot[:, tb * 512:(tb + 1) * 512], lambda tb: "krrot"))
            BAR()
            ld(wukv, d_wukv[l], "wukv")
            ld(Kaug[112:128, :], d_kaug[:, :], "Kaug")
            V("pool", lambda e: e.memset(Vh[:, :, 64:128], 1.0), [], ["Vh"])
            V("pool", lambda e: e.memset(Kaug[64:112, :], 0.0), ["Kaug"], ["Kaug"])
            V("pool", lambda e: e.memset(Qaug[64:112, :], 0.0), ["Qaug"], ["Qaug"])
            for tb in range(NB):
                sl = slice(tb * 512, (tb + 1) * 512)
                for j in range(2):
                    act(sqt, cqT[:, j, sl], AF.Square, ["cqT"], ["sqt"])
                    mm(ps[4][:, :], ones[:, 0:128], sqt, j == 0, j == 1, ["cst", "sqt"], ["ps4"])
                act(rst, ps[4][:, :], AF.Sqrt, ["ps4", "EPS"], ["rst"], bias=EPS6[:, 0:1], scale=1.0 / 256)
                V("dve", lambda e: e.reciprocal(out=rst, in_=rst), ["rst"], ["rst"])
                for j in range(2):
                    V("dve", lambda e, j=j, sl=sl: e.scalar_tensor_tensor(out=cqT[:, j, sl], in0=cqT[:, j, sl], scalar=NG[:, l, 4 + j:5 + j], in1=rst, op0=ALU.mult, op1=ALU.mult), ["cqT", "NG", "rst"], ["cqT"])
                ksl = slice(256 + tb * 512, 256 + (tb + 1) * 512)
                act(sqt, ckvT[:, ksl], AF.Square, ["ckvT"], ["sqt"])
                mm(ps[5][:, :], ones[:, 0:128], sqt, True, True, ["cst", "sqt"], ["ps5"])
                act(rst, ps[5][:, :], AF.Sqrt, ["ps5", "EPS"], ["rst"], bias=EPS6[:, 0:1], scale=1.0 / 128)
                V("dve", lambda e: e.reciprocal(out=rst, in_=rst), ["rst"], ["rst"])
                V("dve", lambda e, ksl=ksl: e.scalar_tensor_tensor(out=ckvT[:, ksl], in0=ckvT[:, ksl], scalar=NG[:, l, 6:7], in1=rst, op0=ALU.mult, op1=ALU.mult), ["ckvT", "NG", "rst"], ["ckvT"])
            ostg4 = MA(4096, 1024).rearrange("p (b n) -> p b n", b=4)
            for tt in range(NT):
                pbo, pko = ps[6 + tt % 2], "ps%d" % (6 + tt % 2)
                ob, ok_ = ostg4[:, tt % 4, :], "ostg%d" % (tt % 4)
                tr(pbo[:, 0:128], ckvT[:, 256 + tt * 128:256 + (tt + 1) * 128], ["ckvT"], [pko])
                tr(pbo[:, 128:256], krraw[:, tt * 128:(tt + 1) * 128], ["krraw"], [pko])
                if tt % 2 == 0:
                    act(ob, pbo[:, 0:256], AF.Copy, [pko], [ok_])
                else:
                    V("dve", lambda e, ob=ob, pbo=pbo: e.tensor_copy(out=ob, in_=pbo[:, 0:256]), [pko], [ok_])
                st_out(o_ckv[l, tt * 128:(tt + 1) * 128, :], ob[:, 0:128], ok_, "o_ckv%d" % (tt % 4))
                st_out(o_kr[l, tt * 128:(tt + 1) * 128, :], ob[:, 128:256], ok_, "o_kr%d" % (tt % 4))
            ld(ctxs[:, :, :], d_ckvc[l].rearrange("(t p) n -> p t n", p=128), "ctxs")
            for t2 in range(2):
                tr(ps[7][:, t2 * 128:(t2 + 1) * 128], ctxs[:, t2, :], ["ctxs"], ["ps7"])
            act(ckvT[:, 0:256], ps[7][:, 0:256], AF.Copy, ["ps7"], ["ckvT"])
            ld(ctxs[:, :, :], d_krc[l].rearrange("(t p) n -> p t n", p=128), "ctxs")
            for t2 in range(2):
                tr(ps[7][:, t2 * 128:(t2 + 1) * 128], ctxs[:, t2, :], ["ctxs"], ["ps7"])
            act(Kaug[64:112, 0:256], ps[7][64:112, 0:256], AF.Copy, ["ps7", "Kaug"], ["Kaug"])

            def rope(dst, raw, rot, rawk, rotk, dstk, qb):
                for (p0, tcos, tsin, mode) in ((64, 0, 32, "r"), (96, 64, 128, "c")):
                    pr = slice(p0, p0 + 16)
                    if mode == "r":
                        cosb = TAB[pr, tcos + qb * 8:tcos + qb * 8 + 8].unsqueeze(2).to_broadcast([16, 8, 64])
                        sinb = TAB[pr, tsin + qb * 8:tsin + qb * 8 + 8].unsqueeze(2).to_broadcast([16, 8, 64])
                    else:
                        cosb = TAB[pr, tcos:tcos + 64].unsqueeze(1).to_broadcast([16, 8, 64])
                        sinb = TAB[pr, tsin:tsin + 64].unsqueeze(1).to_broadcast([16, 8, 64])
                    r3 = lambda ap: ap.rearrange("p (r c) -> p r c", r=8)
                    V("dve", lambda e, pr=pr, cosb=cosb: e.tensor_tensor(out=r3(rt[0][pr, :]), in0=r3(raw[pr, :]), in1=cosb, op=ALU.mult), [rawk, "TAB", "rt0"], ["rt0"])
                    V("dve", lambda e, pr=pr, sinb=sinb: e.tensor_tensor(out=r3(rt[1][pr, :]), in0=r3(rot[pr, :]), in1=sinb, op=ALU.mult), [rotk, "TAB", "rt1"], ["rt1"])
                    V("dve", lambda e, pr=pr: e.tensor_tensor(out=dst[pr, :], in0=rt[0][pr, :], in1=rt[1][pr, :], op=ALU.add), ["rt0", "rt1", dstk], [dstk])

            for tb in range(NB):
                rope(Kaug[:, 256 + tb * 512:256 + (tb + 1) * 512], krraw[:, tb * 512:(tb + 1) * 512], krrot[:, tb * 512:(tb + 1) * 512], "krraw", "krrot", "Kaug", tb)
            BAR()

            KB = [(0, 512), (512, 512), (1024, 512), (1536, 512), (2048, 256)]
            if LOWP:
                ckvM = AR(7552, 1152).bitcast(BF16)[:, 0:NKEY]
                V("dve", lambda e: e.tensor_copy(out=ckvM, in_=ckvT), ["ckvT"], ["ckvM"])
                wukvM = AR(9856, 512).bitcast(BF16)[:, 0:1024]
                act(wukvM, wukv, AF.Copy, ["wukv"], ["wukvM"])
                tA = rt[0].bitcast(BF16)
                tB = rt[1].bitcast(BF16)
                cqf = cqT.rearrange("p c t -> p (c t)")
                cqMf = cqf.bitcast(BF16)
                for j in range(2):
                    for hf in range(2):
                        V("dve", lambda e, j=j, hf=hf: e.tensor_copy(out=(tA if hf == 0 else tB), in_=cqT[:, j, hf * 1024:(hf + 1) * 1024]), ["cqT", "rt%d" % hf], ["rt%d" % hf])
                    if j == 0:
                        cq0a = AR(16640 - 2048, 0) if False else None
                        stash = [AR(15616, 512).bitcast(BF16), AR(16128, 512).bitcast(BF16)]
                        V("dve", lambda e: e.tensor_copy(out=stash[0], in_=tA), ["rt0", "PT0"], ["PT0"])
                        act(stash[1], tB, AF.Copy, ["rt1", "PT1"], ["PT1"])
                V("dve", lambda e: e.tensor_copy(out=cqMf[:, 0:1024], in_=stash[0]), ["PT0", "cqT"], ["cqT"])
                act(cqMf[:, 1024:2048], stash[1], AF.Copy, ["PT1", "cqT"], ["cqT"])
                V("dve", lambda e: e.tensor_copy(out=cqMf[:, 2048:3072], in_=tA), ["rt0", "cqT"], ["cqT"])
                act(cqMf[:, 3072:4096], tB, AF.Copy, ["rt1", "cqT"], ["cqT"])
                cqM = cqMf[:, 0:4096].rearrange("p (c t) -> p c t", c=2)
                wqM = AR(10368, 128).bitcast(BF16).rearrange("p (c n) -> p c n", c=2)
                wqrM = AR(10496, 128).bitcast(BF16).rearrange("p (c n) -> p c n", c=2)
                BAR()
            else:
                ckvM, wukvM, cqM = ckvT, wukv, cqT
            for h in range(8):
                for bi_, (k0, kn) in enumerate(KB):
                    pb = ps[4 + bi_ % 2]
                    pk = "ps%d" % (4 + bi_ % 2)
                    mm(pb[0:64, 0:kn], wukvM[:, h * 128:h * 128 + 64], ckvM[:, k0:k0 + kn], True, True, ["wukvM", "ckvM"], [pk])
                    if bi_ % 2 == 0:
                        act(Kaug[0:64, k0:k0 + kn], pb[0:64, 0:kn], AF.Copy, [pk, "Kaug"], ["Kaug"])
                    else:
                        V("dve", lambda e, k0=k0, kn=kn, pb=pb: e.tensor_copy(out=Kaug[0:64, k0:k0 + kn], in_=pb[0:64, 0:kn]), [pk, "Kaug"], ["Kaug"])
                for g8 in range(3):
                    nt_ = 8 if g8 < 2 else 2
                    pb = ps[6 + g8 % 2]
                    pk = "ps%d" % (6 + g8 % 2)
                    for i in range(nt_):
                        kt = g8 * 8 + i
                        mm(pb[:, i * 64:(i + 1) * 64], ckvM[:, kt * 128:(kt + 1) * 128], wukvM[:, h * 128 + 64:h * 128 + 128], True, True, ["ckvM", "wukvM"], [pk])
                    V("dve", lambda e, g8=g8, nt_=nt_, pb=pb: e.tensor_copy(out=Vh[:, g8 * 8:g8 * 8 + nt_, 0:64], in_=pb[:, 0:nt_ * 64].rearrange("p (t n) -> p t n", t=nt_)), [pk, "Vh"], ["Vh"])
                ld(wq[:, :, :], d_wuq[l, h].rearrange("(c p) n -> p c n", p=128), "wq")
                ld(wqr[:, :, :], d_wuqr[l, h].rearrange("(c p) n -> p c n", p=128), "wqr")
                if LOWP:
                    act(wqM, wq, AF.Copy, ["wq"], ["wqM"])
                    V("dve", lambda e: e.tensor_copy(out=wqrM, in_=wqr), ["wqr"], ["wqrM"])
                    wq_, wqr_, wqk, wqrk = wqM, wqrM, "wqM", "wqrM"
                else:
                    wq_, wqr_, wqk, wqrk = wq, wqr, "wq", "wqr"
                QA = [Qaug, AR(3584, 512).bitcast(BF16)[:, 0:512] if LOWP else AR(3584, 512)]

                def qbuild(qb, Qa, qk):
                    sl = slice(qb * 512, (qb + 1) * 512)
                    for j in range(2):
                        mm(ps[4][:, :], wq_[:, j, :], cqM[:, j, sl], j == 0, j == 1, [wqk, "cqT"], ["ps4"])
                    for j in range(2):
                        mm(ps[5][:, :], wqr_[:, j, :], cqM[:, j, sl], j == 0, j == 1, [wqrk, "cqT"], ["ps5"])
                    ld(Qa[112:128, :], d_qaug[:, sl], qk)
                    act(Qa[0:64, :], ps[4][0:64, :], AF.Copy, ["ps4", qk], [qk])
                    rope(Qa, ps[4], ps[5], "ps4", "ps5", qk, qb)

                if h == 0:
                    V("pool", lambda e: e.memset(QA[1][64:112, :], 0.0), [], ["Qaug1"])
                qbuild(0, QA[0], "Qaug")
                for qb in range(NB):
                    sl = slice(qb * 512, (qb + 1) * 512)
                    Qa, qk = QA[qb % 2], ("Qaug" if qb % 2 == 0 else "Qaug1")
                    po = ps[2 + (h * 4 + qb) % 2]
                    pok = "ps%d" % (2 + (h * 4 + qb) % 2)

                    def mm1(kt):
                        mm(ps[kt % 2][:, :], Kaug[:, kt * 128:(kt + 1) * 128], Qa, True, True, ["Kaug", qk], ["ps%d" % (kt % 2)])

                    mm1(0)
                    for kt in range(NKT):
                        if kt == 3 and qb + 1 < NB:
                            qbuild(qb + 1, QA[(qb + 1) % 2], "Qaug" if (qb + 1) % 2 == 0 else "Qaug1")
                        if kt + 1 < NKT:
                            mm1(kt + 1)
                        ptile = PT[kt % 2]
                        act(ptile, ps[kt % 2][:, :], AF.Exp, ["ps%d" % (kt % 2)], ["PT%d" % (kt % 2)], scale=A_SCALE)
                        mm(po[:, :], Vh[:, kt, :], ptile, kt == 0, kt == NKT - 1, ["Vh", "PT%d" % (kt % 2)], [pok])
                    V("dve", lambda e, po=po: e.reciprocal(out=rcp[64:128, :], in_=po[64:128, :]), [pok], ["rcp"])
                    mc = 4 + h // 2
                    p0 = (h % 2) * 64
                    V("dve", lambda e, po=po, mc=mc, p0=p0, sl=sl: e.tensor_tensor(out=mixT[p0:p0 + 64, mc, sl], in0=po[0:64, :], in1=rcp[64:128, :], op=ALU.mult), [pok, "rcp"], ["mixA"])
            BAR()

            LNT1 = (AR(13312, 512), [AR(13824, 512), AR(14336, 512)], AR(14848, 512))
            LNK1 = ("lnmean", ["lnsq0", "lnsq1"], "lnrstd")
            for tb in range(NB):
                sl = slice(tb * 512, (tb + 1) * 512)
                for c in range(KC):
                    if c % 2 == 0:
                        act(xT[:, c, sl], xT[:, c, sl], AF.Copy, [XK[tb]], [XK[tb]], scale=ALPHA)
                    else:
                        V("dve", lambda e, c=c, sl=sl: e.tensor_scalar(out=xT[:, c, sl], in0=xT[:, c, sl], scalar1=ALPHA, scalar2=None, op0=ALU.mult), [XK[tb]], [XK[tb]])
            if LOWP:
                mixB = AR(5120, 8192).bitcast(BF16).rearrange("p (c t) -> p c t", c=8)
                for c in range(KC):
                    if c % 2 == 0:
                        act(mixB[:, c, :], mixT[:, c, :], AF.Copy, ["mixall"], ["mixB"])
                    else:
                        V("dve", lambda e, c=c: e.tensor_copy(out=mixB[:, c, :], in_=mixT[:, c, :]), ["mixall"], ["mixB"])
                WOB = [AR(4096, 512).bitcast(BF16).rearrange("p (c n) -> p c n", c=8), AR(4608, 512).bitcast(BF16).rearrange("p (c n) -> p c n", c=8)]
            for oc in range(KC):
                W_, wk = wslot()
                Wv = W_[:, 0:1024].rearrange("p (c n) -> p c n", c=8)
                ld(Wv, d_wout[l].rearrange("(c p) n -> p c n", p=128)[:, :, oc * 128:(oc + 1) * 128], wk)
                if LOWP:
                    wob, wobk = WOB[oc % 2], "wob%d" % (oc % 2)
                    if oc % 2 == 0:
                        act(wob, Wv, AF.Copy, [wk], [wobk])
                    else:
                        V("dve", lambda e, wob=wob, Wv=Wv: e.tensor_copy(out=wob, in_=Wv), [wk], [wobk])
                    Wm, wmk, rhsT, rk = wob, wobk, mixB, "mixB"
                else:
                    Wm, wmk, rhsT, rk = Wv, wk, mixT, "mixall"
                for tb in range(NB):
                    sl = slice(tb * 512, (tb + 1) * 512)
                    pb = ps[tb % 2]
                    pk = "ps%d" % (tb % 2)
                    for c in range(KC):
                        mm(pb[:, :], Wm[:, c, :], rhsT[:, c, sl], c == 0, c == KC - 1, [wmk, rk], [pk])
                    V("dve", lambda e, oc=oc, sl=sl, pb=pb: e.scalar_tensor_tensor(out=xT[:, oc, sl], in0=pb[:, :], scalar=MOD[:, l, 16 + oc:17 + oc], in1=xT[:, oc, sl],
                                                                                   op0=ALU.mult, op1=ALU.add), [pk, "MOD", XK[tb]], [XK[tb]])
            for tb in range(NB):
                layernorm_block(l, tb, 0, 8, LNT1, LNK1)
            BAR()

            mflat = mixT[:, :, :].rearrange("p c t -> p (c t)")
            if LOWP:
                mfb = mflat.bitcast(BF16)
                hid = mfb[:, 0:NJ * 512].rearrange("p (j t) -> p j t", j=NJ)
                hh = mfb[:, NJ * 512:NJ * 512 + 4096].rearrange("p (c t) -> p c t", c=8)
            else:
                hid = mflat[:, 0:NJ * 512].rearrange("p (j t) -> p j t", j=NJ)
                hh = mflat[:, NJ * 512:NJ * 512 + 4096].rearrange("p (c t) -> p c t", c=8)
            WG = [AR(0, 2048), AR(2048, 2048)]
            WU = [AR(4096, 2048), AR(6144, 2048)]
            WD = [AR(8192 + i * 1024, 1024) for i in range(4)]
            gtmp = [AR(12288, 512), AR(12800, 512)]
            LNT2 = (AR(13312, 512), [AR(13824, 512), AR(14336, 512)], AR(14848, 512))
            LNK2 = ("lnmean", ["lnsq0", "lnsq1"], "lnrstd")
            if LOWP:
                WGB = [AR(15360, 1024).bitcast(BF16), AR(16384, 1024).bitcast(BF16)]
                WUB = [mflat[:, 8192:9216].bitcast(BF16), mflat[:, 9216:10240].bitcast(BF16)]
                WDB = [mflat[:, 10240:10752].bitcast(BF16), mflat[:, 10752:11264].bitcast(BF16)]
            wdi = 0
            for tb in range(NB):
                sl = slice(tb * 512, (tb + 1) * 512)
                xk = XK[tb]
                for c in range(KC):
                    V("dve", lambda e, c=c, sl=sl: e.tensor_scalar(out=hh[:, c, :], in0=xT[:, c, sl], scalar1=OP1[:, l, 8 + c:9 + c], scalar2=MOD[:, l, 24 + c:25 + c],
                                                                   op0=ALU.mult, op1=ALU.add), [xk, "OP1", "MOD"], ["hh"])
                    act(xT[:, c, sl], xT[:, c, sl], AF.Copy, [xk], [xk], scale=ALPHA)
                for jp in range(NJ // 2):
                    wg_ = WG[jp % 2].rearrange("p (c n) -> p c n", c=8)
                    wu_ = WU[jp % 2].rearrange("p (c n) -> p c n", c=8)
                    gk, uk = "wg%d" % (jp % 2), "wu%d" % (jp % 2)
                    ld(wg_, d_wg[l].rearrange("(c p) n -> p c n", p=128)[:, :, jp * 256:(jp + 1) * 256], gk)
                    ld(wu_, d_wu[l].rearrange("(c p) n -> p c n", p=128)[:, :, jp * 256:(jp + 1) * 256], uk)
                    if LOWP:
                        wgb = WGB[jp % 2].rearrange("p (c n) -> p c n", c=8)
                        wub = WUB[jp % 2].rearrange("p (c n) -> p c n", c=8)
                        gbk, ubk = "wgb%d" % (jp % 2), "wub%d" % (jp % 2)
                        act(wgb, wg_, AF.Copy, [gk], [gbk])
                        V("dve", lambda e, wub=wub, wu_=wu_: e.tensor_copy(out=wub, in_=wu_), [uk], [ubk])
                        wgm, wum, gmk, umk = wgb, wub, gbk, ubk
                    else:
                        wgm, wum, gmk, umk = wg_, wu_, gk, uk
                    for jj in range(2):
                        j = jp * 2 + jj
                        pg, pgk = ps[j % 2], "ps%d" % (j % 2)
                        pu, puk = ps[2 + j % 2], "ps%d" % (2 + j % 2)
                        for c in range(KC):
                            mm(pg[:, :], wgm[:, c, jj * 128:(jj + 1) * 128], hh[:, c, :], c == 0, c == KC - 1, [gmk, "hh"], [pgk])
                        for c in range(KC):
                            mm(pu[:, :], wum[:, c, jj * 128:(jj + 1) * 128], hh[:, c, :], c == 0, c == KC - 1, [umk, "hh"], [puk])
                        gt = gtmp[j % 2]
                        act(gt, pg[:, :], AF.Silu, [pgk], ["gt%d" % (j % 2)])
                        V("dve", lambda e, j=j, pu=pu, gt=gt: e.tensor_tensor(out=hid[:, j, :], in0=pu[:, :], in1=gt, op=ALU.mult), [puk, "gt%d" % (j % 2)], ["hid"])
                for op_ in range(KC // 2):
                    for j4 in range(0, NJ, 4):
                        nj = min(4, NJ - j4)
                        wd_ = WD[wdi % 4][:, 0:nj * 256].rearrange("p (c n) -> p c n", c=nj)
                        dk_ = "wd%d" % (wdi % 4)
                        ld(wd_, d_wd[l][j4 * 128:(j4 + nj) * 128, op_ * 256:(op_ + 1) * 256].rearrange("(c p) n -> p c n", p=128), dk_)
                        if LOWP:
                            wdb = WDB[wdi % 2][:, 0:nj * 256].rearrange("p (c n) -> p c n", c=nj)
                            dbk = "wdb%d" % (wdi % 2)
                            if wdi % 2 == 0:
                                act(wdb, wd_, AF.Copy, [dk_], [dbk])
                            else:
                                V("dve", lambda e, wdb=wdb, wd_=wd_: e.tensor_copy(out=wdb, in_=wd_), [dk_], [dbk])
                            wdm, dmk = wdb, dbk
                        else:
                            wdm, dmk = wd_, dk_
                        wdi += 1
                        for jj in range(nj):
                            j = j4 + jj
                            for o2 in range(2):
                                mm(ps[4 + o2][:, :], wdm[:, jj, o2 * 128:(o2 + 1) * 128], hid[:, j, :], j == 0, j == NJ - 1, [dmk, "hid"], ["ps%d" % (4 + o2)])
                    for o2 in range(2):
                        oc = op_ * 2 + o2
                        V("dve", lambda e, oc=oc, sl=sl, o2=o2: e.scalar_tensor_tensor(out=xT[:, oc, sl], in0=ps[4 + o2][:, :], scalar=MOD[:, l, 40 + oc:41 + oc], in1=xT[:, oc, sl],
                                                                                       op0=ALU.mult, op1=ALU.add), ["ps%d" % (4 + o2), "MOD", xk], [xk])
                layernorm_block(l, tb, 16, 24, LNT2, LNK2)
            BAR()

        ystg = [AR(0, 1024), AR(1024, 1024)]
        for tt in range(NT):
            ys = ystg[tt % 2]
            yk = "ystg%d" % (tt % 2)
            for half in range(2):
                pb = ps[(tt * 2 + half) % 4]
                pk = "ps%d" % ((tt * 2 + half) % 4)
                for q in range(4):
                    c = half * 4 + q
                    tr(pb[:, q * 128:(q + 1) * 128], xT[:, c, tt * 128:(tt + 1) * 128], [XK[tt // 4]], [pk])
                if half == 0:
                    act(ys[:, 0:512], pb[:, :], AF.Copy, [pk, yk], [yk])
                else:
                    V("dve", lambda e, ys=ys, pb=pb: e.tensor_copy(out=ys[:, 512:1024], in_=pb[:, :]), [pk, yk], [yk])
            st_out(o_y[tt * 128:(tt + 1) * 128, :], ys, yk, "o_y%d" % (tt % 2))
        P.emit()
    return nc


_rr2 = [0]


def evq2():
    _rr2[0] ^= 1
    return "act" if False else "dve"


def _consts():
    cst = np.zeros((128, 1292), np.float32)
    cst[:, 0:128] = np.eye(128)
    cst[:, 128:256] = 1.0
    s = np.arange(128)[:, None]
    t = np.arange(128)[None, :]
    cst[:, 256:384] = (s <= t)
    cst[:, 384:512] = (s >= t)
    cst[:, 512:640] = np.where(s <= t, 0.0, NEG)
    cst[:, 640:768] = np.where(s >= t, 0.0, NEG)
    for h in range(4):
        cst[h * 32:(h + 1) * 32, 768 + h] = 1.0
    for base in (0, 32):
        for r in range(4):
            cst[base + r, 772 + r * 128:772 + (r + 1) * 128] = 1.0
            cst[base + r, 1284 + r] = -1.0
            cst[base + r, 1288 + r] = 1.0
    bit = np.zeros((64, 128), np.float32)
    for base in (0, 32):
        for r in range(4):
            bit[base + r, r * 32:(r + 1) * 32] = 1.0
    return cst, bit


def _rope_tab(is_sample):
    tab = np.zeros((128, 192), np.float32)
    half = 8
    inv = (10000.0 ** (-np.arange(half, dtype=np.float32) / half)).astype(np.float32)
    if is_sample:
        rows = np.arange(32, dtype=np.float32)
        cols = np.arange(64, dtype=np.float32)
        angr = (rows[None, :] * inv[:, None]).astype(np.float32)
        angc = (cols[None, :] * inv[:, None]).astype(np.float32)
        cr, sr = np.cos(angr).astype(np.float32), np.sin(angr).astype(np.float32)
        cc, sc = np.cos(angc).astype(np.float32), np.sin(angc).astype(np.float32)
    else:
        cr, sr = np.ones((8, 32), np.float32), np.zeros((8, 32), np.float32)
        cc, sc = np.ones((8, 64), np.float32), np.zeros((8, 64), np.float32)
    tab[64:72, 0:32] = cr
    tab[72:80, 0:32] = cr
    tab[64:72, 32:64] = -sr
    tab[72:80, 32:64] = sr
    tab[96:104, 64:128] = cc
    tab[104:112, 64:128] = cc
    tab[96:104, 128:192] = -sc
    tab[104:112, 128:192] = sc
    return tab


def _pad_rope_cols(w32, rot):
    out = np.zeros(w32.shape[:-1] + (128,), np.float32)
    a = w32[..., 0:16]
    b = w32[..., 16:32]
    if rot:
        a = np.concatenate([a[..., 8:16], a[..., 0:8]], -1)
        b = np.concatenate([b[..., 8:16], b[..., 0:8]], -1)
    out[..., 64:80] = a
    out[..., 96:112] = b
    return out


_NC_CACHE = {}


def kernel(x_prompt, x_sample, state_mlstm_C, state_mlstm_n, state_mlstm_m, state_gla_S,
           cache_mla_ckv, cache_mla_krope, c, c_ctx, w_ada, b_ada, w_in, m_gate_b, m_norm_g,
           g_w2, g_b2, g_norm_g, a_q_norm_g, a_kv_norm_g, a_w_uq, a_w_ukv, w_out, ln1_g, ln1_b,
           w_ffn_gate, w_ffn_up, w_ffn_down, ln2_g, ln2_b, _L=None):
    f = lambda a: np.ascontiguousarray(np.asarray(a, dtype=np.float32))
    L = int(_L) if _L is not None else int(np.asarray(w_ada).shape[0])
    (x_prompt, x_sample, state_mlstm_C, state_mlstm_n, state_mlstm_m, state_gla_S, cache_mla_ckv, cache_mla_krope,
     c, c_ctx, w_ada, b_ada, w_in, m_gate_b, m_norm_g, g_w2, g_b2, g_norm_g, a_q_norm_g, a_kv_norm_g, a_w_uq, a_w_ukv,
     w_out, ln1_g, ln1_b, w_ffn_gate, w_ffn_up, w_ffn_down, ln2_g, ln2_b) = [f(a) for a in (
        x_prompt, x_sample, state_mlstm_C, state_mlstm_n, state_mlstm_m, state_gla_S, cache_mla_ckv, cache_mla_krope,
        c, c_ctx, w_ada, b_ada, w_in, m_gate_b, m_norm_g, g_w2, g_b2, g_norm_g, a_q_norm_g, a_kv_norm_g, a_w_uq, a_w_ukv,
        w_out, ln1_g, ln1_b, w_ffn_gate, w_ffn_up, w_ffn_down, ln2_g, ln2_b)]
    if L not in _NC_CACHE:
        _NC_CACHE[L] = build(L)
    nc = _NC_CACHE[L]

    colT = lambda v: np.ascontiguousarray(v.reshape(-1, 128).T)
    shared = {}
    shared["w_ada"] = w_ada[:L]
    shared["b_adaT"] = np.stack([colT(b_ada[l]) for l in range(L)])
    shared["w_in"] = w_in[:L]
    wgi = np.zeros((L, 1024, 64), np.float32)
    wgf = np.zeros((L, 1024, 64), np.float32)
    gb = np.zeros((L, 64, 2), np.float32)
    wgi[:, :, 0:4] = w_in[:L, :, 768:772]
    wgi[:, :, 32:36] = w_in[:L, :, 776:780]
    wgf[:, :, 0:4] = w_in[:L, :, 772:776]
    wgf[:, :, 32:36] = w_in[:L, :, 780:784]
    gb[:, 0:4, 0] = m_gate_b[:L, 0:4]
    gb[:, 32:36, 0] = m_gate_b[:L, 8:12]
    gb[:, 0:4, 1] = m_gate_b[:L, 4:8]
    gb[:, 32:36, 1] = m_gate_b[:L, 12:16]
    shared["w_gi"], shared["w_gf"], shared["gb"] = wgi, wgf, gb
    shared["w_kr"] = _pad_rope_cols(w_in[:L, :, 1968:2000], False)
    shared["w_krr"] = _pad_rope_cols(w_in[:L, :, 1968:2000], True)
    shared["g_w2"] = g_w2[:L]
    shared["g_b2T"] = np.ascontiguousarray(np.transpose(g_b2[:L], (0, 2, 1)))
    shared["normg"] = np.stack([np.concatenate([colT(m_norm_g[l]), colT(g_norm_g[l]), colT(a_q_norm_g[l]), colT(a_kv_norm_g[l])], 1) for l in range(L)])
    shared["lnp"] = np.stack([np.concatenate([colT(ln1_g[l]), colT(ln1_b[l]), colT(ln2_g[l]), colT(ln2_b[l])], 1) for l in range(L)])
    uq = a_w_uq[:L].reshape(L, 256, 8, 96)
    wuqp = np.zeros((L, 8, 256, 128), np.float32)
    wuqr = np.zeros((L, 8, 256, 128), np.float32)
    for h in range(8):
        wuqp[:, h, :, 0:64] = uq[:, :, h, 0:64]
        wuqr[:, h, :, 0:64] = uq[:, :, h, 0:64]
        wuqp[:, h] += _pad_rope_cols(uq[:, :, h, 64:96], False)
        wuqr[:, h] += _pad_rope_cols(uq[:, :, h, 64:96], True)
    shared["w_uqp"], shared["w_uqr"] = wuqp, wuqr
    shared["w_ukv"] = a_w_ukv[:L]
    shared["w_out"] = w_out[:L]
    shared["w_fg"], shared["w_fu"], shared["w_fd"] = w_ffn_gate[:L], w_ffn_up[:L], w_ffn_down[:L]
    cst, bit = _consts()
    shared["cst"], shared["bit"] = cst, bit

    in_maps = []
    for core in range(8):
        m = dict(shared)
        sample = core >= 4
        b = core - 4
        if sample:
            m["x"] = x_sample[b]
            m["condT"] = colT(c[b])
        else:
            m["x"] = x_prompt[core * 8:(core + 1) * 8].reshape(T, 1024)
            m["condT"] = colT(c_ctx)
        cn0 = np.zeros((L, 2, 128, 4, 65), np.float32)
        s0 = np.zeros((L, 2, 128, 4, 64), np.float32)
        m0 = np.zeros((L, 64, 1), np.float32)
        ckvc = np.zeros((L, 256, 128), np.float32)
        krc = np.zeros((L, 256, 128), np.float32)
        if sample:
            for h in range(4):
                cn0[:, :, h * 32:(h + 1) * 32, h, 0:64] = state_mlstm_C[b, :L, :, h]
                cn0[:, :, h * 32:(h + 1) * 32, h, 64] = state_mlstm_n[b, :L, :, h]
                s0[:, :, h * 32:(h + 1) * 32, h, :] = state_gla_S[b, :L, :, h]
            m0[:, 0:4, 0] = state_mlstm_m[b, :L, 0]
            m0[:, 32:36, 0] = state_mlstm_m[b, :L, 1]
            ckvc = cache_mla_ckv[b, :L]
            krc = _pad_rope_cols(cache_mla_krope[b, :L], False)
        m["cn0"] = cn0.reshape(L, 2, 128, 260)
        m["s0"] = s0.reshape(L, 2, 128, 256)
        m["m0"] = m0
        m["ckv_ctx"] = np.ascontiguousarray(ckvc)
        m["kr_ctx"] = krc
        rz = np.zeros((64, 32), np.float32)
        rn = np.ones((128, 32), np.float32)
        rz[:, 16:32] = NINF
        if not sample:
            for cch in range(16):
                if cch % 2 == 0:
                    rz[0:4, cch] = NINF
                    rz[0:4, 16 + cch] = 0.0
                else:
                    rz[32:36, cch] = NINF
                    rz[32:36, 16 + cch] = 0.0
                if cch % 2 == 1:
                    rn[:, cch] = 0.0
                else:
                    rn[:, 16 + cch] = 0.0
        m["rz"], m["rn"] = rz, rn
        kaug = np.zeros((16, NKEY), np.float32)
        qaug = np.zeros((16, T), np.float32)
        kaug[0, :] = 1.0
        for j in range(8):
            kaug[1 + j, 256 + j * 256:256 + (j + 1) * 256] = 1.0
        kaug[9, 0:256] = 1.0
        if not sample:
            for j in range(8):
                qaug[1 + j, :] = NEG
                qaug[1 + j, j * 256:(j + 1) * 256] = 0.0
            qaug[9, :] = NEG
        if LOWP:
            kaug = kaug.astype(ml_dtypes.bfloat16)
            qaug = qaug.astype(ml_dtypes.bfloat16)
        m["kaugc"], m["qaugc"] = kaug, qaug
        m["ropetab"] = _rope_tab(sample)
        in_maps.append({k: (np.ascontiguousarray(v) if v.dtype == ml_dtypes.bfloat16 else np.ascontiguousarray(v, dtype=np.float32)) for k, v in m.items()})

    res = run_bass_kernel_spmd(nc, in_maps, core_ids=list(range(8)))
    R = res.results
    if DEBUG:
        kernel.dbg = [R[i]["o_dbg"] for i in range(8)]
    y_prompt = np.concatenate([R[i]["o_y"].reshape(8, 256, 1024) for i in range(4)], 0)
    y_sample = np.stack([R[4 + i]["o_y"] for i in range(4)], 0)
    oC = np.concatenate([R[i]["o_C"] for i in range(4)], 1)
    new_C = np.ascontiguousarray(np.transpose(oC, (1, 0, 2, 3, 4)).reshape(32, L, 2, 4, 32, 64))
    on = np.concatenate([R[i]["o_n"].reshape(L, 8, 2, 128) for i in range(4)], 1)
    new_n = np.ascontiguousarray(np.transpose(on, (1, 0, 2, 3)).reshape(32, L, 2, 4, 32))
    om = np.concatenate([R[i]["o_m"] for i in range(4)], 3)
    new_m = np.ascontiguousarray(np.transpose(om, (3, 0, 1, 2)))
    oS = np.concatenate([R[i]["o_S"] for i in range(4)], 1)
    new_S = np.ascontiguousarray(np.transpose(oS, (1, 0, 2, 3, 4)).reshape(32, L, 2, 4, 32, 64))
    ockv = np.concatenate([R[i]["o_ckv"].reshape(L, 8, 256, 128) for i in range(4)], 1)
    new_ckv = np.ascontiguousarray(np.transpose(ockv, (1, 0, 2, 3)))
    okr = np.concatenate([R[i]["o_kr"].reshape(L, 8, 256, 128) for i in range(4)], 1)
    okr = np.concatenate([okr[..., 64:80], okr[..., 96:112]], -1)
    new_kr = np.ascontiguousarray(np.transpose(okr, (1, 0, 2, 3)))
    return (y_prompt, y_sample, new_C, new_n, new_m, new_S, new_ckv, new_kr)
```
